# Optimizing a Trainium2 kernel written in Bass

```python
import math
import jax, jax.numpy as jnp
from jax import lax
import numpy as np

D_MODEL = 1024
BATCH = 16
SEQ = 2048
DEPTH = 1

GRID_W = 64
ROPE_THETA = 10000.0
Q_BLOCK = 128
EPS = 1e-6

MLA_HEADS = 8
Q_LORA = 384
KV_LORA = 256
MLA_NOPE = 64
MLA_ROPE = 32
MLA_V = 64
MLA_QK = MLA_NOPE + MLA_ROPE

GQA_HEADS = 8
GQA_KV_HEADS = 2
GQA_HD = 64

IN_SPLITS = (Q_LORA, KV_LORA, MLA_ROPE, GQA_HEADS * GQA_HD, GQA_KV_HEADS * GQA_HD, GQA_KV_HEADS * GQA_HD)
IN_WIDTH = sum(IN_SPLITS)
MLA_OUT = MLA_HEADS * MLA_V
GQA_OUT = GQA_HEADS * GQA_HD
MIX_WIDTH = MLA_OUT + GQA_OUT

D_FF = int(math.ceil((8 * D_MODEL / 3) / 256) * 256)

kernel_name = "hybrid_mla_gqa_axial_encoder_block"


def rmsnorm(x, g):
    xf = x.astype(jnp.float32)
    y = xf * lax.rsqrt(jnp.mean(xf * xf, axis=-1, keepdims=True) + EPS)
    return (y * g.astype(jnp.float32)).astype(x.dtype)


def axial_angles(rows, d_rot):
    d_ax = d_rot // 2
    inv = ROPE_THETA ** (-(jnp.arange(0, d_ax, 2, dtype=jnp.float32) / d_ax))
    row = jnp.repeat(jnp.arange(rows, dtype=jnp.float32), GRID_W)
    col = jnp.tile(jnp.arange(GRID_W, dtype=jnp.float32), rows)
    return row[:, None] * inv[None, :], col[:, None] * inv[None, :]


def rotate(x, ang):
    n = x.shape[-1] // 2
    c = jnp.cos(ang)[:, None, :].astype(x.dtype)
    s = jnp.sin(ang)[:, None, :].astype(x.dtype)
    x1, x2 = x[..., :n], x[..., n:]
    return jnp.concatenate([x1 * c - x2 * s, x1 * s + x2 * c], axis=-1)


def axial_rope(x, ang_r, ang_c):
    d_ax = x.shape[-1] // 2
    return jnp.concatenate([rotate(x[..., :d_ax], ang_r), rotate(x[..., d_ax:], ang_c)], axis=-1)


def blocked_attention(q, k, v, scale):
    B, S, H, D = q.shape
    Hk = k.shape[2]
    G = H // Hk
    Dv = v.shape[-1]
    nb = S // Q_BLOCK
    kf = k.astype(jnp.float32)
    vf = v.astype(jnp.float32)
    qb = q.reshape(B, nb, Q_BLOCK, Hk, G, D).transpose(1, 0, 3, 4, 2, 5)

    def one_block(qblk):
        s = jnp.einsum('bkgqd,bskd->bkgqs', qblk.astype(jnp.float32), kf) * scale
        p = jax.nn.softmax(s, axis=-1)
        return jnp.einsum('bkgqs,bskd->bkgqd', p, vf)

    o = lax.map(one_block, qb)
    o = o.transpose(1, 0, 4, 2, 3, 5).reshape(B, S, H * Dv)
    return o.astype(q.dtype)


def setup_inputs(seed: int = 0) -> dict:
    key = jax.random.key(seed)
    ks = jax.random.split(key, 20)
    L = DEPTH

    def w(k, shape, fan_in):
        return jax.random.normal(k, shape, jnp.float32) * (fan_in ** -0.5)

    def gain(k, n):
        return 1.0 + 0.02 * jax.random.normal(k, (L, n), jnp.float32)

    return {
        "x": jax.random.normal(ks[0], (BATCH, SEQ, D_MODEL), jnp.float32),
        "norm1_g": gain(ks[1], D_MODEL),
        "w_in": w(ks[2], (L, D_MODEL, IN_WIDTH), D_MODEL),
        "q_a_norm_g": gain(ks[3], Q_LORA),
        "w_q_b": w(ks[4], (L, Q_LORA, MLA_HEADS * MLA_QK), Q_LORA),
        "kv_a_norm_g": gain(ks[5], KV_LORA),
        "w_kv_b": w(ks[6], (L, KV_LORA, MLA_HEADS * (MLA_NOPE + MLA_V)), KV_LORA),
        "mla_q_norm_g": gain(ks[7], MLA_QK),
        "mla_k_norm_g": gain(ks[8], MLA_QK),
        "gqa_q_norm_g": gain(ks[9], GQA_HD),
        "gqa_k_norm_g": gain(ks[10], GQA_HD),
        "mla_out_norm_g": gain(ks[11], MLA_OUT),
        "gqa_out_norm_g": gain(ks[12], GQA_OUT),
        "w_o": w(ks[13], (L, MIX_WIDTH, D_MODEL), MIX_WIDTH),
        "norm2_g": gain(ks[14], D_MODEL),
        "w_gate": w(ks[15], (L, D_MODEL, D_FF), D_MODEL),
        "w_up": w(ks[16], (L, D_MODEL, D_FF), D_MODEL),
        "w_down": w(ks[17], (L, D_FF, D_MODEL), D_FF),
    }


def reference(x, norm1_g, w_in, q_a_norm_g, w_q_b, kv_a_norm_g, w_kv_b,
              mla_q_norm_g, mla_k_norm_g, gqa_q_norm_g, gqa_k_norm_g,
              mla_out_norm_g, gqa_out_norm_g, w_o, norm2_g, w_gate, w_up, w_down):
    B, S, _ = x.shape
    rows = S // GRID_W
    ang_r_mla, ang_c_mla = axial_angles(rows, MLA_ROPE)
    ang_r_gqa, ang_c_gqa = axial_angles(rows, GQA_HD)
    offsets = list(np.cumsum(IN_SPLITS)[:-1])

    for l in range(DEPTH):
        h = rmsnorm(x, norm1_g[l])
        p = h @ w_in[l]
        cq, ckv, kpe, gq, gk, gv = jnp.split(p, offsets, axis=-1)

        cq = rmsnorm(cq, q_a_norm_g[l])
        q_a = (cq @ w_q_b[l]).reshape(B, S, MLA_HEADS, MLA_QK)
        ckv = rmsnorm(ckv, kv_a_norm_g[l])
        kv = (ckv @ w_kv_b[l]).reshape(B, S, MLA_HEADS, MLA_NOPE + MLA_V)
        k_nope, v_a = kv[..., :MLA_NOPE], kv[..., MLA_NOPE:]
        k_pe = jnp.broadcast_to(kpe[:, :, None, :], (B, S, MLA_HEADS, MLA_ROPE))
        k_a = jnp.concatenate([k_nope, k_pe], axis=-1)
        q_a = rmsnorm(q_a, mla_q_norm_g[l])
        k_a = rmsnorm(k_a, mla_k_norm_g[l])
        q_a = jnp.concatenate([q_a[..., :MLA_NOPE], axial_rope(q_a[..., MLA_NOPE:], ang_r_mla, ang_c_mla)], axis=-1)
        k_a = jnp.concatenate([k_a[..., :MLA_NOPE], axial_rope(k_a[..., MLA_NOPE:], ang_r_mla, ang_c_mla)], axis=-1)
        o_a = blocked_attention(q_a, k_a, v_a, MLA_QK ** -0.5)

        q_b = rmsnorm(gq.reshape(B, S, GQA_HEADS, GQA_HD), gqa_q_norm_g[l])
        k_b = rmsnorm(gk.reshape(B, S, GQA_KV_HEADS, GQA_HD), gqa_k_norm_g[l])
        v_b = gv.reshape(B, S, GQA_KV_HEADS, GQA_HD)
        q_b = axial_rope(q_b, ang_r_gqa, ang_c_gqa)
        k_b = axial_rope(k_b, ang_r_gqa, ang_c_gqa)
        o_b = blocked_attention(q_b, k_b, v_b, GQA_HD ** -0.5)

        mixed = jnp.concatenate([rmsnorm(o_a, mla_out_norm_g[l]), rmsnorm(o_b, gqa_out_norm_g[l])], axis=-1)
        x = x + mixed @ w_o[l]

        h2 = rmsnorm(x, norm2_g[l])
        x = x + (jax.nn.silu(h2 @ w_gate[l]) * (h2 @ w_up[l])) @ w_down[l]
    return x
```

```python
from contextlib import ExitStack

import numpy as np
import concourse.bass as bass
import concourse.mybir as mybir
from concourse.bass_utils import run_bass_kernel_spmd

F32 = mybir.dt.float32
BF16 = mybir.dt.bfloat16
AF = mybir.ActivationFunctionType
ALU = mybir.AluOpType
AX = mybir.AxisListType

D = 1024
KC = 8
DFF = 2816
FC = 22
EPS = 1e-6
N_CORES = 8


class TK:
    def __init__(self, nc, es):
        self.nc = nc
        self.es = es
        self.E = {}
        for name, h in (("pe", nc.tensor), ("act", nc.scalar), ("dve", nc.vector),
                        ("pool", nc.gpsimd), ("sp", nc.sync)):
            self.E[name] = dict(h=h, sem=es.enter_context(nc.semaphore("e_" + name)), cnt=0, waited={})
        self.res = {}
        self.dsem = {}

    @staticmethod
    def _excl(key):
        return isinstance(key, tuple) and key[0] == "ps"

    def _collect(self, eng, reads, writes):
        deps = {}

        def add(st, same_ok):
            if st is None:
                return
            key, sem, val = st
            if key == eng and (eng == "pe" or not same_ok):
                return
            if key not in deps or deps[key][1] < val:
                deps[key] = (sem, val)

        for k in reads:
            d = self.res.get(k)
            if d:
                add(d["w"], True)
        for k in writes:
            d = self.res.get(k)
            if d:
                add(d["w"], False)
                for st in d["r"].values():
                    add(st, False)
        return deps

    def _waits(self, eng, deps):
        E = self.E[eng]
        for key, (sem, val) in deps.items():
            if E["waited"].get(key, 0) >= val:
                continue
            E["waited"][key] = val
            E["h"].wait_ge(sem, val)

    def _record(self, stamp, reads, writes):
        for k in reads:
            d = self.res.setdefault(k, {"w": None, "r": {}})
            d["r"][stamp[0]] = stamp
        for k in writes:
            self.res[k] = {"w": stamp, "r": {}}

    def _split(self, r, w):
        r = list(r)
        w = list(w)
        ex = [k for k in r if self._excl(k)]
        r = [k for k in r if not self._excl(k)]
        for k in ex:
            if k not in w:
                w.append(k)
        return r, w

    def op(self, eng, fn, r=(), w=()):
        r, w = self._split(r, w)
        self._waits(eng, self._collect(eng, r, w))
        E = self.E[eng]
        E["cnt"] += 1
        fn(E["h"]).then_inc(E["sem"], 1)
        self._record((eng, E["sem"], E["cnt"]), r, w)

    def group(self, eng, fns, r=(), w=()):
        r, w = self._split(r, w)
        self._waits(eng, self._collect(eng, r, w))
        E = self.E[eng]
        E["cnt"] += 1
        for i, fn in enumerate(fns):
            ins = fn(E["h"])
            if i == len(fns) - 1:
                ins.then_inc(E["sem"], 1)
        self._record((eng, E["sem"], E["cnt"]), r, w)

    def dma(self, eng, out, in_, semkey, r=(), w=(), **kw):
        r, w = self._split(r, w)
        self._waits(eng, self._collect(eng, r, w))
        if semkey not in self.dsem:
            self.dsem[semkey] = [self.es.enter_context(self.nc.semaphore("d_" + semkey)), 0]
        ds = self.dsem[semkey]
        ds[1] += 16
        self.E[eng]["h"].dma_start(out=out, in_=in_, **kw).then_inc(ds[0], 16)
        self._record(("d:" + semkey, ds[0], ds[1]), r, w)

    def barrier(self):
        for eng, E in self.E.items():
            deps = {}
            for e2, E2 in self.E.items():
                if e2 != eng and E2["cnt"] > 0:
                    deps[e2] = (E2["sem"], E2["cnt"])
            for k, ds in self.dsem.items():
                if ds[1] > 0:
                    deps["d:" + k] = (ds[0], ds[1])
            self._waits(eng, deps)

    def finish(self, eng="sp"):
        deps = {}
        for k, ds in self.dsem.items():
            if ds[1] > 0:
                deps["d:" + k] = (ds[0], ds[1])
        for e2, E2 in self.E.items():
            if e2 != eng and E2["cnt"] > 0:
                deps[e2] = (E2["sem"], E2["cnt"])
        self._waits(eng, deps)


def _ap(base, dims):
    return bass.AP(base.tensor, base.offset, [list(base.ap[0])] + [list(d) for d in dims])


def build_nc(NSEQ, S, do_ffn=True, dump=()):
    nc = bass.Bass("TRN2", target_bir_lowering=False)
    NT = NSEQ * S
    NTT = S // 128
    NCH = S // 512
    dram = {}

    def din(name, shape):
        dram[name] = nc.dram_tensor(name, list(shape), F32, kind="ExternalInput").ap()
        return dram[name]

    x = din("x", [NT, D])
    w_in = din("w_in", [D, 1440])
    w_q_b = din("w_q_b", [384, 768])
    w_kv_b = din("w_kv_b", [256, 1024])
    w_o = din("w_o", [D, D])
    w_gate = din("w_gate", [D, DFF])
    w_up = din("w_up", [D, DFF])
    w_down = din("w_down", [DFF, D])
    gcols_d = din("gcols", [128, 29])
    grow_d = din("grow", [1, 320])
    rope_d = din("rope", [S, 96])
    y = nc.dram_tensor("y", [NT, D], F32, kind="ExternalOutput").ap()
    dumps = {}

    with ExitStack() as es:
        tk = TK(nc, es)
        ps = es.enter_context(nc.psum_tensor("ps", [128, 4096], F32))

        uniq = [0]

        def sb(name, shape, dt, stack=es):
            uniq[0] += 1
            return stack.enter_context(nc.sbuf_tensor("%s_%d" % (name, uniq[0]), list(shape), dt))

        def bank(b, lo=0, hi=512):
            return ps[:, b * 512 + lo: b * 512 + hi]

        def bankbf(b, n):
            return ps[:, b * 512: b * 512 + n // 2].bitcast(BF16)

        ident = sb("ident", [128, 128], BF16)
        ones_c = sb("ones_c", [128, 1], BF16)
        eps_t = sb("eps_t", [128, 1], F32)
        gcols = sb("gcols_t", [128, 29], F32)
        rstd2 = sb("rstd2", [128, NT // 128], F32)
        r2t = sb("r2t", [128, 4], F32)
        es1 = es.enter_context(ExitStack())
        grow = sb("grow_t", [128, 320], F32, es1)
        rope = sb("rope_t", [128, NTT, 96], F32, es1)
        tk.op("pool", lambda h: h.memset(ident[:], 0.0), w=["ident"])
        tk.op("pool", lambda h: h.affine_select(out=ident[:], in_=ident[:], compare_op=ALU.not_equal, fill=1.0,
                                                base=0, pattern=[[-1, 128]], channel_multiplier=1),
              r=["ident"], w=["ident"])
        tk.op("pool", lambda h: h.memset(ones_c[:], 1.0), w=["ones_c"])
        tk.op("pool", lambda h: h.memset(eps_t[:], EPS), w=["eps_t"])
        tk.dma("sp", gcols[:], gcols_d[:, :], "c0", w=["gcols"])
        tk.dma("sp", grow[:], grow_d.partition_broadcast(128), "c1", w=["grow"])
        tk.dma("sp", rope[:], rope_d.rearrange("(t p) c -> p t c", p=128), "c2", w=["rope"])
        G_QA, G_KA, G_QB, G_KB = 0, 96, 192, 256

        w_in_bf = sb("w_in_bf", [128, KC, 1440], BF16, es1)
        wqb_bf = sb("wqb_bf", [128, 3, 768], BF16, es1)
        wkvb_bf = sb("wkvb_bf", [128, 2, 1024], BF16, es1)
        wo_bf = sb("wo_bf", [128, KC, 1024], BF16, es1)
        slab_i = [0]

        def load_weight(stg, dst_fn, src, kcn, ncol, gcol0, engs=("dve", "pool")):
            for kc in range(kcn):
                i = slab_i[0]
                slab_i[0] += 1
                sl = i % 2
                tk.dma("sp", stg[sl][:, 0:ncol], src[kc * 128:(kc + 1) * 128, :], "stg%d" % sl, w=[("stg", sl)])
                eng = engs[i % len(engs)]
                dst = dst_fn(kc)
                if gcol0 is None:
                    tk.op(eng, lambda h, dst=dst, sl=sl: h.tensor_copy(out=dst, in_=stg[sl][:, 0:ncol]),
                          r=[("stg", sl)], w=["wts"])
                else:
                    tk.op(eng, lambda h, dst=dst, sl=sl, c=gcol0 + kc: h.tensor_scalar(
                        out=dst, in0=stg[sl][:, 0:ncol], scalar1=gcols[:, c:c + 1], scalar2=None, op0=ALU.mult),
                        r=[("stg", sl), "gcols"], w=["wts"])

        with ExitStack() as ess:
            stg = [sb("stg%d" % i, [128, 1440], F32, ess) for i in range(2)]
            load_weight(stg, lambda kc: w_in_bf[:, kc, :], w_in, KC, 1440, 0)
            load_weight(stg, lambda kc: wqb_bf[:, kc, :], w_q_b, 3, 768, 8)
            load_weight(stg, lambda kc: wkvb_bf[:, kc, :], w_kv_b, 2, 1024, 11)
            load_weight(stg, lambda kc: wo_bf[:, kc, :], w_o, KC, 1024, 13)
            tk.barrier()

        KaT = sb("KaT", [128, 8, S], BF16, es1)
        VA = sb("VA", [128, NTT, 4, 192], BF16, es1)
        KbT = sb("KbT", [128, 2, S], BF16, es1)
        VB = sb("VB", [128, NTT, 2, 192], BF16, es1)
        NXS = 3
        xsl = [sb("xsl%d" % i, [128, D], F32, es1) for i in range(NXS)]
        xres = [sb("xres%d" % i, [128, D], F32, es1) for i in range(2)]
        xs_bf = [sb("xs%d" % i, [128, D], BF16, es1) for i in range(2)]
        junk = sb("junk", [128, 512], BF16, es1)
        hT = sb("hT", [128, KC, 512], BF16, es1)
        fst = [sb("fst%d" % i, [128, 4], F32, es1) for i in range(2)]
        for t_ in (VA, VB):
            tk.op("pool", lambda h, t_=t_: h.memset(t_[:, :, :, 64:128], 1.0), w=["Vones"])

        fe_cnt = [0]

        def rsqrt_chain(src, dst, tmp, n, scale, bias_ap, rkeys, wkey, tmpkey):
            tk.op("act", lambda h: h.activation(out=tmp, in_=src, func=AF.Ln, scale=scale, bias=bias_ap),
                  r=rkeys, w=[tmpkey])
            tk.op("act", lambda h: h.activation(out=dst, in_=tmp, func=AF.Exp, scale=-0.5),
                  r=[tmpkey], w=[wkey])

        def front_end(row0, hcol, src=None):
            i = fe_cnt[0]
            fe_cnt[0] += 1
            xs_i, b_i = i % NXS, i % 2
            xt, xb, st = xsl[xs_i], xs_bf[b_i], fst[b_i]
            srcap = (x if src is None else src)[row0:row0 + 128, :]
            tk.dma("sp", xt[:], srcap, "x%d" % xs_i, w=[("x", xs_i)])
            tk.op("act", lambda h: h.activation(out=xb[:], in_=xt[:], func=AF.Square, accum_out=st[:, 0:1]),
                  r=[("x", xs_i)], w=[("xs", b_i), ("fst", b_i, 0)])
            rsqrt_chain(st[:, 0:1], st[:, 2:3], st[:, 1:2], 1, 1.0 / D, eps_t[:], [("fst", b_i, 0), "eps_t"],
                        ("fst", b_i, 2), ("fst", b_i, 1))
            tk.op("dve", lambda h: h.tensor_scalar(out=xb[:], in0=xt[:], scalar1=st[:, 2:3], scalar2=None,
                                                   op0=ALU.mult),
                  r=[("x", xs_i), ("fst", b_i, 2)], w=[("xs", b_i)])
            tp = bankbf(0, 1024)
            tk.group("pe", [lambda h, kc=kc: h.transpose(tp[:, kc * 128:(kc + 1) * 128], xb[:, kc * 128:(kc + 1) * 128],
                                                         ident[:]) for kc in range(KC)],
                     r=[("xs", b_i), "ident"], w=[("ps", 0)])
            tk.op("dve", lambda h: h.tensor_copy(out=hT[:, :, hcol:hcol + 128],
                                                 in_=tp.rearrange("p (k t) -> p k t", k=KC)),
                  r=[("ps", 0)], w=[("hT", hcol // 128)])
            return xs_i

        def rope_apply(x1, x2, o1, o2, cos, sin, tmpa, tmpb, eng, rk, wk, tmpk):
            tk.op(eng, lambda h: h.tensor_tensor(out=tmpa, in0=x1, in1=cos, op=ALU.mult), r=rk + ["rope"], w=[tmpk + "a"])
            tk.op(eng, lambda h: h.tensor_tensor(out=tmpb, in0=x2, in1=sin, op=ALU.mult), r=rk + ["rope"], w=[tmpk + "b"])
            tk.op(eng, lambda h: h.tensor_tensor(out=o1, in0=tmpa, in1=tmpb, op=ALU.subtract),
                  r=[tmpk + "a", tmpk + "b"], w=wk)
            tk.op(eng, lambda h: h.tensor_tensor(out=tmpa, in0=x2, in1=cos, op=ALU.mult), r=rk + ["rope"], w=[tmpk + "a"])
            tk.op(eng, lambda h: h.tensor_tensor(out=tmpb, in0=x1, in1=sin, op=ALU.mult), r=rk + ["rope"], w=[tmpk + "b"])
            tk.op(eng, lambda h: h.tensor_tensor(out=o2, in0=tmpa, in1=tmpb, op=ALU.add),
                  r=[tmpk + "a", tmpk + "b"], w=wk)

        def rope_views(t, H, gqa):
            n = 16 if gqa else 8
            c0 = 0 if gqa else 32
            base_c = rope[:, t, c0:c0 + 2 * n]
            base_s = rope[:, t, 48 + c0:48 + c0 + 2 * n]
            dims = [[0, H], [n, 2], [1, n]]
            return _ap(base_c, dims), _ap(base_s, dims)

        for s in range(NSEQ):
            with ExitStack() as esa:
                ckv_bf = sb("ckv_bf", [128, 256], BF16, esa)
                ckvT = sb("ckvT", [128, 2, 128], BF16, esa)
                ast = sb("ast", [128, 48], F32, esa)
                sqs = sb("sqs", [128, 8, 64], F32, esa)
                kn = sb("kn", [128, 8, 64], F32, esa)
                ka_tm = sb("ka_tm", [128, 8, 96], BF16, esa)
                kpe_g = sb("kpe_g", [128, 32], F32, esa)
                kpe_r = sb("kpe_r", [128, 32], F32, esa)
                rta = sb("rta", [128, 64], F32, esa)
                rtb = sb("rtb", [128, 64], F32, esa)
                gkn = sb("gkn", [128, 2, 64], F32, esa)
                gkr = sb("gkr", [128, 2, 64], F32, esa)
                kb_tm = sb("kb_tm", [128, 2, 2, 64], BF16, esa)
                for t in range(NTT):
                    row0 = s * S + t * 128
                    front_end(row0, 0)
                    hk = ("hT", 0)
                    tk.group("pe", [lambda h, kc=kc: h.matmul(bank(1, 0, 288), hT[:, kc, 0:128], w_in_bf[:, kc, 384:672],
                                                              start=(kc == 0), stop=(kc == KC - 1)) for kc in range(KC)],
                             r=[hk, "wts"], w=[("ps", 1)])
                    tk.group("pe", [lambda h, kc=kc: h.matmul(bank(2, 0, 256), hT[:, kc, 0:128], w_in_bf[:, kc, 1184:1440],
                                                              start=(kc == 0), stop=(kc == KC - 1)) for kc in range(KC)],
                             r=[hk, "wts"], w=[("ps", 2)])
                    tk.op("act", lambda h: h.activation(out=junk[:, 0:256], in_=bank(1, 0, 256), func=AF.Square,
                                                        accum_out=ast[:, 0:1]), r=[("ps", 1)], w=["junk", "a0"])
                    tk.op("act", lambda h: h.activation(out=junk[:, 0:32], in_=bank(1, 256, 288), func=AF.Square,
                                                        accum_out=ast[:, 1:2]), r=[("ps", 1)], w=["junk", "a1"])
                    tk.op("dve", lambda h: h.tensor_copy(out=ckv_bf[:], in_=bank(1, 0, 256)), r=[("ps", 1)], w=["ckv_bf"])
                    tk.op("dve", lambda h: h.tensor_tensor(out=kpe_g[:], in0=bank(1, 256, 288),
                                                           in1=grow[:, G_KA + 64:G_KA + 96], op=ALU.mult),
                          r=[("ps", 1), "grow"], w=["kpe_g"])
                    tp3 = bankbf(3, 256)
                    tk.group("pe", [lambda h, kc=kc: h.transpose(tp3[:, kc * 128:(kc + 1) * 128],
                                                                 ckv_bf[:, kc * 128:(kc + 1) * 128], ident[:]) for kc in range(2)],
                             r=["ckv_bf", "ident"], w=[("ps", 3)])
                    tk.op("dve", lambda h: h.tensor_copy(out=ckvT[:], in_=tp3.rearrange("p (k t) -> p k t", k=2)),
                          r=[("ps", 3)], w=["ckvT"])
                    for half in range(2):
                        tk.group("pe", [lambda h, kc=kc, half=half: h.matmul(
                            bank(4 + half), ckvT[:, kc, :], wkvb_bf[:, kc, half * 512:(half + 1) * 512],
                            start=(kc == 0), stop=(kc == 1)) for kc in range(2)],
                            r=["ckvT", "wts"], w=[("ps", 4 + half)])
                    kv = ps[:, 4 * 512:6 * 512].rearrange("p (h d) -> p h d", h=8)
                    rsqrt_chain(ast[:, 0:1], ast[:, 3:4], ast[:, 2:3], 1, 1.0 / 256, eps_t[:], ["a0", "eps_t"], "a3", "a2")
                    tk.op("dve", lambda h: h.tensor_scalar(out=_ap(VA[:, t, 0, 0:64], [[192, 4], [128, 2], [1, 64]]),
                                                           in0=_ap(kv[:, 0, 64:128], [[256, 4], [128, 2], [1, 64]]),
                                                           scalar1=ast[:, 3:4], scalar2=None, op0=ALU.mult),
                          r=[("ps", 4), ("ps", 5), "a3"], w=[("VA", t)])
                    tk.op("act", lambda h: h.activation(out=sqs[:], in_=kv[:, :, 0:64], func=AF.Square),
                          r=[("ps", 4), ("ps", 5)], w=["sqs"])
                    tk.op("dve", lambda h: h.tensor_reduce(out=ast[:, 8:16], in_=sqs[:], axis=AX.X, op=ALU.add),
                          r=["sqs"], w=["a8"])
                    tk.op("dve", lambda h: h.tensor_tensor(out=ast[:, 4:5], in0=ast[:, 3:4], in1=ast[:, 3:4], op=ALU.mult),
                          r=["a3"], w=["a4"])
                    tk.op("dve", lambda h: h.tensor_scalar(out=ast[:, 8:16], in0=ast[:, 8:16], scalar1=ast[:, 4:5],
                                                           scalar2=ast[:, 1:2], op0=ALU.mult, op1=ALU.add),
                          r=["a8", "a4", "a1"], w=["a8"])
                    rsqrt_chain(ast[:, 8:16], ast[:, 24:32], ast[:, 16:24], 8, 1.0 / 96, eps_t[:], ["a8", "eps_t"], "a24", "a16")
                    tk.op("dve", lambda h: h.tensor_scalar(out=ast[:, 32:40], in0=ast[:, 24:32], scalar1=ast[:, 3:4],
                                                           scalar2=None, op0=ALU.mult), r=["a24", "a3"], w=["a32"])
                    tk.op("dve", lambda h: h.tensor_tensor(out=kn[:], in0=kv[:, :, 0:64],
                                                           in1=_ap(ast[:, 32:40], [[1, 8], [0, 64]]), op=ALU.mult),
                          r=[("ps", 4), ("ps", 5), "a32"], w=["kn"])
                    tk.op("pool", lambda h: h.tensor_tensor(out=ka_tm[:, :, 0:64], in0=kn[:],
                                                            in1=_ap(grow[:, G_KA:G_KA + 64], [[0, 8], [1, 64]]), op=ALU.mult),
                          r=["kn", "grow"], w=["ka_tm_n"])
                    cv, sv = rope_views(t, 1, False)
                    kx = kpe_g[:].rearrange("p (o r f n) -> p o r f n", o=1, r=2, f=2)
                    ko = kpe_r[:].rearrange("p (o r f n) -> p o r f n", o=1, r=2, f=2)
                    ra = rta[:, 0:16].rearrange("p (o r n) -> p o r n", o=1, r=2)
                    rb = rtb[:, 0:16].rearrange("p (o r n) -> p o r n", o=1, r=2)
                    rope_apply(kx[:, :, :, 0, :], kx[:, :, :, 1, :], ko[:, :, :, 0, :], ko[:, :, :, 1, :], cv, sv, ra, rb,
                               "dve", ["kpe_g"], ["kpe_r"], "rt")
                    tk.op("dve", lambda h: h.tensor_tensor(out=ka_tm[:, :, 64:96], in0=_ap(kpe_r[:], [[0, 8], [1, 32]]),
                                                           in1=_ap(ast[:, 24:32], [[1, 8], [0, 32]]), op=ALU.mult),
                          r=["kpe_r", "a24"], w=["ka_tm_r"])
                    tp6 = bankbf(6, 1024)
                    tk.group("pe", [lambda h, hh=hh: h.transpose(tp6[0:96, hh * 128:(hh + 1) * 128], ka_tm[:, hh, :], ident[:])
                                    for hh in range(8)], r=["ka_tm_n", "ka_tm_r", "ident"], w=[("ps", 6)])
                    tk.op("act", lambda h: h.copy(out=KaT[0:96, :, t * 128:(t + 1) * 128],
                                                  in_=tp6[0:96, :].rearrange("p (k t) -> p k t", k=8)),
                          r=[("ps", 6)], w=[("KaT", t)])
                    gk = bank(2, 0, 128).rearrange("p (g d) -> p g d", g=2)
                    tk.op("dve", lambda h: h.tensor_copy(out=_ap(VB[:, t, 0, 0:64], [[192, 2], [128, 2], [1, 64]]),
                                                         in_=_ap(bank(2, 128, 256), [[64, 2], [0, 2], [1, 64]])),
                          r=[("ps", 2)], w=[("VB", t)])
                    tk.op("act", lambda h: h.activation(out=sqs[:, 0:2, :], in_=gk, func=AF.Square), r=[("ps", 2)], w=["sqs"])
                    tk.op("dve", lambda h: h.tensor_reduce(out=ast[:, 40:42], in_=sqs[:, 0:2, :], axis=AX.X, op=ALU.add),
                          r=["sqs"], w=["a40"])
                    rsqrt_chain(ast[:, 40:42], ast[:, 44:46], ast[:, 42:44], 2, 1.0 / 64, eps_t[:], ["a40", "eps_t"], "a44", "a42")
                    tk.op("dve", lambda h: h.tensor_tensor(out=gkn[:], in0=gk, in1=_ap(ast[:, 44:46], [[1, 2], [0, 64]]),
                                                           op=ALU.mult), r=[("ps", 2), "a44"], w=["gkn"])
                    tk.op("dve", lambda h: h.tensor_tensor(out=gkn[:], in0=gkn[:],
                                                           in1=_ap(grow[:, G_KB:G_KB + 64], [[0, 2], [1, 64]]), op=ALU.mult),
                          r=["gkn", "grow"], w=["gkn"])
                    cv, sv = rope_views(t, 2, True)
                    gx = gkn[:].rearrange("p g (r f n) -> p g r f n", r=2, f=2)
                    go = gkr[:].rearrange("p g (r f n) -> p g r f n", r=2, f=2)
                    ra = rta[:, 0:64].rearrange("p (g r n) -> p g r n", g=2, r=2)
                    rb = rtb[:, 0:64].rearrange("p (g r n) -> p g r n", g=2, r=2)
                    rope_apply(gx[:, :, :, 0, :], gx[:, :, :, 1, :], go[:, :, :, 0, :], go[:, :, :, 1, :], cv, sv, ra, rb,
                               "dve", ["gkn"], ["gkr"], "rt")
                    tk.op("dve", lambda h: h.tensor_copy(out=kb_tm[:], in_=_ap(gkr[:], [[64, 2], [0, 2], [1, 64]])),
                          r=["gkr"], w=["kb_tm"])
                    tp3b = ps[:, 3 * 512 + 128: 3 * 512 + 256].bitcast(BF16)
                    tk.group("pe", [lambda h, g=g: h.transpose(tp3b[:, g * 128:(g + 1) * 128],
                                                               kb_tm[:, g, :, :].rearrange("p a d -> p (a d)"), ident[:])
                                    for g in range(2)], r=["kb_tm", "ident"], w=[("ps", 3)])
                    tk.op("act", lambda h: h.copy(out=KbT[:, :, t * 128:(t + 1) * 128],
                                                  in_=tp3b.rearrange("p (g t) -> p g t", g=2)),
                          r=[("ps", 3)], w=[("KbT", t)])
            tk.barrier()
            if "kv" in dump and s == NSEQ - 1:
                for nm, t_ in (("KaT", KaT), ("VA", VA), ("KbT", KbT), ("VB", VB)):
                    shp = [128, int(np.prod(t_.shape[1:]))]
                    dumps[nm] = nc.dram_tensor("dump_" + nm, shp, BF16, kind="ExternalOutput").ap()
                    nd = len(t_.shape)
                    src = t_[:] if nd == 2 else (t_[:].rearrange("p a b -> p (a b)") if nd == 3
                                                 else t_[:].rearrange("p a b c -> p (a b c)"))
                    tk.dma("sp", dumps[nm][:, :], src, "dump", r=[])

            with ExitStack() as esb:
                cq_bf = sb("cq_bf", [128, 384], BF16, esb)
                cqT = sb("cqT", [128, 3, 128], BF16, esb)
                bst = sb("bst", [128, 64], F32, esb)
                sqq = sb("sqq", [128, 8, 96], F32, esb)
                qn = sb("qn", [128, 8, 96], F32, esb)
                qa_tm = sb("qa_tm", [128, 8, 96], BF16, esb)
                qb_tm = sb("qb_tm", [128, 8, 64], BF16, esb)
                rta = sb("rtaB", [128, 256], F32, esb)
                rtb = sb("rtbB", [128, 256], F32, esb)
                QaT = sb("QaT", [128, 8, 512], BF16, esb)
                QbT = sb("QbT", [128, 4, 512], BF16, esb)
                NP = 3
                PT = [sb("PT%d" % i, [128, 1024], BF16, esb) for i in range(NP)]
                rec = sb("rec", [128, 512], F32, esb)
                onT = sb("onT", [128, KC, 512], BF16, esb)
                sq_on = [sb("sq_on%d" % i, [128, 512], BF16, esb) for i in range(2)]
                ost = sb("ost", [128, 32], F32, esb)
                for c in range(NCH):
                    for i in range(4):
                        front_end(s * S + c * 512 + i * 128, i * 128)
                    for i in range(4):
                        t = c * 4 + i
                        hk = ("hT", i)
                        tk.group("pe", [lambda h, kc=kc: h.matmul(bank(1, 128, 512), hT[:, kc, i * 128:(i + 1) * 128],
                                                                  w_in_bf[:, kc, 0:384], start=(kc == 0), stop=(kc == KC - 1))
                                        for kc in range(KC)], r=[hk, "wts"], w=[("ps", 1)])
                        tk.group("pe", [lambda h, kc=kc: h.matmul(bank(2), hT[:, kc, i * 128:(i + 1) * 128],
                                                                  w_in_bf[:, kc, 672:1184], start=(kc == 0), stop=(kc == KC - 1))
                                        for kc in range(KC)], r=[hk, "wts"], w=[("ps", 2)])
                        tk.op("act", lambda h: h.activation(out=junk[:, 0:384], in_=bank(1, 128, 512), func=AF.Square,
                                                            accum_out=bst[:, 0:1]), r=[("ps", 1)], w=["junk", "b0"])
                        tk.op("dve", lambda h: h.tensor_copy(out=cq_bf[:], in_=bank(1, 128, 512)), r=[("ps", 1)], w=["cq_bf"])
                        tp3 = bankbf(3, 384)
                        tk.group("pe", [lambda h, kc=kc: h.transpose(tp3[:, kc * 128:(kc + 1) * 128],
                                                                     cq_bf[:, kc * 128:(kc + 1) * 128], ident[:])
                                        for kc in range(3)], r=["cq_bf", "ident"], w=[("ps", 3)])
                        tk.op("dve", lambda h: h.tensor_copy(out=cqT[:], in_=tp3.rearrange("p (k t) -> p k t", k=3)),
                              r=[("ps", 3)], w=["cqT"])
                        for half in range(2):
                            dst = bank(4, 128, 512) if half == 0 else bank(5, 0, 384)
                            tk.group("pe", [lambda h, kc=kc, half=half, dst=dst: h.matmul(
                                dst, cqT[:, kc, :], wqb_bf[:, kc, half * 384:(half + 1) * 384],
                                start=(kc == 0), stop=(kc == 2)) for kc in range(3)],
                                r=["cqT", "wts"], w=[("ps", 4 + half)])
                        qa_ps = ps[:, 4 * 512 + 128: 4 * 512 + 128 + 768].rearrange("p (h d) -> p h d", h=8)
                        tk.op("dve", lambda h: h.tensor_scalar(out=bst[:, 1:2], in0=bst[:, 0:1], scalar1=EPS / 384.0,
                                                               scalar2=EPS * EPS, op0=ALU.mult, op1=ALU.add),
                              r=["b0"], w=["b1"])
                        tk.op("act", lambda h: h.activation(out=sqq[:], in_=qa_ps, func=AF.Square),
                              r=[("ps", 4), ("ps", 5)], w=["sqq"])
                        tk.op("dve", lambda h: h.tensor_reduce(out=bst[:, 8:16], in_=sqq[:], axis=AX.X, op=ALU.add),
                              r=["sqq"], w=["b8"])
                        rsqrt_chain(bst[:, 8:16], bst[:, 24:32], bst[:, 16:24], 8, 1.0 / 96, bst[:, 1:2], ["b8", "b1"], "b24", "b16")
                        tk.op("dve", lambda h: h.tensor_tensor(out=qn[:], in0=qa_ps, in1=_ap(bst[:, 24:32], [[1, 8], [0, 96]]),
                                                               op=ALU.mult), r=[("ps", 4), ("ps", 5), "b24"], w=["qn"])
                        tk.op("pool", lambda h: h.tensor_tensor(out=qn[:], in0=qn[:],
                                                                in1=_ap(grow[:, G_QA:G_QA + 96], [[0, 8], [1, 96]]), op=ALU.mult),
                              r=["qn", "grow"], w=["qn"])
                        tk.op("pool", lambda h: h.tensor_copy(out=qa_tm[:, :, 0:64], in_=qn[:, :, 0:64]), r=["qn"], w=["qa_tm_n"])
                        cv, sv = rope_views(t, 8, False)
                        qx = qn[:, :, 64:96].rearrange("p h (r f n) -> p h r f n", r=2, f=2)
                        qo = qa_tm[:, :, 64:96].rearrange("p h (r f n) -> p h r f n", r=2, f=2)
                        ra = rta[:, 0:128].rearrange("p (h r n) -> p h r n", h=8, r=2)
                        rb = rtb[:, 0:128].rearrange("p (h r n) -> p h r n", h=8, r=2)
                        rope_apply(qx[:, :, :, 0, :], qx[:, :, :, 1, :], qo[:, :, :, 0, :], qo[:, :, :, 1, :], cv, sv, ra, rb,
                                   "dve", ["qn"], ["qa_tm_r"], "rtB")
                        tp6 = bankbf(6, 1024)
                        tk.group("pe", [lambda h, hh=hh: h.transpose(tp6[0:96, hh * 128:(hh + 1) * 128], qa_tm[:, hh, :], ident[:])
                                        for hh in range(8)], r=["qa_tm_n", "qa_tm_r", "ident"], w=[("ps", 6)])
                        tk.op("act", lambda h: h.copy(out=QaT[0:96, :, i * 128:(i + 1) * 128],
                                                      in_=tp6[0:96, :].rearrange("p (k t) -> p k t", k=8)),
                              r=[("ps", 6)], w=[("QaT", i)])
                        gq = bank(2).rearrange("p (h d) -> p h d", h=8)
                        sqb = sqq[:].rearrange("p h d -> p (h d)")[:, 0:512].rearrange("p (h d) -> p h d", h=8)
                        qbn = qn[:].rearrange("p h d -> p (h d)")[:, 0:512].rearrange("p (h d) -> p h d", h=8)
                        tk.op("act", lambda h: h.activation(out=sqb, in_=gq, func=AF.Square), r=[("ps", 2)], w=["sqq"])
                        tk.op("dve", lambda h: h.tensor_reduce(out=bst[:, 32:40], in_=sqb, axis=AX.X, op=ALU.add),
                              r=["sqq"], w=["b32"])
                        rsqrt_chain(bst[:, 32:40], bst[:, 48:56], bst[:, 40:48], 8, 1.0 / 64, eps_t[:], ["b32", "eps_t"], "b48", "b40")
                        tk.op("dve", lambda h: h.tensor_tensor(out=qbn, in0=gq, in1=_ap(bst[:, 48:56], [[1, 8], [0, 64]]),
                                                               op=ALU.mult), r=[("ps", 2), "b48", "qa_tm_n", "qa_tm_r"], w=["qn"])
                        tk.op("pool", lambda h: h.tensor_tensor(out=qbn, in0=qbn,
                                                                in1=_ap(grow[:, G_QB:G_QB + 64], [[0, 8], [1, 64]]), op=ALU.mult),
                              r=["qn", "grow"], w=["qn"])
                        cv, sv = rope_views(t, 8, True)
                        bx = qbn.rearrange("p h (r f n) -> p h r f n", r=2, f=2)
                        bo = qb_tm[:].rearrange("p h (r f n) -> p h r f n", r=2, f=2)
                        ra = rta[:, 0:256].rearrange("p (h r n) -> p h r n", h=8, r=2)
                        rb = rtb[:, 0:256].rearrange("p (h r n) -> p h r n", h=8, r=2)
                        rope_apply(bx[:, :, :, 0, :], bx[:, :, :, 1, :], bo[:, :, :, 0, :], bo[:, :, :, 1, :], cv, sv, ra, rb,
                                   "dve", ["qn"], ["qb_tm"], "rtB")
                        tp7 = bankbf(7, 512)
                        qbf = qb_tm[:].rearrange("p h d -> p (h d)")
                        tk.group("pe", [lambda h, j=j: h.transpose(tp7[:, j * 128:(j + 1) * 128], qbf[:, j * 128:(j + 1) * 128],
                                                                   ident[:]) for j in range(4)],
                                 r=["qb_tm", "ident"], w=[("ps", 7)])
                        tk.op("act", lambda h: h.copy(out=QbT[:, :, i * 128:(i + 1) * 128],
                                                      in_=tp7.rearrange("p (k t) -> p k t", k=4)),
                              r=[("ps", 7)], w=[("QbT", i)])

                    qkeys_a = [("QaT", i) for i in range(4)]
                    qkeys_b = [("QbT", i) for i in range(4)]
                    units = [(hd, p) for hd in range(16) for p in range(NTT // 2)]
                    pend = None
                    pcount = [0]

                    def s_mm(hd, p, sp_i):
                        fns = []
                        rk = []
                        for j in range(2):
                            kb = 2 * p + j
                            if hd < 8:
                                lhsT = KaT[0:96, hd, kb * 128:(kb + 1) * 128]
                                rhs = QaT[0:96, hd, :]
                                rk += [("KaT", kb)]
                            else:
                                jq = hd - 8
                                g, hf = jq // 4, jq % 2
                                lhsT = KbT[hf * 64:(hf + 1) * 64, g, kb * 128:(kb + 1) * 128]
                                rhs = QbT[hf * 64:(hf + 1) * 64, jq // 2, :]
                                rk += [("KbT", kb)]
                            dst = bank(2 * sp_i + j)
                            fns.append(lambda h, dst=dst, lhsT=lhsT, rhs=rhs: h.matmul(dst, lhsT, rhs, start=True, stop=True))
                        rk += (qkeys_a if hd < 8 else qkeys_b)
                        tk.group("pe", fns, r=rk, w=[("ps", 2 * sp_i), ("ps", 2 * sp_i + 1)])

                    def exp_pv(hd, p, sp_i):
                        slot = pcount[0] % NP
                        pcount[0] += 1
                        scale = (96.0 if hd < 8 else 64.0) ** -0.5
                        src = ps[:, sp_i * 1024:(sp_i + 1) * 1024]
                        tk.op("act", lambda h: h.activation(out=PT[slot][:], in_=src, func=AF.Exp, scale=scale),
                              r=[("ps", 2 * sp_i), ("ps", 2 * sp_i + 1)], w=[("PT", slot)])
                        ob = 4 + hd % 2
                        fns = []
                        rk = [("PT", slot), "Vones"]
                        for j in range(2):
                            kb = 2 * p + j
                            if hd < 8:
                                lo = 0 if hd % 2 == 0 else 64
                                lhsT = VA[:, kb, hd // 2, lo:lo + 128]
                                rk.append(("VA", kb))
                            else:
                                jq = hd - 8
                                g = jq // 4
                                lo = 0 if jq % 2 == 0 else 64
                                lhsT = VB[:, kb, g, lo:lo + 128]
                                rk.append(("VB", kb))
                            fns.append(lambda h, lhsT=lhsT, j=j, kb=kb: h.matmul(
                                bank(ob), lhsT, PT[slot][:, j * 512:(j + 1) * 512], start=(kb == 0), stop=(kb == NTT - 1)))
                        tk.group("pe", fns, r=rk, w=[("ps", ob)])
                        if p == NTT // 2 - 1:
                            head_done(hd, ob)

                    def head_done(hd, ob):
                        cidx = hd // 2
                        if hd % 2 == 0:
                            num, den, o_lo = bank(ob)[0:64, :], bank(ob)[64:128, :], 0
                            rc = rec[64:128, :]
                        else:
                            num, den, o_lo = bank(ob)[64:128, :], bank(ob)[0:64, :], 64
                            rc = rec[0:64, :]
                        tk.op("dve", lambda h: h.reciprocal(out=rc, in_=den), r=[("ps", ob)], w=["rec"])
                        tk.op("dve", lambda h: h.tensor_tensor(out=onT[o_lo:o_lo + 64, cidx, :], in0=num, in1=rc, op=ALU.mult),
                              r=[("ps", ob), "rec"], w=[("onT", cidx, hd % 2)])
                        if hd % 2 == 1:
                            sq = sq_on[cidx % 2]
                            tk.op("pool", lambda h: h.tensor_tensor(out=sq[:], in0=onT[:, cidx, :], in1=onT[:, cidx, :], op=ALU.mult),
                                  r=[("onT", cidx, 0), ("onT", cidx, 1)], w=[("sq_on", cidx % 2)])
                            tk.group("pe", [lambda h, i=i: h.matmul(bank(6, i * 8 + cidx, i * 8 + cidx + 1), sq[:, i * 128:(i + 1) * 128],
                                                                    ones_c[:], start=True, stop=True) for i in range(4)],
                                     r=[("sq_on", cidx % 2), "ones_c"], w=[("ps", 6)])

                    for ui, (hd, p) in enumerate(units):
                        s_mm(hd, p, ui % 2)
                        if pend is not None:
                            exp_pv(*pend)
                        pend = (hd, p, ui % 2)
                    exp_pv(*pend)

                    tk.op("dve", lambda h: h.tensor_reduce(out=ost[:, 0:8],
                                                           in_=bank(6, 0, 32).rearrange("p (i g k) -> p i g k", i=4, g=2),
                                                           axis=AX.X, op=ALU.add), r=[("ps", 6)], w=["o0"])
                    rsqrt_chain(ost[:, 0:8], ost[:, 16:24], ost[:, 8:16], 8, 1.0 / 512, eps_t[:], ["o0", "eps_t"], "o16", "o8")
                    for i in range(4):
                        row0 = s * S + c * 512 + i * 128
                        xr = xres[i % 2]
                        tk.dma("sp", xr[:], x[row0:row0 + 128, :], "xr%d" % (i % 2), w=[("xres", i % 2)])
                        b0 = 0 if i % 2 == 0 else 4
                        for g in range(2):
                            for half in range(2):
                                bk = b0 + g * 2 + half
                                tk.group("pe", [lambda h, kc=kc, bk=bk, g=g, half=half: h.matmul(
                                    bank(bk), onT[:, g * 4 + kc, i * 128:(i + 1) * 128],
                                    wo_bf[:, g * 4 + kc, half * 512:(half + 1) * 512], start=(kc == 0), stop=(kc == 3))
                                    for kc in range(4)],
                                    r=[("onT", g * 4 + kc, q) for kc in range(4) for q in range(2)] + ["wts"], w=[("ps", bk)])
                        for g in range(2):
                            acc = ps[:, (b0 + g * 2) * 512:(b0 + g * 2 + 2) * 512]
                            tk.op("dve", lambda h, acc=acc, g=g: h.scalar_tensor_tensor(
                                out=xr[:], in0=acc, scalar=ost[:, 16 + i * 2 + g:16 + i * 2 + g + 1], in1=xr[:],
                                op0=ALU.mult, op1=ALU.add),
                                r=[("ps", b0 + g * 2), ("ps", b0 + g * 2 + 1), "o16", ("xres", i % 2)], w=[("xres", i % 2)])
                        tk.dma("sp", y[row0:row0 + 128, :], xr[:], "yst%d" % (i % 2), r=[("xres", i % 2)])
                        gt = row0 // 128
                        tk.op("act", lambda h: h.activation(out=junk[:, 0:512], in_=xr[:, 0:512], func=AF.Square,
                                                            accum_out=r2t[:, 0:1]), r=[("xres", i % 2)], w=["junk", "r2a"])
                        tk.op("act", lambda h: h.activation(out=junk[:, 0:512], in_=xr[:, 512:1024], func=AF.Square,
                                                            accum_out=r2t[:, 1:2]), r=[("xres", i % 2)], w=["junk", "r2b"])
                        tk.op("dve", lambda h: h.tensor_tensor(out=r2t[:, 2:3], in0=r2t[:, 0:1], in1=r2t[:, 1:2], op=ALU.add),
                              r=["r2a", "r2b"], w=["r2c"])
                        rsqrt_chain(r2t[:, 2:3], rstd2[:, gt:gt + 1], r2t[:, 3:4], 1, 1.0 / D, eps_t[:], ["r2c", "eps_t"],
                                    ("rstd2", gt), "r2d")
            tk.barrier()
        es1.close()
        tk.barrier()

        if do_ffn:
            with ExitStack() as es2:
                wg_bf = sb("wg_bf", [128, KC, DFF], BF16, es2)
                wu_bf = sb("wu_bf", [128, KC, DFF], BF16, es2)
                wd_bf = sb("wd_bf", [128, FC, D], BF16, es2)
                NX2 = 6
                x2sl = [sb("x2sl%d" % i, [128, D], F32, es2) for i in range(NX2)]
                xs2 = [sb("xs2_%d" % i, [128, D], BF16, es2) for i in range(2)]
                hT2 = [sb("hT2_%d" % i, [128, KC, 512], BF16, es2) for i in range(2)]
                with ExitStack() as ess:
                    stg = [sb("stgf%d" % i, [128, 1408], F32, ess) for i in range(2)]
                    for hlf in range(2):
                        c0 = hlf * 1408
                        load_weight(stg, lambda kc, c0=c0: wg_bf[:, kc, c0:c0 + 1408], w_gate[:, c0:c0 + 1408], KC, 1408, 21)
                        load_weight(stg, lambda kc, c0=c0: wu_bf[:, kc, c0:c0 + 1408], w_up[:, c0:c0 + 1408], KC, 1408, 21)
                    load_weight(stg, lambda kc: wd_bf[:, kc, :], w_down, FC, 1024, None)
                    tk.barrier()
                aT = sb("aT", [128, FC, 512], BF16, es2)
                sg = [sb("sg%d" % i, [128, 512], F32, es2) for i in range(2)]
                NCH2 = NT // 512
                fe2 = [0]

                def front_end2(c, i):
                    n = fe2[0]
                    fe2[0] += 1
                    sl, b_i = (c * 4 + i) % NX2, n % 2
                    gt = c * 4 + i
                    xt, xb = x2sl[sl], xs2[b_i]
                    tk.dma("sp", xt[:], y[gt * 128:(gt + 1) * 128, :], "x2_%d" % sl, w=[("x2", sl)])
                    tk.op("dve", lambda h: h.tensor_scalar(out=xb[:], in0=xt[:], scalar1=rstd2[:, gt:gt + 1], scalar2=None,
                                                           op0=ALU.mult), r=[("x2", sl), ("rstd2", gt)], w=[("xs2", b_i)])
                    tp = bankbf(7, 1024)
                    tk.group("pe", [lambda h, kc=kc: h.transpose(tp[:, kc * 128:(kc + 1) * 128], xb[:, kc * 128:(kc + 1) * 128],
                                                                 ident[:]) for kc in range(KC)],
                             r=[("xs2", b_i), "ident"], w=[("ps", 7)])
                    tk.op("dve", lambda h: h.tensor_copy(out=hT2[c % 2][:, :, i * 128:(i + 1) * 128],
                                                         in_=tp.rearrange("p (k t) -> p k t", k=KC)),
                          r=[("ps", 7)], w=[("hT2", c % 2, i)])

                def down(c, i):
                    sl = (c * 4 + i) % NX2
                    gt = c * 4 + i
                    for half in range(2):
                        bk = 4 + (2 * i + half) % 3
                        tk.group("pe", [lambda h, f=f, bk=bk, half=half: h.matmul(
                            bank(bk), aT[:, f, i * 128:(i + 1) * 128], wd_bf[:, f, half * 512:(half + 1) * 512],
                            start=(f == 0), stop=(f == FC - 1)) for f in range(FC)],
                            r=[("aT", f) for f in range(FC)] + ["wts"], w=[("ps", bk)])
                        tk.op("dve", lambda h, bk=bk, half=half: h.tensor_tensor(
                            out=x2sl[sl][:, half * 512:(half + 1) * 512], in0=bank(bk),
                            in1=x2sl[sl][:, half * 512:(half + 1) * 512], op=ALU.add),
                            r=[("ps", bk), ("x2", sl)], w=[("x2", sl)])
                    tk.dma("sp", y[gt * 128:(gt + 1) * 128, :], x2sl[sl][:], "y2_%d" % sl, r=[("x2", sl)])

                for i in range(4):
                    front_end2(0, i)
                for c in range(NCH2):
                    hk = [("hT2", c % 2, i) for i in range(4)]
                    for f in range(FC):
                        gb, ub = f % 2, 2 + f % 2
                        tk.group("pe", [lambda h, kc=kc, f=f, gb=gb: h.matmul(
                            bank(gb), wg_bf[:, kc, f * 128:(f + 1) * 128], hT2[c % 2][:, kc, :],
                            start=(kc == 0), stop=(kc == KC - 1)) for kc in range(KC)], r=hk + ["wts"], w=[("ps", gb)])
                        tk.group("pe", [lambda h, kc=kc, f=f, ub=ub: h.matmul(
                            bank(ub), wu_bf[:, kc, f * 128:(f + 1) * 128], hT2[c % 2][:, kc, :],
                            start=(kc == 0), stop=(kc == KC - 1)) for kc in range(KC)], r=hk + ["wts"], w=[("ps", ub)])
                        tk.op("act", lambda h, f=f, gb=gb: h.activation(out=sg[f % 2][:], in_=bank(gb), func=AF.Silu),
                              r=[("ps", gb)], w=[("sg", f % 2)])
                        tk.op("dve", lambda h, f=f, ub=ub: h.tensor_tensor(out=aT[:, f, :], in0=bank(ub), in1=sg[f % 2][:],
                                                                          op=ALU.mult),
                              r=[("ps", ub), ("sg", f % 2)], w=[("aT", f)])
                        if c + 1 < NCH2 and f in (7, 15):
                            front_end2(c + 1, 0 if f == 7 else 1)
                    down(c, 0)
                    down(c, 1)
                    if c + 1 < NCH2:
                        front_end2(c + 1, 2)
                    down(c, 2)
                    if c + 1 < NCH2:
                        front_end2(c + 1, 3)
                    down(c, 3)
        tk.finish("sp")
    return nc, dumps


def _rope_table(S):
    t = np.arange(S)
    row = (t // 64).astype(np.float32)
    col = (t % 64).astype(np.float32)
    out = np.zeros((S, 96), np.float32)
    inv_g = (np.float32(10000.0) ** (-(np.arange(0, 32, 2, dtype=np.float32) / np.float32(32)))).astype(np.float32)
    inv_m = (np.float32(10000.0) ** (-(np.arange(0, 16, 2, dtype=np.float32) / np.float32(16)))).astype(np.float32)
    ang = np.concatenate([row[:, None] * inv_g[None], col[:, None] * inv_g[None],
                          row[:, None] * inv_m[None], col[:, None] * inv_m[None]], axis=1).astype(np.float32)
    out[:, 0:48] = np.cos(ang)
    out[:, 48:96] = np.sin(ang)
    return out


def _host_inputs(inp, S):
    f = lambda a: np.ascontiguousarray(np.asarray(a, dtype=np.float32))
    col = lambda g: f(g).reshape(-1, 128).T
    gcols = np.concatenate([col(inp["norm1_g"][0]), col(inp["q_a_norm_g"][0]), col(inp["kv_a_norm_g"][0]),
                            col(np.concatenate([f(inp["mla_out_norm_g"][0]), f(inp["gqa_out_norm_g"][0])])),
                            col(inp["norm2_g"][0])], axis=1)
    grow = np.concatenate([f(inp["mla_q_norm_g"][0]), f(inp["mla_k_norm_g"][0]),
                           f(inp["gqa_q_norm_g"][0]), f(inp["gqa_k_norm_g"][0])])[None, :]
    shared = {
        "w_in": f(inp["w_in"][0]), "w_q_b": f(inp["w_q_b"][0]), "w_kv_b": f(inp["w_kv_b"][0]), "w_o": f(inp["w_o"][0]),
        "w_gate": f(inp["w_gate"][0]), "w_up": f(inp["w_up"][0]), "w_down": f(inp["w_down"][0]),
        "gcols": f(gcols), "grow": f(grow), "rope": _rope_table(S),
    }
    return shared


def kernel(**inputs):
    x = np.asarray(inputs["x"], dtype=np.float32)
    B, S, _ = x.shape
    nseq = B // N_CORES
    shared = _host_inputs(inputs, S)
    nc, _ = build_nc(nseq, S)
    in_maps = []
    for c in range(N_CORES):
        m = dict(shared)
        m["x"] = np.ascontiguousarray(x[c * nseq:(c + 1) * nseq].reshape(nseq * S, D))
        in_maps.append(m)
    res = run_bass_kernel_spmd(nc, in_maps, core_ids=list(range(N_CORES)))
    out = np.concatenate([np.asarray(r["y"]).reshape(nseq, S, D) for r in res.results], axis=0)
    return out.astype(np.float32)
```

```python
from contextlib import ExitStack

import numpy as np
import concourse.bass as bass
import concourse.mybir as mybir
from concourse.bass_utils import run_bass_kernel_spmd

F32 = mybir.dt.float32
BF16 = mybir.dt.bfloat16
AF = mybir.ActivationFunctionType
ALU = mybir.AluOpType
AX = mybir.AxisListType

D = 1024
KC = 8
DFF = 2816
FC = 22
EPS = 1e-6
N_CORES = 8


class TK:
    def __init__(self, nc, es):
        self.nc = nc
        self.es = es
        self.E = {}
        for name, h in (("pe", nc.tensor), ("act", nc.scalar), ("dve", nc.vector),
                        ("pool", nc.gpsimd), ("sp", nc.sync)):
            self.E[name] = dict(h=h, sem=es.enter_context(nc.semaphore("e_" + name)), cnt=0, waited={})
        self.res = {}
        self.dsem = {}

    @staticmethod
    def _excl(key):
        return isinstance(key, tuple) and key[0] == "ps"

    def _collect(self, eng, reads, writes):
        deps = {}

        def add(st, same_ok):
            if st is None:
                return
            key, sem, val = st
            if key == eng and (eng == "pe" or not same_ok):
                return
            if key not in deps or deps[key][1] < val:
                deps[key] = (sem, val)

        for k in reads:
            d = self.res.get(k)
            if d:
                add(d["w"], True)
        for k in writes:
            d = self.res.get(k)
            if d:
                add(d["w"], False)
                for st in d["r"].values():
                    add(st, False)
        return deps

    def _waits(self, eng, deps):
        E = self.E[eng]
        for key, (sem, val) in deps.items():
            if E["waited"].get(key, 0) >= val:
                continue
            E["waited"][key] = val
            E["h"].wait_ge(sem, val)

    def _record(self, stamp, reads, writes):
        for k in reads:
            d = self.res.setdefault(k, {"w": None, "r": {}})
            d["r"][stamp[0]] = stamp
        for k in writes:
            self.res[k] = {"w": stamp, "r": {}}

    def _split(self, r, w):
        r = list(r)
        w = list(w)
        ex = [k for k in r if self._excl(k)]
        r = [k for k in r if not self._excl(k)]
        for k in ex:
            if k not in w:
                w.append(k)
        return r, w

    def op(self, eng, fn, r=(), w=()):
        if eng == "act_tt":
            eng = "dve"
        r, w = self._split(r, w)
        self._waits(eng, self._collect(eng, r, w))
        E = self.E[eng]
        E["cnt"] += 1
        fn(E["h"]).then_inc(E["sem"], 1)
        self._record((eng, E["sem"], E["cnt"]), r, w)

    def group(self, eng, fns, r=(), w=()):
        r, w = self._split(r, w)
        self._waits(eng, self._collect(eng, r, w))
        E = self.E[eng]
        E["cnt"] += 1
        for i, fn in enumerate(fns):
            ins = fn(E["h"])
            if i == len(fns) - 1:
                ins.then_inc(E["sem"], 1)
        self._record((eng, E["sem"], E["cnt"]), r, w)

    def dma(self, eng, out, in_, semkey, r=(), w=(), **kw):
        r, w = self._split(r, w)
        self._waits(eng, self._collect(eng, r, w))
        if semkey not in self.dsem:
            self.dsem[semkey] = [self.es.enter_context(self.nc.semaphore("d_" + semkey)), 0]
        ds = self.dsem[semkey]
        ds[1] += 16
        self.E[eng]["h"].dma_start(out=out, in_=in_, **kw).then_inc(ds[0], 16)
        self._record(("d:" + semkey, ds[0], ds[1]), r, w)

    def barrier(self):
        for eng, E in self.E.items():
            deps = {}
            for e2, E2 in self.E.items():
                if e2 != eng and E2["cnt"] > 0:
                    deps[e2] = (E2["sem"], E2["cnt"])
            for k, ds in self.dsem.items():
                if ds[1] > 0:
                    deps["d:" + k] = (ds[0], ds[1])
            self._waits(eng, deps)

    def finish(self, eng="sp"):
        deps = {}
        for k, ds in self.dsem.items():
            if ds[1] > 0:
                deps["d:" + k] = (ds[0], ds[1])
        for e2, E2 in self.E.items():
            if e2 != eng and E2["cnt"] > 0:
                deps[e2] = (E2["sem"], E2["cnt"])
        self._waits(eng, deps)


def _ap(base, dims):
    return bass.AP(base.tensor, base.offset, [list(base.ap[0])] + [list(d) for d in dims])


def drive(gens, width):
    pending = list(gens)
    active = []
    while pending or active:
        while pending and len(active) < width:
            active.append(pending.pop(0))
        for g in list(active):
            try:
                next(g)
            except StopIteration:
                active.remove(g)


def build_nc(NSEQ, S, do_ffn=True, dump=()):
    nc = bass.Bass("TRN2", target_bir_lowering=False)
    NT = NSEQ * S
    NTT = S // 128
    NCH = S // 512

    def din(name, shape):
        return nc.dram_tensor(name, list(shape), F32, kind="ExternalInput").ap()

    x = din("x", [NT, D])
    w_in = din("w_in", [D, 1440])
    w_q_b = din("w_q_b", [384, 768])
    w_kv_b = din("w_kv_b", [256, 1024])
    w_o = din("w_o", [D, D])
    w_gate = din("w_gate", [D, DFF])
    w_up = din("w_up", [D, DFF])
    w_down = din("w_down", [DFF, D])
    gcols_d = din("gcols", [128, 29])
    grow_d = din("grow", [1, 320])
    rope_d = din("rope", [S, 96])
    y = nc.dram_tensor("y", [NT, D], F32, kind="ExternalOutput").ap()
    dumps = {}
    GC_N1, GC_QA, GC_KVA, GC_OUT, GC_N2 = 0, 8, 11, 13, 21
    G_QA, G_KA, G_QB, G_KB = 0, 96, 192, 256

    with ExitStack() as es:
        tk = TK(nc, es)
        ps = es.enter_context(nc.psum_tensor("ps", [128, 4096], F32))
        uniq = [0]

        def sb(name, shape, dt, stack=es):
            uniq[0] += 1
            return stack.enter_context(nc.sbuf_tensor("%s_%d" % (name, uniq[0]), list(shape), dt))

        def bank(b, lo=0, hi=512):
            return ps[:, b * 512 + lo: b * 512 + hi]

        def bankbf(b, n):
            return ps[:, b * 512: b * 512 + n // 2].bitcast(BF16)

        def mm(dst, lhsT, rhs, start, stop):
            return lambda h: h.matmul(dst, lhsT, rhs, start=start, stop=stop)

        def tr(dst, src):
            return lambda h: h.transpose(dst, src, ident[:])

        ident = sb("ident", [128, 128], BF16)
        ones_c = sb("ones_c", [128, 1], BF16)
        eps_t = sb("eps_t", [128, 1], F32)
        gcols = sb("gcols_t", [128, 29], F32)
        rstd2 = sb("rstd2", [128, NT // 128], F32)
        r2t = sb("r2t", [128, 4], F32)
        es1 = es.enter_context(ExitStack())
        grow = sb("grow_t", [128, 320], F32, es1)
        ropeT = [sb("rope_t%d" % i, [128, 96], F32, es1) for i in range(2)]
        tk.op("pool", lambda h: h.memset(ident[:], 0.0), w=["ident"])
        tk.op("pool", lambda h: h.affine_select(out=ident[:], in_=ident[:], compare_op=ALU.not_equal, fill=1.0,
                                                base=0, pattern=[[-1, 128]], channel_multiplier=1),
              r=["ident"], w=["ident"])
        tk.op("pool", lambda h: h.memset(ones_c[:], 1.0), w=["ones_c"])
        tk.op("pool", lambda h: h.memset(eps_t[:], EPS), w=["eps_t"])
        tk.dma("sp", gcols[:], gcols_d[:, :], "c0", w=["gcols"])
        tk.dma("sp", grow[:], grow_d.partition_broadcast(128), "c1", w=["grow"])

        w_in_bf = sb("w_in_bf", [128, KC, 1440], BF16, es1)
        wqb_bf = sb("wqb_bf", [128, 3, 768], BF16, es1)
        wkvb_bf = sb("wkvb_bf", [128, 2, 1024], BF16, es1)
        wo_bf = sb("wo_bf", [128, KC, 1024], BF16, es1)
        WIN = [("w_in", kc, hh) for kc in range(KC) for hh in range(2)]

        KaT = sb("KaT", [128, 8, S], BF16, es1)
        VA = sb("VA", [128, NTT, 4, 192], BF16, es1)
        KbT = sb("KbT", [128, 2, S], BF16, es1)
        VB = sb("VB", [128, NTT, 2, 192], BF16, es1)
        NXS = 2
        xsl = [sb("xsl%d" % i, [128, D], F32, es1) for i in range(NXS)]
        xres = xsl
        xs_bf = [sb("xs%d" % i, [128, D], BF16, es1) for i in range(NXS)]
        junk = sb("junk", [128, 512], BF16, es1)
        hT1 = [sb("hT%d" % i, [128, KC, 128], BF16, es1) for i in range(NXS)]
        fst = [sb("fst%d" % i, [128, 4], F32, es1) for i in range(NXS)]
        for t_ in (VA, VB):
            tk.op("pool", lambda h, t_=t_: h.memset(t_[:, :, :, 64:128], 1.0), w=["Vones"])
        stg_n = [0]

        def stage_cast(dst, src, ncol, wkey):
            n = stg_n[0]
            stg_n[0] += 1
            sl = n % 2
            tk.dma("sp", xsl[sl][:, 0:ncol], src, "x%d" % sl, w=[("x", sl)])
            if n % 2 == 0:
                tk.op("dve", lambda h: h.tensor_copy(out=dst, in_=xsl[sl][:, 0:ncol]), r=[("x", sl)], w=[wkey])
            else:
                tk.op("act", lambda h: h.copy(out=dst, in_=xsl[sl][:, 0:ncol]), r=[("x", sl)], w=[wkey])

        for kc in range(KC):
            for hh in range(2):
                stage_cast(w_in_bf[:, kc, hh * 720:(hh + 1) * 720], w_in[kc * 128:(kc + 1) * 128, hh * 720:(hh + 1) * 720],
                           720, ("w_in", kc, hh))
        for kc in range(2):
            stage_cast(wkvb_bf[:, kc, :], w_kv_b[kc * 128:(kc + 1) * 128, :], 1024, ("wkvb", kc))
        for kc in range(3):
            stage_cast(wqb_bf[:, kc, :], w_q_b[kc * 128:(kc + 1) * 128, :], 768, ("wqb", kc))
        WKVB = [("wkvb", kc) for kc in range(2)]
        WQB = [("wqb", kc) for kc in range(3)]
        for kc in range(KC):
            sl = kc % 2
            tk.dma("sp", xres[sl][:], w_o[kc * 128:(kc + 1) * 128, :], "x%d" % sl, w=[("x", sl)])
            if kc % 2 == 0:
                tk.op("dve", lambda h, kc=kc, sl=sl: h.tensor_scalar(
                    out=wo_bf[:, kc, :], in0=xres[sl][:], scalar1=gcols[:, GC_OUT + kc:GC_OUT + kc + 1], scalar2=None,
                    op0=ALU.mult), r=[("x", sl), "gcols"], w=[("wo", kc)])
            else:
                tk.op("act", lambda h, kc=kc, sl=sl: h.activation(
                    out=wo_bf[:, kc, :], in_=xres[sl][:], func=AF.Identity, scale=gcols[:, GC_OUT + kc:GC_OUT + kc + 1]),
                    r=[("x", sl), "gcols"], w=[("wo", kc)])
        WO = [("wo", kc) for kc in range(KC)]

        def rsq(src, dst, tmp, scale, bias_ap, rkeys, wkey, tmpkey):
            tk.op("act", lambda h: h.activation(out=tmp, in_=src, func=AF.Ln, scale=scale, bias=bias_ap),
                  r=rkeys, w=[tmpkey])
            yield
            tk.op("act", lambda h: h.activation(out=dst, in_=tmp, func=AF.Exp, scale=-0.5),
                  r=[tmpkey], w=[wkey])
            yield

        def fe(row0, xi, bk, gc0, evac_eng):
            xt, xb, st, hTt = xsl[xi], xs_bf[xi], fst[xi], hT1[xi]
            tk.dma("sp", xt[:], x[row0:row0 + 128, :], "x%d" % xi, w=[("x", xi)])
            yield
            tk.op("act", lambda h: h.activation(out=xb[:], in_=xt[:], func=AF.Square, accum_out=st[:, 0:1]),
                  r=[("x", xi)], w=[("xs", xi), ("fst", xi, 0)])
            yield
            yield from rsq(st[:, 0:1], st[:, 2:3], st[:, 1:2], 1.0 / D, eps_t[:], [("fst", xi, 0), "eps_t"],
                           ("fst", xi, 2), ("fst", xi, 1))
            tk.op("dve", lambda h: h.tensor_scalar(out=xb[:], in0=xt[:], scalar1=st[:, 2:3], scalar2=None, op0=ALU.mult),
                  r=[("x", xi), ("fst", xi, 2)], w=[("xs", xi)])
            yield
            tp = bankbf(bk, 1024)
            tk.group("pe", [tr(tp[:, kc * 128:(kc + 1) * 128], xb[:, kc * 128:(kc + 1) * 128]) for kc in range(KC)],
                     r=[("xs", xi), "ident"], w=[("ps", bk)])
            yield
            tk.op(evac_eng, lambda h: h.tensor_tensor(out=hTt[:], in0=tp.rearrange("p (k t) -> p k t", k=KC),
                                                      in1=_ap(gcols[:, gc0:gc0 + KC], [[1, KC], [0, 128]]), op=ALU.mult),
                  r=[("ps", bk), "gcols"], w=[("hT", xi)])
            yield

        def rope_apply(x1, x2, o1, o2, cos, sin, tmpa, tmpb, rk, wk, tmpk, ropek):
            tk.op("dve", lambda h: h.tensor_tensor(out=tmpa, in0=x1, in1=cos, op=ALU.mult), r=rk + [ropek], w=[tmpk + "a"])
            yield
            tk.op("dve", lambda h: h.tensor_tensor(out=tmpb, in0=x2, in1=sin, op=ALU.mult), r=rk + [ropek], w=[tmpk + "b"])
            yield
            tk.op("dve", lambda h: h.tensor_tensor(out=o1, in0=tmpa, in1=tmpb, op=ALU.subtract),
                  r=[tmpk + "a", tmpk + "b"], w=wk)
            yield
            tk.op("dve", lambda h: h.tensor_tensor(out=tmpa, in0=x2, in1=cos, op=ALU.mult), r=rk + [ropek], w=[tmpk + "a"])
            yield
            tk.op("dve", lambda h: h.tensor_tensor(out=tmpb, in0=x1, in1=sin, op=ALU.mult), r=rk + [ropek], w=[tmpk + "b"])
            yield
            tk.op("dve", lambda h: h.tensor_tensor(out=o2, in0=tmpa, in1=tmpb, op=ALU.add),
                  r=[tmpk + "a", tmpk + "b"], w=wk)
            yield

        def rope_load(t, ri):
            tk.dma("sp", ropeT[ri][:], rope_d[t * 128:(t + 1) * 128, :], "rope%d" % ri, w=[("rope", ri)])

        def rope_views(ri, H, gqa):
            n = 16 if gqa else 8
            c0 = 0 if gqa else 32
            dims = [[0, H], [n, 2], [1, n]]
            return _ap(ropeT[ri][:, c0:c0 + 2 * n], dims), _ap(ropeT[ri][:, 48 + c0:48 + c0 + 2 * n], dims)

        def chainA(s, t, sl, A):
            b = 4 * sl
            k = lambda n: (n, "A", sl)
            ast, sqs, kn, ka_tm = A["ast"], A["sqs"], A["kn"], A["ka_tm"]
            rope_load(t, sl)
            yield from fe(s * S + t * 128, sl, b, GC_N1, "act_tt")
            hTt, hk = hT1[sl], ("hT", sl)
            tk.group("pe", [mm(bank(b + 1, 0, 288), hTt[:, kc, :], w_in_bf[:, kc, 384:672], kc == 0, kc == KC - 1)
                            for kc in range(KC)], r=[hk] + WIN, w=[("ps", b + 1)])
            tk.group("pe", [mm(bank(b, 0, 256), hTt[:, kc, :], w_in_bf[:, kc, 1184:1440], kc == 0, kc == KC - 1)
                            for kc in range(KC)], r=[hk] + WIN, w=[("ps", b)])
            yield
            tk.op("act", lambda h: h.activation(out=junk[:, 0:256], in_=bank(b + 1, 0, 256), func=AF.Square,
                                                accum_out=ast[:, 0:1]), r=[("ps", b + 1)], w=["junk", k("a0")])
            yield
            tk.op("act", lambda h: h.activation(out=junk[:, 0:32], in_=bank(b + 1, 256, 288), func=AF.Square,
                                                accum_out=ast[:, 1:2]), r=[("ps", b + 1)], w=["junk", k("a1")])
            yield
            tk.op("dve", lambda h: h.tensor_copy(out=A["ckv_bf"][:], in_=bank(b + 1, 0, 256)), r=[("ps", b + 1)], w=[k("ckv_bf")])
            yield
            tk.op("dve", lambda h: h.tensor_tensor(out=A["kpe_g"][:], in0=bank(b + 1, 256, 288),
                                                   in1=grow[:, G_KA + 64:G_KA + 96], op=ALU.mult),
                  r=[("ps", b + 1), "grow"], w=[k("kpe_g")])
            yield
            tp3 = bankbf(b + 1, 256)
            tk.group("pe", [tr(tp3[:, kc * 128:(kc + 1) * 128], A["ckv_bf"][:, kc * 128:(kc + 1) * 128]) for kc in range(2)],
                     r=[k("ckv_bf"), "ident"], w=[("ps", b + 1)])
            yield
            tk.op("dve", lambda h: h.tensor_tensor(out=A["ckvT"][:], in0=tp3.rearrange("p (k t) -> p k t", k=2),
                                                   in1=_ap(gcols[:, GC_KVA:GC_KVA + 2], [[1, 2], [0, 128]]), op=ALU.mult),
                  r=[("ps", b + 1), "gcols"], w=[k("ckvT")])
            yield
            for half in range(2):
                tk.group("pe", [mm(bank(b + 2 + half), A["ckvT"][:, kc, :], wkvb_bf[:, kc, half * 512:(half + 1) * 512],
                                   kc == 0, kc == 1) for kc in range(2)], r=[k("ckvT")] + WKVB, w=[("ps", b + 2 + half)])
            yield
            kv = ps[:, (b + 2) * 512:(b + 4) * 512].rearrange("p (h d) -> p h d", h=8)
            KV = [("ps", b + 2), ("ps", b + 3)]
            yield from rsq(ast[:, 0:1], ast[:, 3:4], ast[:, 2:3], 1.0 / 256, eps_t[:], [k("a0"), "eps_t"], k("a3"), k("a2"))
            tk.op("act", lambda h: h.activation(out=sqs[:], in_=kv[:, :, 0:64], func=AF.Square), r=KV, w=[k("sqs")])
            yield
            tk.op("dve", lambda h: h.tensor_scalar(out=_ap(VA[:, t, 0, 0:64], [[192, 4], [128, 2], [1, 64]]),
                                                   in0=_ap(kv[:, 0, 64:128], [[256, 4], [128, 2], [1, 64]]),
                                                   scalar1=ast[:, 3:4], scalar2=None, op0=ALU.mult),
                  r=KV + [k("a3")], w=[("VA", t)])
            yield
            tk.op("dve", lambda h: h.tensor_reduce(out=ast[:, 8:16], in_=sqs[:], axis=AX.X, op=ALU.add), r=[k("sqs")], w=[k("a8")])
            yield
            tk.op("dve", lambda h: h.tensor_tensor(out=ast[:, 4:5], in0=ast[:, 3:4], in1=ast[:, 3:4], op=ALU.mult),
                  r=[k("a3")], w=[k("a4")])
            yield
            tk.op("dve", lambda h: h.tensor_scalar(out=ast[:, 8:16], in0=ast[:, 8:16], scalar1=ast[:, 4:5],
                                                   scalar2=ast[:, 1:2], op0=ALU.mult, op1=ALU.add),
                  r=[k("a8"), k("a4"), k("a1")], w=[k("a8")])
            yield
            yield from rsq(ast[:, 8:16], ast[:, 24:32], ast[:, 16:24], 1.0 / 96, eps_t[:], [k("a8"), "eps_t"], k("a24"), k("a16"))
            tk.op("dve", lambda h: h.tensor_scalar(out=ast[:, 32:40], in0=ast[:, 24:32], scalar1=ast[:, 3:4],
                                                   scalar2=None, op0=ALU.mult), r=[k("a24"), k("a3")], w=[k("a32")])
            yield
            tk.op("dve", lambda h: h.tensor_tensor(out=kn[:], in0=kv[:, :, 0:64],
                                                   in1=_ap(ast[:, 32:40], [[1, 8], [0, 64]]), op=ALU.mult),
                  r=KV + [k("a32")], w=[k("kn")])
            yield
            tk.op("dve", lambda h: h.tensor_tensor(out=ka_tm[:, :, 0:64], in0=kn[:],
                                                   in1=_ap(grow[:, G_KA:G_KA + 64], [[0, 8], [1, 64]]), op=ALU.mult),
                  r=[k("kn"), "grow"], w=[k("ka_n")])
            yield
            cv, sv = rope_views(sl, 1, False)
            kx = A["kpe_g"][:].rearrange("p (o r f n) -> p o r f n", o=1, r=2, f=2)
            ko = A["kpe_r"][:].rearrange("p (o r f n) -> p o r f n", o=1, r=2, f=2)
            ra = A["rta"][:, 0:16].rearrange("p (o r n) -> p o r n", o=1, r=2)
            rb = A["rtb"][:, 0:16].rearrange("p (o r n) -> p o r n", o=1, r=2)
            yield from rope_apply(kx[:, :, :, 0, :], kx[:, :, :, 1, :], ko[:, :, :, 0, :], ko[:, :, :, 1, :], cv, sv, ra, rb,
                                  [k("kpe_g")], [k("kpe_r")], "rtA%d" % sl, ("rope", sl))
            tk.op("dve", lambda h: h.tensor_tensor(out=ka_tm[:, :, 64:96], in0=_ap(A["kpe_r"][:], [[0, 8], [1, 32]]),
                                                   in1=_ap(ast[:, 24:32], [[1, 8], [0, 32]]), op=ALU.mult),
                  r=[k("kpe_r"), k("a24")], w=[k("ka_r")])
            yield
            gk = bank(b, 0, 128).rearrange("p (g d) -> p g d", g=2)
            tk.op("dve", lambda h: h.tensor_copy(out=_ap(VB[:, t, 0, 0:64], [[192, 2], [128, 2], [1, 64]]),
                                                 in_=_ap(bank(b, 128, 256), [[64, 2], [0, 2], [1, 64]])),
                  r=[("ps", b)], w=[("VB", t)])
            yield
            tk.op("act", lambda h: h.activation(out=sqs[:, 0:2, :], in_=gk, func=AF.Square), r=[("ps", b), k("a8")], w=[k("sqs")])
            yield
            tk.op("dve", lambda h: h.tensor_reduce(out=ast[:, 40:42], in_=sqs[:, 0:2, :], axis=AX.X, op=ALU.add),
                  r=[k("sqs")], w=[k("a40")])
            yield
            yield from rsq(ast[:, 40:42], ast[:, 44:46], ast[:, 42:44], 1.0 / 64, eps_t[:], [k("a40"), "eps_t"], k("a44"), k("a42"))
            gkn, gkr = A["gkn"], A["gkr"]
            tk.op("dve", lambda h: h.tensor_tensor(out=gkn[:], in0=gk, in1=_ap(ast[:, 44:46], [[1, 2], [0, 64]]),
                                                   op=ALU.mult), r=[("ps", b), k("a44")], w=[k("gkn")])
            yield
            tp6 = bankbf(b, 1024)
            tk.group("pe", [tr(tp6[0:96, hh * 128:(hh + 1) * 128], ka_tm[:, hh, :]) for hh in range(8)],
                     r=[k("ka_n"), k("ka_r"), "ident"], w=[("ps", b)])
            yield
            tk.op("dve", lambda h: h.tensor_tensor(out=gkn[:], in0=gkn[:],
                                                   in1=_ap(grow[:, G_KB:G_KB + 64], [[0, 2], [1, 64]]), op=ALU.mult),
                  r=[k("gkn"), "grow"], w=[k("gkn")])
            yield
            tk.op("act", lambda h: h.copy(out=KaT[0:96, :, t * 128:(t + 1) * 128],
                                          in_=tp6[0:96, :].rearrange("p (k t) -> p k t", k=8)),
                  r=[("ps", b)], w=[("KaT", t)])
            yield
            cv, sv = rope_views(sl, 2, True)
            gx = gkn[:].rearrange("p g (r f n) -> p g r f n", r=2, f=2)
            go = gkr[:].rearrange("p g (r f n) -> p g r f n", r=2, f=2)
            ra = A["rta"][:, 0:64].rearrange("p (g r n) -> p g r n", g=2, r=2)
            rb = A["rtb"][:, 0:64].rearrange("p (g r n) -> p g r n", g=2, r=2)
            yield from rope_apply(gx[:, :, :, 0, :], gx[:, :, :, 1, :], go[:, :, :, 0, :], go[:, :, :, 1, :], cv, sv, ra, rb,
                                  [k("gkn")], [k("gkr")], "rtA%d" % sl, ("rope", sl))
            tk.op("dve", lambda h: h.tensor_copy(out=A["kb_tm"][:], in_=_ap(gkr[:], [[64, 2], [0, 2], [1, 64]])),
                  r=[k("gkr")], w=[k("kb_tm")])
            yield
            tp3b = bankbf(b + 1, 256)
            tk.group("pe", [tr(tp3b[:, g * 128:(g + 1) * 128], A["kb_tm"][:, g, :, :].rearrange("p a d -> p (a d)"))
                            for g in range(2)], r=[k("kb_tm"), "ident"], w=[("ps", b + 1)])
            yield
            tk.op("act", lambda h: h.copy(out=KbT[:, :, t * 128:(t + 1) * 128], in_=tp3b.rearrange("p (g t) -> p g t", g=2)),
                  r=[("ps", b + 1)], w=[("KbT", t)])
            yield

        def prep(s, c, i, qb, B):
            t = c * 4 + i
            xi, pb = 0, 7
            rope_load(t, 0)
            k = lambda n: (n, "B")
            bst, sqq, qn, qa_tm, qb_tm = B["bst"], B["sqq"], B["qn"], B["qa_tm"], B["qb_tm"]
            yield from fe(s * S + c * 512 + i * 128, xi, pb, GC_N1, "dve")
            hTt, hk = hT1[xi], ("hT", xi)
            tk.group("pe", [mm(bank(pb, 0, 384), hTt[:, kc, :], w_in_bf[:, kc, 0:384], kc == 0, kc == KC - 1)
                            for kc in range(KC)], r=[hk] + WIN, w=[("ps", pb)])
            yield
            c32 = qn[:].rearrange("p h d -> p (h d)")
            tk.op("dve", lambda h: h.tensor_copy(out=c32[:, 0:384], in_=bank(pb, 0, 384)), r=[("ps", pb)], w=[k("c32")])
            yield
            tk.group("pe", [mm(bank(pb), hTt[:, kc, :], w_in_bf[:, kc, 672:1184], kc == 0, kc == KC - 1)
                            for kc in range(KC)], r=[hk] + WIN, w=[("ps", pb)])
            yield
            tk.op("dve", lambda h: h.tensor_copy(out=B["cq_bf"][:], in_=c32[:, 0:384]), r=[k("c32")], w=[k("cq_bf")])
            yield
            tk.op("dve", lambda h: h.tensor_tensor(out=sqq[:].rearrange("p h d -> p (h d)")[:, 0:384], in0=c32[:, 0:384],
                                                   in1=c32[:, 0:384], op=ALU.mult), r=[k("c32")], w=[k("sqq")])
            yield
            tk.op("dve", lambda h: h.tensor_reduce(out=bst[:, 0:1], in_=sqq[:].rearrange("p h d -> p (h d)")[:, 0:384],
                                                   axis=AX.X, op=ALU.add), r=[k("sqq")], w=[k("b0")])
            yield
            gq32 = B["gq32"]
            tk.op("dve", lambda h: h.tensor_copy(out=gq32[:], in_=bank(pb)), r=[("ps", pb), k("c32")], w=[k("gq32")])
            yield
            tp3 = bankbf(pb, 384)
            tk.group("pe", [tr(tp3[:, kc * 128:(kc + 1) * 128], B["cq_bf"][:, kc * 128:(kc + 1) * 128]) for kc in range(3)],
                     r=[k("cq_bf"), "ident"], w=[("ps", pb)])
            yield
            tk.op("dve", lambda h: h.tensor_tensor(out=B["cqT"][:], in0=tp3.rearrange("p (k t) -> p k t", k=3),
                                                   in1=_ap(gcols[:, GC_QA:GC_QA + 3], [[1, 3], [0, 128]]), op=ALU.mult),
                  r=[("ps", pb), "gcols"], w=[k("cqT")])
            yield
            for half in range(2):
                tk.group("pe", [mm(bank(pb, 0, 384), B["cqT"][:, kc, :], wqb_bf[:, kc, half * 384:(half + 1) * 384],
                                   kc == 0, kc == 2) for kc in range(3)], r=[k("cqT")] + WQB, w=[("ps", pb)])
                yield
                tk.op("dve", lambda h, half=half: h.tensor_copy(
                    out=qn[:, 4 * half:4 * half + 4, :], in_=bank(pb, 0, 384).rearrange("p (h d) -> p h d", h=4)),
                    r=[("ps", pb)], w=[k("qn%d" % half)])
                yield
            QN = [k("qn0"), k("qn1")]
            tk.op("dve", lambda h: h.tensor_scalar(out=bst[:, 1:2], in0=bst[:, 0:1], scalar1=EPS / 384.0,
                                                   scalar2=EPS * EPS, op0=ALU.mult, op1=ALU.add), r=[k("b0")], w=[k("b1")])
            yield
            tk.op("dve", lambda h: h.tensor_tensor(out=sqq[:], in0=qn[:], in1=qn[:], op=ALU.mult), r=QN, w=[k("sqq")])
            yield
            tk.op("dve", lambda h: h.tensor_reduce(out=bst[:, 8:16], in_=sqq[:], axis=AX.X, op=ALU.add), r=[k("sqq")], w=[k("b8")])
            yield
            yield from rsq(bst[:, 8:16], bst[:, 24:32], bst[:, 16:24], 1.0 / 96, bst[:, 1:2], [k("b8"), k("b1")], k("b24"), k("b16"))
            tk.op("dve", lambda h: h.tensor_tensor(out=qn[:], in0=qn[:], in1=_ap(bst[:, 24:32], [[1, 8], [0, 96]]),
                                                   op=ALU.mult), r=QN + [k("b24")], w=[k("qn0"), k("qn1")])
            yield
            tk.op("dve", lambda h: h.tensor_tensor(out=qa_tm[:, :, 0:64], in0=qn[:, :, 0:64],
                                                   in1=_ap(grow[:, G_QA:G_QA + 64], [[0, 8], [1, 64]]), op=ALU.mult),
                  r=QN + ["grow"], w=[k("qa_n")])
            yield
            tk.op("dve", lambda h: h.tensor_tensor(out=qn[:, :, 64:96], in0=qn[:, :, 64:96],
                                                   in1=_ap(grow[:, G_QA + 64:G_QA + 96], [[0, 8], [1, 32]]), op=ALU.mult),
                  r=QN + ["grow"], w=[k("qn0"), k("qn1")])
            yield
            cv, sv = rope_views(0, 8, False)
            qx = qn[:, :, 64:96].rearrange("p h (r f n) -> p h r f n", r=2, f=2)
            qo = qa_tm[:, :, 64:96].rearrange("p h (r f n) -> p h r f n", r=2, f=2)
            ra = B["rta"][:, 0:128].rearrange("p (h r n) -> p h r n", h=8, r=2)
            rb = B["rtb"][:, 0:128].rearrange("p (h r n) -> p h r n", h=8, r=2)
            yield from rope_apply(qx[:, :, :, 0, :], qx[:, :, :, 1, :], qo[:, :, :, 0, :], qo[:, :, :, 1, :], cv, sv, ra, rb,
                                  QN, [k("qa_r")], "rtB", ("rope", 0))
            tp6 = bankbf(pb, 1024)
            tk.group("pe", [tr(tp6[0:96, hh * 128:(hh + 1) * 128], qa_tm[:, hh, :]) for hh in range(8)],
                     r=[k("qa_n"), k("qa_r"), "ident"], w=[("ps", pb)])
            yield
            tk.op("dve", lambda h: h.tensor_copy(out=QaT[qb][0:96, :, i * 128:(i + 1) * 128],
                                                 in_=tp6[0:96, :].rearrange("p (k t) -> p k t", k=8)),
                  r=[("ps", pb)], w=[("QaT", qb, i)])
            yield
            g3 = gq32[:].rearrange("p (h d) -> p h d", h=8)
            sqb = sqq[:].rearrange("p h d -> p (h d)")[:, 0:512].rearrange("p (h d) -> p h d", h=8)
            tk.op("dve", lambda h: h.tensor_tensor(out=sqb, in0=g3, in1=g3, op=ALU.mult), r=[k("gq32")], w=[k("sqq")])
            yield
            tk.op("dve", lambda h: h.tensor_reduce(out=bst[:, 32:40], in_=sqb, axis=AX.X, op=ALU.add), r=[k("sqq")], w=[k("b32")])
            yield
            yield from rsq(bst[:, 32:40], bst[:, 48:56], bst[:, 40:48], 1.0 / 64, eps_t[:], [k("b32"), "eps_t"], k("b48"), k("b40"))
            tk.op("dve", lambda h: h.tensor_tensor(out=g3, in0=g3, in1=_ap(bst[:, 48:56], [[1, 8], [0, 64]]), op=ALU.mult),
                  r=[k("gq32"), k("b48")], w=[k("gq32")])
            yield
            tk.op("dve", lambda h: h.tensor_tensor(out=g3, in0=g3, in1=_ap(grow[:, G_QB:G_QB + 64], [[0, 8], [1, 64]]),
                                                   op=ALU.mult), r=[k("gq32"), "grow"], w=[k("gq32")])
            yield
            cv, sv = rope_views(0, 8, True)
            bx = g3.rearrange("p h (r f n) -> p h r f n", r=2, f=2)
            bo = qb_tm[:].rearrange("p h (r f n) -> p h r f n", r=2, f=2)
            ra = B["rta"][:, 0:256].rearrange("p (h r n) -> p h r n", h=8, r=2)
            rb = B["rtb"][:, 0:256].rearrange("p (h r n) -> p h r n", h=8, r=2)
            yield from rope_apply(bx[:, :, :, 0, :], bx[:, :, :, 1, :], bo[:, :, :, 0, :], bo[:, :, :, 1, :], cv, sv, ra, rb,
                                  [k("gq32")], [k("qb_tm")], "rtB", ("rope", 0))
            tp7 = bankbf(pb, 512)
            qbf = qb_tm[:].rearrange("p h d -> p (h d)")
            tk.group("pe", [tr(tp7[:, j * 128:(j + 1) * 128], qbf[:, j * 128:(j + 1) * 128]) for j in range(4)],
                     r=[k("qb_tm"), "ident"], w=[("ps", pb)])
            yield
            tk.op("dve", lambda h: h.tensor_copy(out=QbT[qb][:, :, i * 128:(i + 1) * 128],
                                                 in_=tp7.rearrange("p (k t) -> p k t", k=4)),
                  r=[("ps", pb)], w=[("QbT", qb, i)])
            yield

        def prep_chunk(s, c, qb, B):
            for i in range(4):
                yield from prep(s, c, i, qb, B)

        with ExitStack() as esb:
            B = dict(
                cq_bf=sb("cq_bf", [128, 384], BF16, esb), cqT=sb("cqT", [128, 3, 128], BF16, esb),
                bst=sb("bst", [128, 64], F32, esb), sqq=sb("sqq", [128, 8, 96], F32, esb),
                qn=sb("qn", [128, 8, 96], F32, esb), qa_tm=sb("qa_tm", [128, 8, 96], BF16, esb),
                qb_tm=sb("qb_tm", [128, 8, 64], BF16, esb), rta=sb("rtaB", [128, 256], F32, esb),
                rtb=sb("rtbB", [128, 256], F32, esb),
                gq32=sb("gq32", [128, 512], F32, esb))
            QaT = [sb("QaT%d" % i, [128, 8, 512], BF16, esb) for i in range(2)]
            QbT = [sb("QbT%d" % i, [128, 4, 512], BF16, esb) for i in range(2)]
            NP = 3
            shared = sb("shared", [128, 9216], BF16, esb)

            def carve(off, shape, dt):
                nel = int(np.prod(shape))
                size = nel * (2 if dt == BF16 else 4)
                assert off % 64 == 0 and off + size <= 18432
                a = shared[:, off // 2:(off + size) // 2]
                if dt == F32:
                    a = a.bitcast(F32)
                if len(shape) > 1:
                    names = ["a", "b", "c", "d"][:len(shape)]
                    a = a.rearrange("p (%s) -> p %s" % (" ".join(names), " ".join(names)),
                                    **{n: int(v) for n, v in zip(names, shape)})
                return a

            PT = [carve(i * 2048, [1024], BF16) for i in range(NP)]
            rec = carve(6144, [512], F32)
            onT = carve(8192, [KC, 512], BF16)
            sq_on = [carve(16384 + i * 1024, [512], BF16) for i in range(2)]
            ost = sb("ost", [128, 32], F32, esb)
            chunks = [(s, c) for s in range(NSEQ) for c in range(NCH)]
            pcount = [0]

            def attention(s, c, qb, hook):
                qkeys_a = [("QaT", qb, i) for i in range(4)]
                qkeys_b = [("QbT", qb, i) for i in range(4)]
                units = [(hd, p) for hd in range(16) for p in range(NTT // 2)]

                def s_mm(hd, p, sp_i):
                    fns, rk = [], []
                    for j in range(2):
                        kb = 2 * p + j
                        if hd < 8:
                            lhsT = KaT[0:96, hd, kb * 128:(kb + 1) * 128]
                            rhs = QaT[qb][0:96, hd, :]
                            rk.append(("KaT", kb))
                        else:
                            jq = hd - 8
                            g, hf = jq // 4, jq % 2
                            lhsT = KbT[hf * 64:(hf + 1) * 64, g, kb * 128:(kb + 1) * 128]
                            rhs = QbT[qb][hf * 64:(hf + 1) * 64, jq // 2, :]
                            rk.append(("KbT", kb))
                        fns.append(mm(bank(2 * sp_i + j), lhsT, rhs, True, True))
                    rk += (qkeys_a if hd < 8 else qkeys_b)
                    tk.group("pe", fns, r=rk, w=[("ps", 2 * sp_i), ("ps", 2 * sp_i + 1)])

                def exp_pv(hd, p, sp_i):
                    slot = pcount[0] % NP
                    pcount[0] += 1
                    scale = (96.0 if hd < 8 else 64.0) ** -0.5
                    src = ps[:, sp_i * 1024:(sp_i + 1) * 1024]
                    tk.op("act", lambda h: h.activation(out=PT[slot][:], in_=src, func=AF.Exp, scale=scale),
                          r=[("ps", 2 * sp_i), ("ps", 2 * sp_i + 1)], w=[("PT", slot)])
                    ob = 4 + hd % 2
                    fns, rk = [], [("PT", slot), "Vones"]
                    for j in range(2):
                        kb = 2 * p + j
                        if hd < 8:
                            lo = 0 if hd % 2 == 0 else 64
                            lhsT = VA[:, kb, hd // 2, lo:lo + 128]
                            rk.append(("VA", kb))
                        else:
                            jq = hd - 8
                            lo = 0 if jq % 2 == 0 else 64
                            lhsT = VB[:, kb, jq // 4, lo:lo + 128]
                            rk.append(("VB", kb))
                        fns.append(mm(bank(ob), lhsT, PT[slot][:, j * 512:(j + 1) * 512], kb == 0, kb == NTT - 1))
                    tk.group("pe", fns, r=rk, w=[("ps", ob)])
                    if p == NTT // 2 - 1:
                        head_done(hd, ob)

                def head_done(hd, ob):
                    cidx = hd // 2
                    if hd % 2 == 0:
                        num, den, o_lo, rc = bank(ob)[0:64, :], bank(ob)[64:128, :], 0, rec[64:128, :]
                    else:
                        num, den, o_lo, rc = bank(ob)[64:128, :], bank(ob)[0:64, :], 64, rec[0:64, :]
                    tk.op("dve", lambda h: h.reciprocal(out=rc, in_=den), r=[("ps", ob)], w=["rec"])
                    tk.op("dve", lambda h: h.tensor_tensor(out=onT[o_lo:o_lo + 64, cidx, :], in0=num, in1=rc, op=ALU.mult),
                          r=[("ps", ob), "rec"], w=[("onT", cidx, hd % 2)])
                    if hd % 2 == 1:
                        sq = sq_on[cidx % 2]
                        tk.op("dve", lambda h: h.tensor_tensor(out=sq[:], in0=onT[:, cidx, :], in1=onT[:, cidx, :], op=ALU.mult),
                              r=[("onT", cidx, 0), ("onT", cidx, 1)], w=[("sq_on", cidx % 2)])
                        tk.group("pe", [mm(bank(6, i * 8 + cidx, i * 8 + cidx + 1), sq[:, i * 128:(i + 1) * 128], ones_c[:],
                                           True, True) for i in range(4)],
                                 r=[("sq_on", cidx % 2), "ones_c"], w=[("ps", 6)])

                pend = None
                for ui, (hd, p) in enumerate(units):
                    s_mm(hd, p, ui % 2)
                    if pend is not None:
                        exp_pv(*pend)
                    pend = (hd, p, ui % 2)
                    hook()
                exp_pv(*pend)

            def wo_residual(s, c):
                tk.op("dve", lambda h: h.tensor_reduce(out=ost[:, 0:8],
                                                       in_=bank(6, 0, 32).rearrange("p (i g k) -> p i g k", i=4, g=2),
                                                       axis=AX.X, op=ALU.add), r=[("ps", 6)], w=["o0"])
                for _ in rsq(ost[:, 0:8], ost[:, 16:24], ost[:, 8:16], 1.0 / 512, eps_t[:], ["o0", "eps_t"], "o16", "o8"):
                    pass
                for i in range(4):
                    row0 = s * S + c * 512 + i * 128
                    xr = xres[i % 2]
                    tk.dma("sp", xr[:], x[row0:row0 + 128, :], "x%d" % (i % 2), w=[("x", i % 2)])
                    b0 = 0 if i % 2 == 0 else 4
                    for g in range(2):
                        for half in range(2):
                            bk = b0 + g * 2 + half
                            tk.group("pe", [mm(bank(bk), onT[:, g * 4 + kc, i * 128:(i + 1) * 128],
                                               wo_bf[:, g * 4 + kc, half * 512:(half + 1) * 512], kc == 0, kc == 3)
                                            for kc in range(4)],
                                     r=[("onT", g * 4 + kc, q) for kc in range(4) for q in range(2)] + WO, w=[("ps", bk)])
                    for g in range(2):
                        acc = ps[:, (b0 + g * 2) * 512:(b0 + g * 2 + 2) * 512]
                        tk.op("dve", lambda h, acc=acc, g=g: h.scalar_tensor_tensor(
                            out=xr[:], in0=acc, scalar=ost[:, 16 + i * 2 + g:16 + i * 2 + g + 1], in1=xr[:],
                            op0=ALU.mult, op1=ALU.add),
                            r=[("ps", b0 + g * 2), ("ps", b0 + g * 2 + 1), "o16", ("x", i % 2)], w=[("x", i % 2)])
                    tk.dma("sp", y[row0:row0 + 128, :], xr[:], "yst%d" % (i % 2), r=[("x", i % 2)])
                    gt = row0 // 128
                    tk.op("act", lambda h: h.activation(out=junk[:, 0:512], in_=xr[:, 0:512], func=AF.Square,
                                                        accum_out=r2t[:, 0:1]), r=[("x", i % 2)], w=["junk", "r2a"])
                    tk.op("act", lambda h: h.activation(out=junk[:, 0:512], in_=xr[:, 512:1024], func=AF.Square,
                                                        accum_out=r2t[:, 1:2]), r=[("x", i % 2)], w=["junk", "r2b"])
                    tk.op("dve", lambda h: h.tensor_tensor(out=r2t[:, 2:3], in0=r2t[:, 0:1], in1=r2t[:, 1:2], op=ALU.add),
                          r=["r2a", "r2b"], w=["r2c"])
                    for _ in rsq(r2t[:, 2:3], rstd2[:, gt:gt + 1], r2t[:, 3:4], 1.0 / D, eps_t[:], ["r2c", "eps_t"],
                                 ("rstd2", gt), "r2d"):
                        pass

            for s in range(NSEQ):
                if True:
                    AS = []
                    for sl in range(2):
                        o = [sl * 9216]

                        def cv_(shape, dt):
                            nel = int(np.prod(shape)) * (2 if dt == BF16 else 4)
                            a = carve(o[0], shape, dt)
                            o[0] += (nel + 63) // 64 * 64
                            return a
                        AS.append(dict(ckv_bf=cv_([256], BF16), ckvT=cv_([2, 128], BF16), ast=cv_([48], F32),
                                       sqs=cv_([8, 64], F32), kn=cv_([8, 64], F32), ka_tm=cv_([8, 96], BF16),
                                       kpe_g=cv_([32], F32), kpe_r=cv_([32], F32), rta=cv_([64], F32), rtb=cv_([64], F32),
                                       gkn=cv_([2, 64], F32), gkr=cv_([2, 64], F32), kb_tm=cv_([2, 2, 64], BF16)))
                        assert o[0] <= (sl + 1) * 9216

                    def seqA(sl):
                        for t in range(sl, NTT, 2):
                            yield from chainA(s, t, sl, AS[sl])

                    gens = [seqA(0), seqA(1)]
                    drive(gens, 2)
                    tk.barrier()
                if "kv" in dump and s == NSEQ - 1:
                    for nm, t_ in (("KaT", KaT), ("VA", VA), ("KbT", KbT), ("VB", VB)):
                        shp = [128, int(np.prod(t_.shape[1:]))]
                        dumps[nm] = nc.dram_tensor("dump_" + nm, shp, BF16, kind="ExternalOutput").ap()
                        nd = len(t_.shape)
                        src = t_[:].rearrange("p a b -> p (a b)") if nd == 3 else t_[:].rearrange("p a b c -> p (a b c)")
                        tk.dma("sp", dumps[nm][:, :], src, "dump", r=[])
                if s == 0:
                    drive([prep_chunk(0, 0, 0, B)], 1)
                for c in range(NCH):
                    ci = s * NCH + c
                    nxt = chunks[ci + 1] if ci + 1 < len(chunks) else None
                    pg = prep_chunk(nxt[0], nxt[1], (ci + 1) % 2, B) if nxt else iter(())

                    def hook(pg=pg):
                        for _ in range(2):
                            try:
                                next(pg)
                            except StopIteration:
                                return
                    attention(s, c, ci % 2, hook)
                    for _ in pg:
                        pass
                    wo_residual(s, c)
                tk.barrier()
        es1.close()
        tk.barrier()

        if do_ffn:
            with ExitStack() as es2:
                wg_bf = sb("wg_bf", [128, KC, DFF], BF16, es2)
                wu_bf = sb("wu_bf", [128, KC, DFF], BF16, es2)
                wd_bf = sb("wd_bf", [128, FC, D], BF16, es2)
                NX2 = 5
                x2sl = [sb("x2sl%d" % i, [128, D], F32, es2) for i in range(NX2)]
                xs2 = [sb("xs2_%d" % i, [128, D], BF16, es2) for i in range(2)]
                hT2 = [sb("hT2_%d" % i, [128, KC, 512], BF16, es2) for i in range(2)]
                aT = sb("aT", [128, FC, 512], BF16, es2)
                sg = [sb("sg%d" % i, [128, 512], F32, es2) for i in range(2)]
                stg2 = [sb("stg2_%d" % i, [128, 1024], F32, es2) for i in range(2)]
                st2n = [0]

                def stage2(dst, src, wkey, view3):
                    n = st2n[0]
                    st2n[0] += 1
                    sl = n % 2
                    sv = stg2[sl][:].rearrange("p (k n) -> p k n", k=KC) if view3 else stg2[sl][:]
                    tk.dma("sp", sv, src, "stg2_%d" % sl, w=[("stg2", sl)])
                    if n % 2 == 0:
                        tk.op("dve", lambda h: h.tensor_copy(out=dst, in_=sv), r=[("stg2", sl)], w=[wkey])
                    else:
                        tk.op("act", lambda h: h.copy(out=dst, in_=sv), r=[("stg2", sl)], w=[wkey])

                def load_gu(f):
                    stage2(wg_bf[:, :, f * 128:(f + 1) * 128], wg_v[:, :, f * 128:(f + 1) * 128], ("wg", f), True)
                    stage2(wu_bf[:, :, f * 128:(f + 1) * 128], wu_v[:, :, f * 128:(f + 1) * 128], ("wu", f), True)

                def load_d(f):
                    stage2(wd_bf[:, f, :], w_down[f * 128:(f + 1) * 128, :], ("wd", f), False)
                wg_v = w_gate.rearrange("(k p) n -> p k n", p=128)
                wu_v = w_up.rearrange("(k p) n -> p k n", p=128)
                NCH2 = NT // 512
                fe2 = [0]

                def front_end2(c, i):
                    n = fe2[0]
                    fe2[0] += 1
                    sl, b_i = (c * 4 + i) % NX2, n % 2
                    gt = c * 4 + i
                    xt, xb = x2sl[sl], xs2[b_i]
                    tk.dma("sp", xt[:], y[gt * 128:(gt + 1) * 128, :], "x2_%d" % sl, w=[("x2", sl)])
                    tk.op("dve", lambda h: h.tensor_scalar(out=xb[:], in0=xt[:], scalar1=rstd2[:, gt:gt + 1], scalar2=None,
                                                           op0=ALU.mult), r=[("x2", sl), ("rstd2", gt)], w=[("xs2", b_i)])
                    tp = bankbf(7, 1024)
                    tk.group("pe", [tr(tp[:, kc * 128:(kc + 1) * 128], xb[:, kc * 128:(kc + 1) * 128]) for kc in range(KC)],
                             r=[("xs2", b_i), "ident"], w=[("ps", 7)])
                    tk.op("dve", lambda h: h.tensor_tensor(out=hT2[c % 2][:, :, i * 128:(i + 1) * 128],
                                                           in0=tp.rearrange("p (k t) -> p k t", k=KC),
                                                           in1=_ap(gcols[:, GC_N2:GC_N2 + KC], [[1, KC], [0, 128]]), op=ALU.mult),
                          r=[("ps", 7), "gcols"], w=[("hT2", c % 2, i)])

                def down(c, i):
                    sl = (c * 4 + i) % NX2
                    gt = c * 4 + i
                    for half in range(2):
                        bk = 4 + (2 * i + half) % 3
                        tk.group("pe", [mm(bank(bk), aT[:, f, i * 128:(i + 1) * 128], wd_bf[:, f, half * 512:(half + 1) * 512],
                                           f == 0, f == FC - 1) for f in range(FC)],
                                 r=[("aT", f) for f in range(FC)] + [("wd", f) for f in range(FC)], w=[("ps", bk)])
                        tk.op("dve", lambda h, bk=bk, half=half: h.tensor_tensor(
                            out=x2sl[sl][:, half * 512:(half + 1) * 512], in0=bank(bk),
                            in1=x2sl[sl][:, half * 512:(half + 1) * 512], op=ALU.add),
                            r=[("ps", bk), ("x2", sl)], w=[("x2", sl)])
                    tk.dma("sp", y[gt * 128:(gt + 1) * 128, :], x2sl[sl][:], "y2_%d" % sl, r=[("x2", sl)])

                for i in range(4):
                    front_end2(0, i)
                load_gu(0)
                for c in range(NCH2):
                    hk = [("hT2", c % 2, i) for i in range(4)]
                    for f in range(FC):
                        gb, ub = f % 2, 2 + f % 2
                        if c == 0:
                            if f + 1 < FC:
                                load_gu(f + 1)
                            load_d(f)
                        tk.group("pe", [mm(bank(gb), wg_bf[:, kc, f * 128:(f + 1) * 128], hT2[c % 2][:, kc, :],
                                           kc == 0, kc == KC - 1) for kc in range(KC)], r=hk + [("wg", f)], w=[("ps", gb)])
                        tk.group("pe", [mm(bank(ub), wu_bf[:, kc, f * 128:(f + 1) * 128], hT2[c % 2][:, kc, :],
                                           kc == 0, kc == KC - 1) for kc in range(KC)], r=hk + [("wu", f)], w=[("ps", ub)])
                        tk.op("act", lambda h, f=f, gb=gb: h.activation(out=sg[f % 2][:], in_=bank(gb), func=AF.Silu),
                              r=[("ps", gb)], w=[("sg", f % 2)])
                        tk.op("dve", lambda h, f=f, ub=ub: h.tensor_tensor(out=aT[:, f, :], in0=bank(ub), in1=sg[f % 2][:],
                                                                          op=ALU.mult),
                              r=[("ps", ub), ("sg", f % 2)], w=[("aT", f)])
                        if c + 1 < NCH2 and f == 10:
                            front_end2(c + 1, 0)
                    for i in range(4):
                        down(c, i)
                        if c + 1 < NCH2 and i < 3:
                            front_end2(c + 1, i + 1)
        tk.finish("sp")
    return nc, dumps


def _rope_table(S):
    t = np.arange(S)
    row = (t // 64).astype(np.float32)
    col = (t % 64).astype(np.float32)
    out = np.zeros((S, 96), np.float32)
    inv_g = (np.float32(10000.0) ** (-(np.arange(0, 32, 2, dtype=np.float32) / np.float32(32)))).astype(np.float32)
    inv_m = (np.float32(10000.0) ** (-(np.arange(0, 16, 2, dtype=np.float32) / np.float32(16)))).astype(np.float32)
    ang = np.concatenate([row[:, None] * inv_g[None], col[:, None] * inv_g[None],
                          row[:, None] * inv_m[None], col[:, None] * inv_m[None]], axis=1).astype(np.float32)
    out[:, 0:48] = np.cos(ang)
    out[:, 48:96] = np.sin(ang)
    return out


def _host_inputs(inp, S):
    f = lambda a: np.ascontiguousarray(np.asarray(a, dtype=np.float32))
    col = lambda g: f(g).reshape(-1, 128).T
    gcols = np.concatenate([col(inp["norm1_g"][0]), col(inp["q_a_norm_g"][0]), col(inp["kv_a_norm_g"][0]),
                            col(np.concatenate([f(inp["mla_out_norm_g"][0]), f(inp["gqa_out_norm_g"][0])])),
                            col(inp["norm2_g"][0])], axis=1)
    grow = np.concatenate([f(inp["mla_q_norm_g"][0]), f(inp["mla_k_norm_g"][0]),
                           f(inp["gqa_q_norm_g"][0]), f(inp["gqa_k_norm_g"][0])])[None, :]
    shared = {
        "w_in": f(inp["w_in"][0]), "w_q_b": f(inp["w_q_b"][0]), "w_kv_b": f(inp["w_kv_b"][0]), "w_o": f(inp["w_o"][0]),
        "w_gate": f(inp["w_gate"][0]), "w_up": f(inp["w_up"][0]), "w_down": f(inp["w_down"][0]),
        "gcols": f(gcols), "grow": f(grow), "rope": _rope_table(S),
    }
    return shared


def kernel(**inputs):
    x = np.asarray(inputs["x"], dtype=np.float32)
    B, S, _ = x.shape
    nseq = B // N_CORES
    shared = _host_inputs(inputs, S)
    nc, _ = build_nc(nseq, S)
    in_maps = []
    for c in range(N_CORES):
        m = dict(shared)
        m["x"] = np.ascontiguousarray(x[c * nseq:(c + 1) * nseq].reshape(nseq * S, D))
        in_maps.append(m)
    res = run_bass_kernel_spmd(nc, in_maps, core_ids=list(range(N_CORES)))
    out = np.concatenate([np.asarray(r["y"]).reshape(nseq, S, D) for r in res.results], axis=0)
    return out.astype(np.float32)
```

```python
from contextlib import ExitStack

import numpy as np
import concourse.bass as bass
import concourse.mybir as mybir
from concourse.bass_utils import run_bass_kernel_spmd

F32 = mybir.dt.float32
BF16 = mybir.dt.bfloat16
AF = mybir.ActivationFunctionType
ALU = mybir.AluOpType
AX = mybir.AxisListType

D = 1024
KC = 8
DFF = 2816
FC = 22
EPS = 1e-6
N_CORES = 8


class TK:
    def __init__(self, nc, es):
        self.nc = nc
        self.es = es
        self.E = {}
        for name, h in (("pe", nc.tensor), ("act", nc.scalar), ("dve", nc.vector),
                        ("pool", nc.gpsimd), ("sp", nc.sync)):
            self.E[name] = dict(h=h, sem=es.enter_context(nc.semaphore("e_" + name)), cnt=0, waited={})
        self.res = {}
        self.dsem = {}

    @staticmethod
    def _excl(key):
        return isinstance(key, tuple) and key[0] == "ps"

    def _collect(self, eng, reads, writes):
        deps = {}

        def add(st, same_ok):
            if st is None:
                return
            key, sem, val = st
            if key == eng and (eng == "pe" or not same_ok):
                return
            if key not in deps or deps[key][1] < val:
                deps[key] = (sem, val)

        for k in reads:
            d = self.res.get(k)
            if d:
                add(d["w"], True)
        for k in writes:
            d = self.res.get(k)
            if d:
                add(d["w"], False)
                for st in d["r"].values():
                    add(st, False)
        return deps

    def _waits(self, eng, deps):
        E = self.E[eng]
        for key, (sem, val) in deps.items():
            if E["waited"].get(key, 0) >= val:
                continue
            E["waited"][key] = val
            E["h"].wait_ge(sem, val)

    def _record(self, stamp, reads, writes):
        for k in reads:
            d = self.res.setdefault(k, {"w": None, "r": {}})
            d["r"][stamp[0]] = stamp
        for k in writes:
            self.res[k] = {"w": stamp, "r": {}}

    def _split(self, r, w):
        r = list(r)
        w = list(w)
        ex = [k for k in r if self._excl(k)]
        r = [k for k in r if not self._excl(k)]
        for k in ex:
            if k not in w:
                w.append(k)
        return r, w

    def op(self, eng, fn, r=(), w=()):
        if eng == "act_tt":
            eng = "dve"
        r, w = self._split(r, w)
        self._waits(eng, self._collect(eng, r, w))
        E = self.E[eng]
        E["cnt"] += 1
        fn(E["h"]).then_inc(E["sem"], 1)
        self._record((eng, E["sem"], E["cnt"]), r, w)

    def group(self, eng, fns, r=(), w=()):
        r, w = self._split(r, w)
        self._waits(eng, self._collect(eng, r, w))
        E = self.E[eng]
        E["cnt"] += 1
        for i, fn in enumerate(fns):
            ins = fn(E["h"])
            if i == len(fns) - 1:
                ins.then_inc(E["sem"], 1)
        self._record((eng, E["sem"], E["cnt"]), r, w)

    def dma(self, eng, out, in_, semkey, r=(), w=(), **kw):
        r, w = self._split(r, w)
        self._waits(eng, self._collect(eng, r, w))
        if semkey not in self.dsem:
            self.dsem[semkey] = [self.es.enter_context(self.nc.semaphore("d_" + semkey)), 0]
        ds = self.dsem[semkey]
        ds[1] += 16
        self.E[eng]["h"].dma_start(out=out, in_=in_, **kw).then_inc(ds[0], 16)
        self._record(("d:" + semkey, ds[0], ds[1]), r, w)

    def barrier(self):
        for eng, E in self.E.items():
            deps = {}
            for e2, E2 in self.E.items():
                if e2 != eng and E2["cnt"] > 0:
                    deps[e2] = (E2["sem"], E2["cnt"])
            for k, ds in self.dsem.items():
                if ds[1] > 0:
                    deps["d:" + k] = (ds[0], ds[1])
            self._waits(eng, deps)

    def finish(self, eng="sp"):
        deps = {}
        for k, ds in self.dsem.items():
            if ds[1] > 0:
                deps["d:" + k] = (ds[0], ds[1])
        for e2, E2 in self.E.items():
            if e2 != eng and E2["cnt"] > 0:
                deps[e2] = (E2["sem"], E2["cnt"])
        self._waits(eng, deps)


def _ap(base, dims):
    return bass.AP(base.tensor, base.offset, [list(base.ap[0])] + [list(d) for d in dims])


def drive(gens, width):
    pending = list(gens)
    active = []
    while pending or active:
        while pending and len(active) < width:
            active.append(pending.pop(0))
        for g in list(active):
            try:
                next(g)
            except StopIteration:
                active.remove(g)


def build_nc(NSEQ, S, do_ffn=True, dump=()):
    nc = bass.Bass("TRN2", target_bir_lowering=False)
    NT = NSEQ * S
    NTT = S // 128
    NCH = S // 512

    def din(name, shape):
        return nc.dram_tensor(name, list(shape), F32, kind="ExternalInput").ap()

    x = din("x", [NT, D])
    w_in = din("w_in", [D, 1440])
    w_q_b = din("w_q_b", [384, 768])
    w_kv_b = din("w_kv_b", [256, 1024])
    w_o = din("w_o", [D, D])
    w_gate = din("w_gate", [D, DFF])
    w_up = din("w_up", [D, DFF])
    w_down = din("w_down", [DFF, D])
    gcols_d = din("gcols", [128, 29])
    grow_d = din("grow", [1, 320])
    rope_d = din("rope", [S, 96])
    y = nc.dram_tensor("y", [NT, D], F32, kind="ExternalOutput").ap()
    dumps = {}
    GC_N1, GC_QA, GC_KVA, GC_OUT, GC_N2 = 0, 8, 11, 13, 21
    G_QA, G_KA, G_QB, G_KB = 0, 96, 192, 256

    with ExitStack() as es:
        tk = TK(nc, es)
        ps = es.enter_context(nc.psum_tensor("ps", [128, 4096], F32))
        uniq = [0]

        def sb(name, shape, dt, stack=es):
            uniq[0] += 1
            return stack.enter_context(nc.sbuf_tensor("%s_%d" % (name, uniq[0]), list(shape), dt))

        def bank(b, lo=0, hi=512):
            return ps[:, b * 512 + lo: b * 512 + hi]

        def bankbf(b, n):
            return ps[:, b * 512: b * 512 + n // 2].bitcast(BF16)

        def mm(dst, lhsT, rhs, start, stop):
            return lambda h: h.matmul(dst, lhsT, rhs, start=start, stop=stop)

        def tr(dst, src):
            return lambda h: h.transpose(dst, src, ident[:])

        ident = sb("ident", [128, 128], BF16)
        ones_c = sb("ones_c", [128, 1], BF16)
        eps_t = sb("eps_t", [128, 1], F32)
        gcols = sb("gcols_t", [128, 29], F32)
        rstd2 = sb("rstd2", [128, NT // 128], F32)
        r2t = sb("r2t", [128, 4], F32)
        es1 = es.enter_context(ExitStack())
        grow = sb("grow_t", [128, 320], F32, es1)
        ropeT = [sb("rope_t%d" % i, [128, 96], F32, es1) for i in range(2)]
        tk.op("pool", lambda h: h.memset(ident[:], 0.0), w=["ident"])
        tk.op("pool", lambda h: h.affine_select(out=ident[:], in_=ident[:], compare_op=ALU.not_equal, fill=1.0,
                                                base=0, pattern=[[-1, 128]], channel_multiplier=1),
              r=["ident"], w=["ident"])
        tk.op("pool", lambda h: h.memset(ones_c[:], 1.0), w=["ones_c"])
        tk.op("pool", lambda h: h.memset(eps_t[:], EPS), w=["eps_t"])
        tk.dma("sp", gcols[:], gcols_d[:, :], "c0", w=["gcols"])
        tk.dma("sp", grow[:], grow_d.partition_broadcast(128), "c1", w=["grow"])

        w_in_bf = sb("w_in_bf", [128, KC, 1440], BF16, es1)
        wqb_bf = sb("wqb_bf", [128, 3, 768], BF16, es1)
        wkvb_bf = sb("wkvb_bf", [128, 2, 1024], BF16, es1)
        wo_bf = sb("wo_bf", [128, KC, 1024], BF16, es1)
        WIN = [("w_in", kc, hh) for kc in range(KC) for hh in range(2)]

        KaT = sb("KaT", [128, 8, S], BF16, es1)
        VA = sb("VA", [128, NTT, 4, 192], BF16, es1)
        KbT = sb("KbT", [128, 2, S], BF16, es1)
        VB = sb("VB", [128, NTT, 2, 192], BF16, es1)
        NXS = 2
        xsl = [sb("xsl%d" % i, [128, D], F32, es1) for i in range(NXS)]
        xres = xsl
        xs_bf = [sb("xs%d" % i, [128, D], BF16, es1) for i in range(NXS)]
        junk = sb("junk", [128, 512], BF16, es1)
        hT1 = [sb("hT%d" % i, [128, KC, 128], BF16, es1) for i in range(NXS)]
        fst = [sb("fst%d" % i, [128, 4], F32, es1) for i in range(NXS)]
        for t_ in (VA, VB):
            tk.op("pool", lambda h, t_=t_: h.memset(t_[:, :, :, 64:128], 1.0), w=["Vones"])
        stg_n = [0]

        def stage_cast(dst, src, ncol, wkey):
            n = stg_n[0]
            stg_n[0] += 1
            sl = n % 2
            tk.dma("sp", xsl[sl][:, 0:ncol], src, "x%d" % sl, w=[("x", sl)])
            if n % 2 == 0:
                tk.op("dve", lambda h: h.tensor_copy(out=dst, in_=xsl[sl][:, 0:ncol]), r=[("x", sl)], w=[wkey])
            else:
                tk.op("act", lambda h: h.copy(out=dst, in_=xsl[sl][:, 0:ncol]), r=[("x", sl)], w=[wkey])

        for kc in range(KC):
            for hh in range(2):
                stage_cast(w_in_bf[:, kc, hh * 720:(hh + 1) * 720], w_in[kc * 128:(kc + 1) * 128, hh * 720:(hh + 1) * 720],
                           720, ("w_in", kc, hh))
        for kc in range(2):
            stage_cast(wkvb_bf[:, kc, :], w_kv_b[kc * 128:(kc + 1) * 128, :], 1024, ("wkvb", kc))
        for kc in range(3):
            stage_cast(wqb_bf[:, kc, :], w_q_b[kc * 128:(kc + 1) * 128, :], 768, ("wqb", kc))
        WKVB = [("wkvb", kc) for kc in range(2)]
        WQB = [("wqb", kc) for kc in range(3)]
        for kc in range(KC):
            sl = kc % 2
            tk.dma("sp", xres[sl][:], w_o[kc * 128:(kc + 1) * 128, :], "x%d" % sl, w=[("x", sl)])
            if kc % 2 == 0:
                tk.op("dve", lambda h, kc=kc, sl=sl: h.tensor_scalar(
                    out=wo_bf[:, kc, :], in0=xres[sl][:], scalar1=gcols[:, GC_OUT + kc:GC_OUT + kc + 1], scalar2=None,
                    op0=ALU.mult), r=[("x", sl), "gcols"], w=[("wo", kc)])
            else:
                tk.op("act", lambda h, kc=kc, sl=sl: h.activation(
                    out=wo_bf[:, kc, :], in_=xres[sl][:], func=AF.Identity, scale=gcols[:, GC_OUT + kc:GC_OUT + kc + 1]),
                    r=[("x", sl), "gcols"], w=[("wo", kc)])
        WO = [("wo", kc) for kc in range(KC)]

        def rsq(src, dst, tmp, scale, bias_ap, rkeys, wkey, tmpkey):
            tk.op("act", lambda h: h.activation(out=tmp, in_=src, func=AF.Ln, scale=scale, bias=bias_ap),
                  r=rkeys, w=[tmpkey])
            yield
            tk.op("act", lambda h: h.activation(out=dst, in_=tmp, func=AF.Exp, scale=-0.5),
                  r=[tmpkey], w=[wkey])
            yield

        def fe(row0, xi, bk, gc0, evac_eng):
            xt, xb, st, hTt = xsl[xi], xs_bf[xi], fst[xi], hT1[xi]
            tk.dma("sp", xt[:], x[row0:row0 + 128, :], "x%d" % xi, w=[("x", xi)])
            yield
            tk.op("act", lambda h: h.activation(out=xb[:], in_=xt[:], func=AF.Square, accum_out=st[:, 0:1]),
                  r=[("x", xi)], w=[("xs", xi), ("fst", xi, 0)])
            yield
            yield from rsq(st[:, 0:1], st[:, 2:3], st[:, 1:2], 1.0 / D, eps_t[:], [("fst", xi, 0), "eps_t"],
                           ("fst", xi, 2), ("fst", xi, 1))
            tk.op("dve", lambda h: h.tensor_scalar(out=xb[:], in0=xt[:], scalar1=st[:, 2:3], scalar2=None, op0=ALU.mult),
                  r=[("x", xi), ("fst", xi, 2)], w=[("xs", xi)])
            yield
            tp = bankbf(bk, 1024)
            tk.group("pe", [tr(tp[:, kc * 128:(kc + 1) * 128], xb[:, kc * 128:(kc + 1) * 128]) for kc in range(KC)],
                     r=[("xs", xi), "ident"], w=[("ps", bk)])
            yield
            tk.op(evac_eng, lambda h: h.tensor_tensor(out=hTt[:], in0=tp.rearrange("p (k t) -> p k t", k=KC),
                                                      in1=_ap(gcols[:, gc0:gc0 + KC], [[1, KC], [0, 128]]), op=ALU.mult),
                  r=[("ps", bk), "gcols"], w=[("hT", xi)])
            yield

        def rope_apply(x1, x2, o1, o2, cos, sin, tmpa, tmpb, rk, wk, tmpk, ropek):
            tk.op("dve", lambda h: h.tensor_tensor(out=tmpa, in0=x1, in1=cos, op=ALU.mult), r=rk + [ropek], w=[tmpk + "a"])
            yield
            tk.op("dve", lambda h: h.tensor_tensor(out=tmpb, in0=x2, in1=sin, op=ALU.mult), r=rk + [ropek], w=[tmpk + "b"])
            yield
            tk.op("dve", lambda h: h.tensor_tensor(out=o1, in0=tmpa, in1=tmpb, op=ALU.subtract),
                  r=[tmpk + "a", tmpk + "b"], w=wk)
            yield
            tk.op("dve", lambda h: h.tensor_tensor(out=tmpa, in0=x2, in1=cos, op=ALU.mult), r=rk + [ropek], w=[tmpk + "a"])
            yield
            tk.op("dve", lambda h: h.tensor_tensor(out=tmpb, in0=x1, in1=sin, op=ALU.mult), r=rk + [ropek], w=[tmpk + "b"])
            yield
            tk.op("dve", lambda h: h.tensor_tensor(out=o2, in0=tmpa, in1=tmpb, op=ALU.add),
                  r=[tmpk + "a", tmpk + "b"], w=wk)
            yield

        def rope_load(t, ri):
            tk.dma("sp", ropeT[ri][:], rope_d[t * 128:(t + 1) * 128, :], "rope%d" % ri, w=[("rope", ri)])

        def rope_views(ri, H, gqa):
            n = 16 if gqa else 8
            c0 = 0 if gqa else 32
            dims = [[0, H], [n, 2], [1, n]]
            return _ap(ropeT[ri][:, c0:c0 + 2 * n], dims), _ap(ropeT[ri][:, 48 + c0:48 + c0 + 2 * n], dims)

        def chainA(s, t, sl, A):
            b = 4 * sl
            k = lambda n: (n, "A", sl)
            ast, sqs, kn, ka_tm = A["ast"], A["sqs"], A["kn"], A["ka_tm"]
            rope_load(t, sl)
            yield from fe(s * S + t * 128, sl, b, GC_N1, "act_tt")
            hTt, hk = hT1[sl], ("hT", sl)
            tk.group("pe", [mm(bank(b + 1, 0, 288), hTt[:, kc, :], w_in_bf[:, kc, 384:672], kc == 0, kc == KC - 1)
                            for kc in range(KC)], r=[hk] + WIN, w=[("ps", b + 1)])
            tk.group("pe", [mm(bank(b, 0, 256), hTt[:, kc, :], w_in_bf[:, kc, 1184:1440], kc == 0, kc == KC - 1)
                            for kc in range(KC)], r=[hk] + WIN, w=[("ps", b)])
            yield
            tk.op("act", lambda h: h.activation(out=junk[:, 0:256], in_=bank(b + 1, 0, 256), func=AF.Square,
                                                accum_out=ast[:, 0:1]), r=[("ps", b + 1)], w=["junk", k("a0")])
            yield
            tk.op("act", lambda h: h.activation(out=junk[:, 0:32], in_=bank(b + 1, 256, 288), func=AF.Square,
                                                accum_out=ast[:, 1:2]), r=[("ps", b + 1)], w=["junk", k("a1")])
            yield
            tk.op("dve", lambda h: h.tensor_copy(out=A["ckv_bf"][:], in_=bank(b + 1, 0, 256)), r=[("ps", b + 1)], w=[k("ckv_bf")])
            yield
            tk.op("dve", lambda h: h.tensor_tensor(out=A["kpe_g"][:], in0=bank(b + 1, 256, 288),
                                                   in1=grow[:, G_KA + 64:G_KA + 96], op=ALU.mult),
                  r=[("ps", b + 1), "grow"], w=[k("kpe_g")])
            yield
            tp3 = bankbf(b + 1, 256)
            tk.group("pe", [tr(tp3[:, kc * 128:(kc + 1) * 128], A["ckv_bf"][:, kc * 128:(kc + 1) * 128]) for kc in range(2)],
                     r=[k("ckv_bf"), "ident"], w=[("ps", b + 1)])
            yield
            tk.op("dve", lambda h: h.tensor_tensor(out=A["ckvT"][:], in0=tp3.rearrange("p (k t) -> p k t", k=2),
                                                   in1=_ap(gcols[:, GC_KVA:GC_KVA + 2], [[1, 2], [0, 128]]), op=ALU.mult),
                  r=[("ps", b + 1), "gcols"], w=[k("ckvT")])
            yield
            for half in range(2):
                tk.group("pe", [mm(bank(b + 2 + half), A["ckvT"][:, kc, :], wkvb_bf[:, kc, half * 512:(half + 1) * 512],
                                   kc == 0, kc == 1) for kc in range(2)], r=[k("ckvT")] + WKVB, w=[("ps", b + 2 + half)])
            yield
            kv = ps[:, (b + 2) * 512:(b + 4) * 512].rearrange("p (h d) -> p h d", h=8)
            KV = [("ps", b + 2), ("ps", b + 3)]
            yield from rsq(ast[:, 0:1], ast[:, 3:4], ast[:, 2:3], 1.0 / 256, eps_t[:], [k("a0"), "eps_t"], k("a3"), k("a2"))
            tk.op("act", lambda h: h.activation(out=sqs[:], in_=kv[:, :, 0:64], func=AF.Square), r=KV, w=[k("sqs")])
            yield
            tk.op("dve", lambda h: h.tensor_scalar(out=_ap(VA[:, t, 0, 0:64], [[192, 4], [128, 2], [1, 64]]),
                                                   in0=_ap(kv[:, 0, 64:128], [[256, 4], [128, 2], [1, 64]]),
                                                   scalar1=ast[:, 3:4], scalar2=None, op0=ALU.mult),
                  r=KV + [k("a3")], w=[("VA", t)])
            yield
            tk.op("dve", lambda h: h.tensor_reduce(out=ast[:, 8:16], in_=sqs[:], axis=AX.X, op=ALU.add), r=[k("sqs")], w=[k("a8")])
            yield
            tk.op("dve", lambda h: h.tensor_tensor(out=ast[:, 4:5], in0=ast[:, 3:4], in1=ast[:, 3:4], op=ALU.mult),
                  r=[k("a3")], w=[k("a4")])
            yield
            tk.op("dve", lambda h: h.tensor_scalar(out=ast[:, 8:16], in0=ast[:, 8:16], scalar1=ast[:, 4:5],
                                                   scalar2=ast[:, 1:2], op0=ALU.mult, op1=ALU.add),
                  r=[k("a8"), k("a4"), k("a1")], w=[k("a8")])
            yield
            yield from rsq(ast[:, 8:16], ast[:, 24:32], ast[:, 16:24], 1.0 / 96, eps_t[:], [k("a8"), "eps_t"], k("a24"), k("a16"))
            tk.op("dve", lambda h: h.tensor_scalar(out=ast[:, 32:40], in0=ast[:, 24:32], scalar1=ast[:, 3:4],
                                                   scalar2=None, op0=ALU.mult), r=[k("a24"), k("a3")], w=[k("a32")])
            yield
            tk.op("dve", lambda h: h.tensor_tensor(out=kn[:], in0=kv[:, :, 0:64],
                                                   in1=_ap(ast[:, 32:40], [[1, 8], [0, 64]]), op=ALU.mult),
                  r=KV + [k("a32")], w=[k("kn")])
            yield
            tk.op("dve", lambda h: h.tensor_tensor(out=ka_tm[:, :, 0:64], in0=kn[:],
                                                   in1=_ap(grow[:, G_KA:G_KA + 64], [[0, 8], [1, 64]]), op=ALU.mult),
                  r=[k("kn"), "grow"], w=[k("ka_n")])
            yield
            cv, sv = rope_views(sl, 1, False)
            kx = A["kpe_g"][:].rearrange("p (o r f n) -> p o r f n", o=1, r=2, f=2)
            ko = A["kpe_r"][:].rearrange("p (o r f n) -> p o r f n", o=1, r=2, f=2)
            ra = A["rta"][:, 0:16].rearrange("p (o r n) -> p o r n", o=1, r=2)
            rb = A["rtb"][:, 0:16].rearrange("p (o r n) -> p o r n", o=1, r=2)
            yield from rope_apply(kx[:, :, :, 0, :], kx[:, :, :, 1, :], ko[:, :, :, 0, :], ko[:, :, :, 1, :], cv, sv, ra, rb,
                                  [k("kpe_g")], [k("kpe_r")], "rtA%d" % sl, ("rope", sl))
            tk.op("dve", lambda h: h.tensor_tensor(out=ka_tm[:, :, 64:96], in0=_ap(A["kpe_r"][:], [[0, 8], [1, 32]]),
                                                   in1=_ap(ast[:, 24:32], [[1, 8], [0, 32]]), op=ALU.mult),
                  r=[k("kpe_r"), k("a24")], w=[k("ka_r")])
            yield
            gk = bank(b, 0, 128).rearrange("p (g d) -> p g d", g=2)
            tk.op("dve", lambda h: h.tensor_copy(out=_ap(VB[:, t, 0, 0:64], [[192, 2], [128, 2], [1, 64]]),
                                                 in_=_ap(bank(b, 128, 256), [[64, 2], [0, 2], [1, 64]])),
                  r=[("ps", b)], w=[("VB", t)])
            yield
            tk.op("act", lambda h: h.activation(out=sqs[:, 0:2, :], in_=gk, func=AF.Square), r=[("ps", b), k("a8")], w=[k("sqs")])
            yield
            tk.op("dve", lambda h: h.tensor_reduce(out=ast[:, 40:42], in_=sqs[:, 0:2, :], axis=AX.X, op=ALU.add),
                  r=[k("sqs")], w=[k("a40")])
            yield
            yield from rsq(ast[:, 40:42], ast[:, 44:46], ast[:, 42:44], 1.0 / 64, eps_t[:], [k("a40"), "eps_t"], k("a44"), k("a42"))
            gkn, gkr = A["gkn"], A["gkr"]
            tk.op("dve", lambda h: h.tensor_tensor(out=gkn[:], in0=gk, in1=_ap(ast[:, 44:46], [[1, 2], [0, 64]]),
                                                   op=ALU.mult), r=[("ps", b), k("a44")], w=[k("gkn")])
            yield
            tp6 = bankbf(b, 1024)
            tk.group("pe", [tr(tp6[0:96, hh * 128:(hh + 1) * 128], ka_tm[:, hh, :]) for hh in range(8)],
                     r=[k("ka_n"), k("ka_r"), "ident"], w=[("ps", b)])
            yield
            tk.op("dve", lambda h: h.tensor_tensor(out=gkn[:], in0=gkn[:],
                                                   in1=_ap(grow[:, G_KB:G_KB + 64], [[0, 2], [1, 64]]), op=ALU.mult),
                  r=[k("gkn"), "grow"], w=[k("gkn")])
            yield
            tk.op("act", lambda h: h.copy(out=KaT[0:96, :, t * 128:(t + 1) * 128],
                                          in_=tp6[0:96, :].rearrange("p (k t) -> p k t", k=8)),
                  r=[("ps", b)], w=[("KaT", t)])
            yield
            cv, sv = rope_views(sl, 2, True)
            gx = gkn[:].rearrange("p g (r f n) -> p g r f n", r=2, f=2)
            go = gkr[:].rearrange("p g (r f n) -> p g r f n", r=2, f=2)
            ra = A["rta"][:, 0:64].rearrange("p (g r n) -> p g r n", g=2, r=2)
            rb = A["rtb"][:, 0:64].rearrange("p (g r n) -> p g r n", g=2, r=2)
            yield from rope_apply(gx[:, :, :, 0, :], gx[:, :, :, 1, :], go[:, :, :, 0, :], go[:, :, :, 1, :], cv, sv, ra, rb,
                                  [k("gkn")], [k("gkr")], "rtA%d" % sl, ("rope", sl))
            tk.op("dve", lambda h: h.tensor_copy(out=A["kb_tm"][:], in_=_ap(gkr[:], [[64, 2], [0, 2], [1, 64]])),
                  r=[k("gkr")], w=[k("kb_tm")])
            yield
            tp3b = bankbf(b + 1, 256)
            tk.group("pe", [tr(tp3b[:, g * 128:(g + 1) * 128], A["kb_tm"][:, g, :, :].rearrange("p a d -> p (a d)"))
                            for g in range(2)], r=[k("kb_tm"), "ident"], w=[("ps", b + 1)])
            yield
            tk.op("act", lambda h: h.copy(out=KbT[:, :, t * 128:(t + 1) * 128], in_=tp3b.rearrange("p (g t) -> p g t", g=2)),
                  r=[("ps", b + 1)], w=[("KbT", t)])
            yield

        def prep(s, c, i, j, B):
            t = c * 4 + i
            xi, pb, pb2 = j, 4 + 2 * j, 5 + 2 * j
            k = lambda n: (n, "B", j)
            bst, sqq, qn, qa_tm, qb_tm, gq32 = B["bst"], B["sqq"], B["qn"], B["qa_tm"], B["qb_tm"], B["gq32"]
            sqf = sqq[:].rearrange("p h d -> p (h d)")
            rope_load(t, j)
            yield from fe(s * S + c * 512 + i * 128, xi, pb, GC_N1, "dve")
            hTt, hk = hT1[xi], ("hT", xi)
            tk.group("pe", [mm(bank(pb, 0, 384), hTt[:, kc, :], w_in_bf[:, kc, 0:384], kc == 0, kc == KC - 1)
                            for kc in range(KC)], r=[hk] + WIN, w=[("ps", pb)])
            tk.group("pe", [mm(bank(pb2), hTt[:, kc, :], w_in_bf[:, kc, 672:1184], kc == 0, kc == KC - 1)
                            for kc in range(KC)], r=[hk] + WIN, w=[("ps", pb2)])
            yield
            tk.op("act", lambda h: h.activation(out=sqf[:, 0:384], in_=bank(pb, 0, 384), func=AF.Square,
                                                accum_out=bst[:, 0:1]), r=[("ps", pb)], w=[k("sqq"), k("b0")])
            yield
            tk.op("dve", lambda h: h.tensor_copy(out=B["cq_bf"][:], in_=bank(pb, 0, 384)), r=[("ps", pb)], w=[k("cq_bf")])
            yield
            tk.op("act", lambda h: h.copy(out=gq32[:], in_=bank(pb2)), r=[("ps", pb2)], w=[k("gq32")])
            yield
            tp3 = bankbf(pb, 384)
            tk.group("pe", [tr(tp3[:, kc * 128:(kc + 1) * 128], B["cq_bf"][:, kc * 128:(kc + 1) * 128]) for kc in range(3)],
                     r=[k("cq_bf"), "ident"], w=[("ps", pb)])
            yield
            tk.op("dve", lambda h: h.tensor_tensor(out=B["cqT"][:], in0=tp3.rearrange("p (k t) -> p k t", k=3),
                                                   in1=_ap(gcols[:, GC_QA:GC_QA + 3], [[1, 3], [0, 128]]), op=ALU.mult),
                  r=[("ps", pb), "gcols"], w=[k("cqT")])
            yield
            for half, bk in ((0, pb), (1, pb2)):
                tk.group("pe", [mm(bank(bk, 0, 384), B["cqT"][:, kc, :], wqb_bf[:, kc, half * 384:(half + 1) * 384],
                                   kc == 0, kc == 2) for kc in range(3)], r=[k("cqT")] + WQB, w=[("ps", bk)])
            yield
            for half, bk in ((0, pb), (1, pb2)):
                tk.op("act", lambda h, half=half, bk=bk: h.copy(
                    out=qn[:, 4 * half:4 * half + 4, :], in_=bank(bk, 0, 384).rearrange("p (h d) -> p h d", h=4)),
                    r=[("ps", bk)], w=[k("qn%d" % half)])
                yield
            QN = [k("qn0"), k("qn1")]
            tk.op("dve", lambda h: h.tensor_scalar(out=bst[:, 1:2], in0=bst[:, 0:1], scalar1=EPS / 384.0,
                                                   scalar2=EPS * EPS, op0=ALU.mult, op1=ALU.add), r=[k("b0")], w=[k("b1")])
            yield
            tk.op("act", lambda h: h.activation(out=sqq[:], in_=qn[:], func=AF.Square), r=QN, w=[k("sqq")])
            yield
            tk.op("dve", lambda h: h.tensor_reduce(out=bst[:, 8:16], in_=sqq[:], axis=AX.X, op=ALU.add), r=[k("sqq")], w=[k("b8")])
            yield
            yield from rsq(bst[:, 8:16], bst[:, 24:32], bst[:, 16:24], 1.0 / 96, bst[:, 1:2], [k("b8"), k("b1")], k("b24"), k("b16"))
            tk.op("dve", lambda h: h.tensor_tensor(out=qn[:], in0=qn[:], in1=_ap(bst[:, 24:32], [[1, 8], [0, 96]]),
                                                   op=ALU.mult), r=QN + [k("b24")], w=[k("qn0"), k("qn1")])
            yield
            tk.op("dve", lambda h: h.tensor_tensor(out=qa_tm[:, :, 0:64], in0=qn[:, :, 0:64],
                                                   in1=_ap(grow[:, G_QA:G_QA + 64], [[0, 8], [1, 64]]), op=ALU.mult),
                  r=QN + ["grow"], w=[k("qa_n")])
            yield
            tk.op("dve", lambda h: h.tensor_tensor(out=qn[:, :, 64:96], in0=qn[:, :, 64:96],
                                                   in1=_ap(grow[:, G_QA + 64:G_QA + 96], [[0, 8], [1, 32]]), op=ALU.mult),
                  r=QN + ["grow"], w=[k("qn0"), k("qn1")])
            yield
            cv, sv = rope_views(j, 8, False)
            qx = qn[:, :, 64:96].rearrange("p h (r f n) -> p h r f n", r=2, f=2)
            qo = qa_tm[:, :, 64:96].rearrange("p h (r f n) -> p h r f n", r=2, f=2)
            ra = B["rta"][:, 0:128].rearrange("p (h r n) -> p h r n", h=8, r=2)
            rb = B["rtb"][:, 0:128].rearrange("p (h r n) -> p h r n", h=8, r=2)
            yield from rope_apply(qx[:, :, :, 0, :], qx[:, :, :, 1, :], qo[:, :, :, 0, :], qo[:, :, :, 1, :], cv, sv, ra, rb,
                                  QN, [k("qa_r")], "rtB%d" % j, ("rope", j))
            tp6 = bankbf(pb, 1024)
            tk.group("pe", [tr(tp6[0:96, hh * 128:(hh + 1) * 128], qa_tm[:, hh, :]) for hh in range(8)],
                     r=[k("qa_n"), k("qa_r"), "ident"], w=[("ps", pb)])
            yield
            tk.op("act", lambda h: h.copy(out=QaT[0:96, :, i * 128:(i + 1) * 128],
                                          in_=tp6[0:96, :].rearrange("p (k t) -> p k t", k=8)),
                  r=[("ps", pb)], w=[("QaT", i)])
            yield
            g3 = gq32[:].rearrange("p (h d) -> p h d", h=8)
            sqb = sqf[:, 0:512].rearrange("p (h d) -> p h d", h=8)
            tk.op("act", lambda h: h.activation(out=sqb, in_=g3, func=AF.Square), r=[k("gq32"), k("b8")], w=[k("sqq")])
            yield
            tk.op("dve", lambda h: h.tensor_reduce(out=bst[:, 32:40], in_=sqb, axis=AX.X, op=ALU.add), r=[k("sqq")], w=[k("b32")])
            yield
            yield from rsq(bst[:, 32:40], bst[:, 48:56], bst[:, 40:48], 1.0 / 64, eps_t[:], [k("b32"), "eps_t"], k("b48"), k("b40"))
            tk.op("dve", lambda h: h.tensor_tensor(out=g3, in0=g3, in1=_ap(bst[:, 48:56], [[1, 8], [0, 64]]), op=ALU.mult),
                  r=[k("gq32"), k("b48")], w=[k("gq32")])
            yield
            tk.op("dve", lambda h: h.tensor_tensor(out=g3, in0=g3, in1=_ap(grow[:, G_QB:G_QB + 64], [[0, 8], [1, 64]]),
                                                   op=ALU.mult), r=[k("gq32"), "grow"], w=[k("gq32")])
            yield
            cv, sv = rope_views(j, 8, True)
            bx = g3.rearrange("p h (r f n) -> p h r f n", r=2, f=2)
            bo = qb_tm[:].rearrange("p h (r f n) -> p h r f n", r=2, f=2)
            ra = B["rta"][:, 0:256].rearrange("p (h r n) -> p h r n", h=8, r=2)
            rb = B["rtb"][:, 0:256].rearrange("p (h r n) -> p h r n", h=8, r=2)
            yield from rope_apply(bx[:, :, :, 0, :], bx[:, :, :, 1, :], bo[:, :, :, 0, :], bo[:, :, :, 1, :], cv, sv, ra, rb,
                                  [k("gq32")], [k("qb_tm")], "rtB%d" % j, ("rope", j))
            tp7 = bankbf(pb2, 512)
            qbf = qb_tm[:].rearrange("p h d -> p (h d)")
            tk.group("pe", [tr(tp7[:, jj * 128:(jj + 1) * 128], qbf[:, jj * 128:(jj + 1) * 128]) for jj in range(4)],
                     r=[k("qb_tm"), "ident"], w=[("ps", pb2)])
            yield
            tk.op("act", lambda h: h.copy(out=QbT[:, :, i * 128:(i + 1) * 128], in_=tp7.rearrange("p (k t) -> p k t", k=4)),
                  r=[("ps", pb2)], w=[("QbT", i)])
            yield

        def prep_seq(s, c, j, B):
            for i in (j, j + 2):
                yield from prep(s, c, i, j, B)

        with ExitStack() as esb:
            def mkB():
                return dict(
                    cq_bf=sb("cq_bf", [128, 384], BF16, esb), cqT=sb("cqT", [128, 3, 128], BF16, esb),
                    bst=sb("bst", [128, 64], F32, esb), sqq=sb("sqq", [128, 8, 96], F32, esb),
                    qn=sb("qn", [128, 8, 96], F32, esb), qa_tm=sb("qa_tm", [128, 8, 96], BF16, esb),
                    qb_tm=sb("qb_tm", [128, 8, 64], BF16, esb), rta=sb("rtaB", [128, 256], F32, esb),
                    rtb=sb("rtbB", [128, 256], F32, esb), gq32=sb("gq32", [128, 512], F32, esb))
            BS = [mkB(), mkB()]
            QaT = sb("QaT", [128, 8, 512], BF16, esb)
            QbT = sb("QbT", [128, 4, 512], BF16, esb)
            xres1 = sb("xres1", [128, D], F32, esb)
            NP = 3
            shared = sb("shared", [128, 9216], BF16, esb)

            def carve(off, shape, dt):
                nel = int(np.prod(shape))
                size = nel * (2 if dt == BF16 else 4)
                assert off % 64 == 0 and off + size <= 18432
                a = shared[:, off // 2:(off + size) // 2]
                if dt == F32:
                    a = a.bitcast(F32)
                if len(shape) > 1:
                    names = ["a", "b", "c", "d"][:len(shape)]
                    a = a.rearrange("p (%s) -> p %s" % (" ".join(names), " ".join(names)),
                                    **{n: int(v) for n, v in zip(names, shape)})
                return a

            PT = [carve(i * 2048, [1024], BF16) for i in range(NP)]
            rec = carve(6144, [512], F32)
            onT = carve(8192, [KC, 512], BF16)
            sq_on = [carve(16384 + i * 1024, [512], BF16) for i in range(2)]
            ost = sb("ost", [128, 32], F32, esb)
            chunks = [(s, c) for s in range(NSEQ) for c in range(NCH)]
            pcount = [0]
            NSP = 3

            def attention(s, c):
                qkeys_a = [("QaT", i) for i in range(4)]
                qkeys_b = [("QbT", i) for i in range(4)]
                units = [(hd, p) for hd in range(16) for p in range(NTT // 2)]

                def s_mm(hd, p, sp_i):
                    fns, rk = [], []
                    for j in range(2):
                        kb = 2 * p + j
                        if hd < 8:
                            lhsT = KaT[0:96, hd, kb * 128:(kb + 1) * 128]
                            rhs = QaT[0:96, hd, :]
                            rk.append(("KaT", kb))
                        else:
                            jq = hd - 8
                            g, hf = jq // 4, jq % 2
                            lhsT = KbT[hf * 64:(hf + 1) * 64, g, kb * 128:(kb + 1) * 128]
                            rhs = QbT[hf * 64:(hf + 1) * 64, jq // 2, :]
                            rk.append(("KbT", kb))
                        fns.append(mm(bank(2 * sp_i + j), lhsT, rhs, True, True))
                    rk += (qkeys_a if hd < 8 else qkeys_b)
                    tk.group("pe", fns, r=rk, w=[("ps", 2 * sp_i), ("ps", 2 * sp_i + 1)])

                def exp_pv(hd, p, sp_i):
                    slot = pcount[0] % NP
                    pcount[0] += 1
                    scale = (96.0 if hd < 8 else 64.0) ** -0.5
                    src = ps[:, sp_i * 1024:(sp_i + 1) * 1024]
                    tk.op("act", lambda h: h.activation(out=PT[slot][:], in_=src, func=AF.Exp, scale=scale),
                          r=[("ps", 2 * sp_i), ("ps", 2 * sp_i + 1)], w=[("PT", slot)])
                    ob = 6 + hd % 2
                    fns, rk = [], [("PT", slot), "Vones"]
                    for j in range(2):
                        kb = 2 * p + j
                        if hd < 8:
                            lo = 0 if hd % 2 == 0 else 64
                            lhsT = VA[:, kb, hd // 2, lo:lo + 128]
                            rk.append(("VA", kb))
                        else:
                            jq = hd - 8
                            lo = 0 if jq % 2 == 0 else 64
                            lhsT = VB[:, kb, jq // 4, lo:lo + 128]
                            rk.append(("VB", kb))
                        fns.append(mm(bank(ob), lhsT, PT[slot][:, j * 512:(j + 1) * 512], kb == 0, kb == NTT - 1))
                    tk.group("pe", fns, r=rk, w=[("ps", ob)])
                    if p == NTT // 2 - 1:
                        head_done(hd, ob)

                def head_done(hd, ob):
                    cidx = hd // 2
                    if hd % 2 == 0:
                        num, den, o_lo, rc = bank(ob)[0:64, :], bank(ob)[64:128, :], 0, rec[64:128, :]
                    else:
                        num, den, o_lo, rc = bank(ob)[64:128, :], bank(ob)[0:64, :], 64, rec[0:64, :]
                    tk.op("dve", lambda h: h.reciprocal(out=rc, in_=den), r=[("ps", ob)], w=["rec"])
                    tk.op("dve", lambda h: h.tensor_tensor(out=onT[o_lo:o_lo + 64, cidx, :], in0=num, in1=rc, op=ALU.mult),
                          r=[("ps", ob), "rec"], w=[("onT", cidx, hd % 2)])

                for u in range(NSP - 1):
                    s_mm(units[u][0], units[u][1], u % NSP)
                for ui, (hd, p) in enumerate(units):
                    if ui + NSP - 1 < len(units):
                        h2, p2 = units[ui + NSP - 1]
                        s_mm(h2, p2, (ui + NSP - 1) % NSP)
                    exp_pv(hd, p, ui % NSP)

            def wo_residual(s, c):
                for cidx in range(KC):
                    sq = sq_on[cidx % 2]
                    tk.op("dve", lambda h, sq=sq, cidx=cidx: h.tensor_tensor(out=sq[:], in0=onT[:, cidx, :], in1=onT[:, cidx, :],
                                                                            op=ALU.mult),
                          r=[("onT", cidx, 0), ("onT", cidx, 1)], w=[("sq_on", cidx % 2)])
                    tk.group("pe", [mm(bank(0, i * 8 + cidx, i * 8 + cidx + 1), sq[:, i * 128:(i + 1) * 128], ones_c[:],
                                       True, True) for i in range(4)],
                             r=[("sq_on", cidx % 2), "ones_c"], w=[("ps", 0)])
                    yield
                tk.op("dve", lambda h: h.tensor_reduce(out=ost[:, 0:8],
                                                       in_=bank(0, 0, 32).rearrange("p (i g k) -> p i g k", i=4, g=2),
                                                       axis=AX.X, op=ALU.add), r=[("ps", 0)], w=["o0"])
                yield
                yield from rsq(ost[:, 0:8], ost[:, 16:24], ost[:, 8:16], 1.0 / 512, eps_t[:], ["o0", "eps_t"], "o16", "o8")
                xr = xres1
                for i in range(4):
                    row0 = s * S + c * 512 + i * 128
                    tk.dma("sp", xr[:], x[row0:row0 + 128, :], "xres1", w=["xres1"])
                    for g in range(2):
                        for half in range(2):
                            bk = g * 2 + half
                            tk.group("pe", [mm(bank(bk), onT[:, g * 4 + kc, i * 128:(i + 1) * 128],
                                               wo_bf[:, g * 4 + kc, half * 512:(half + 1) * 512], kc == 0, kc == 3)
                                            for kc in range(4)],
                                     r=[("onT", g * 4 + kc, q) for kc in range(4) for q in range(2)] + WO, w=[("ps", bk)])
                        yield
                    for g in range(2):
                        acc = ps[:, (g * 2) * 512:(g * 2 + 2) * 512]
                        tk.op("dve", lambda h, acc=acc, g=g: h.scalar_tensor_tensor(
                            out=xr[:], in0=acc, scalar=ost[:, 16 + i * 2 + g:16 + i * 2 + g + 1], in1=xr[:],
                            op0=ALU.mult, op1=ALU.add),
                            r=[("ps", g * 2), ("ps", g * 2 + 1), "o16", "xres1"], w=["xres1"])
                        yield
                    tk.dma("sp", y[row0:row0 + 128, :], xr[:], "yst", r=["xres1"])
                    gt = row0 // 128
                    tk.op("act", lambda h: h.activation(out=junk[:, 0:512], in_=xr[:, 0:512], func=AF.Square,
                                                        accum_out=r2t[:, 0:1]), r=["xres1"], w=["junk", "r2a"])
                    yield
                    tk.op("act", lambda h: h.activation(out=junk[:, 0:512], in_=xr[:, 512:1024], func=AF.Square,
                                                        accum_out=r2t[:, 1:2]), r=["xres1"], w=["junk", "r2b"])
                    yield
                    tk.op("dve", lambda h: h.tensor_tensor(out=r2t[:, 2:3], in0=r2t[:, 0:1], in1=r2t[:, 1:2], op=ALU.add),
                          r=["r2a", "r2b"], w=["r2c"])
                    yield
                    yield from rsq(r2t[:, 2:3], rstd2[:, gt:gt + 1], r2t[:, 3:4], 1.0 / D, eps_t[:], ["r2c", "eps_t"],
                                   ("rstd2", gt), "r2d")

            for s in range(NSEQ):
                if True:
                    AS = []
                    for sl in range(2):
                        o = [sl * 9216]

                        def cv_(shape, dt):
                            nel = int(np.prod(shape)) * (2 if dt == BF16 else 4)
                            a = carve(o[0], shape, dt)
                            o[0] += (nel + 63) // 64 * 64
                            return a
                        AS.append(dict(ckv_bf=cv_([256], BF16), ckvT=cv_([2, 128], BF16), ast=cv_([48], F32),
                                       sqs=cv_([8, 64], F32), kn=cv_([8, 64], F32), ka_tm=cv_([8, 96], BF16),
                                       kpe_g=cv_([32], F32), kpe_r=cv_([32], F32), rta=cv_([64], F32), rtb=cv_([64], F32),
                                       gkn=cv_([2, 64], F32), gkr=cv_([2, 64], F32), kb_tm=cv_([2, 2, 64], BF16)))
                        assert o[0] <= (sl + 1) * 9216

                    def seqA(sl):
                        for t in range(sl, NTT, 2):
                            yield from chainA(s, t, sl, AS[sl])

                    tk.barrier()
                    drive([seqA(0), seqA(1)], 2)
                    tk.barrier()
                if "kv" in dump and s == NSEQ - 1:
                    for nm, t_ in (("KaT", KaT), ("VA", VA), ("KbT", KbT), ("VB", VB)):
                        shp = [128, int(np.prod(t_.shape[1:]))]
                        dumps[nm] = nc.dram_tensor("dump_" + nm, shp, BF16, kind="ExternalOutput").ap()
                        nd = len(t_.shape)
                        src = t_[:].rearrange("p a b -> p (a b)") if nd == 3 else t_[:].rearrange("p a b c -> p (a b c)")
                        tk.dma("sp", dumps[nm][:, :], src, "dump", r=[])
                if s == 0:
                    drive([prep_seq(0, 0, 0, BS[0]), prep_seq(0, 0, 1, BS[1])], 2)
                for c in range(NCH):
                    ci = s * NCH + c
                    nxt = chunks[ci + 1] if ci + 1 < len(chunks) else None
                    attention(s, c)
                    gens = [wo_residual(s, c)]
                    if nxt:
                        gens += [prep_seq(nxt[0], nxt[1], 0, BS[0]), prep_seq(nxt[0], nxt[1], 1, BS[1])]
                    drive(gens, 3)
        es1.close()
        tk.barrier()

        if do_ffn:
            with ExitStack() as es2:
                wg_bf = sb("wg_bf", [128, KC, DFF], BF16, es2)
                wu_bf = sb("wu_bf", [128, KC, DFF], BF16, es2)
                wd_bf = sb("wd_bf", [128, FC, D], BF16, es2)
                NX2 = 5
                x2sl = [sb("x2sl%d" % i, [128, D], F32, es2) for i in range(NX2)]
                xs2 = [sb("xs2_%d" % i, [128, D], BF16, es2) for i in range(2)]
                hT2 = [sb("hT2_%d" % i, [128, KC, 512], BF16, es2) for i in range(2)]
                aT = sb("aT", [128, FC, 512], BF16, es2)
                sg = [sb("sg%d" % i, [128, 512], F32, es2) for i in range(2)]
                stg2 = [sb("stg2_%d" % i, [128, 1024], F32, es2) for i in range(2)]
                st2n = [0]

                def stage2(dst, src, wkey, view3):
                    n = st2n[0]
                    st2n[0] += 1
                    sl = n % 2
                    sv = stg2[sl][:].rearrange("p (k n) -> p k n", k=KC) if view3 else stg2[sl][:]
                    tk.dma("sp", sv, src, "stg2_%d" % sl, w=[("stg2", sl)])
                    if n % 2 == 0:
                        tk.op("dve", lambda h: h.tensor_copy(out=dst, in_=sv), r=[("stg2", sl)], w=[wkey])
                    else:
                        tk.op("act", lambda h: h.copy(out=dst, in_=sv), r=[("stg2", sl)], w=[wkey])

                def load_gu(f):
                    stage2(wg_bf[:, :, f * 128:(f + 1) * 128], wg_v[:, :, f * 128:(f + 1) * 128], ("wg", f), True)
                    stage2(wu_bf[:, :, f * 128:(f + 1) * 128], wu_v[:, :, f * 128:(f + 1) * 128], ("wu", f), True)

                def load_d(f):
                    stage2(wd_bf[:, f, :], w_down[f * 128:(f + 1) * 128, :], ("wd", f), False)
                wg_v = w_gate.rearrange("(k p) n -> p k n", p=128)
                wu_v = w_up.rearrange("(k p) n -> p k n", p=128)
                NCH2 = NT // 512
                fe2 = [0]

                def front_end2(c, i):
                    n = fe2[0]
                    fe2[0] += 1
                    sl, b_i = (c * 4 + i) % NX2, n % 2
                    gt = c * 4 + i
                    xt, xb = x2sl[sl], xs2[b_i]
                    tk.dma("sp", xt[:], y[gt * 128:(gt + 1) * 128, :], "x2_%d" % sl, w=[("x2", sl)])
                    tk.op("dve", lambda h: h.tensor_scalar(out=xb[:], in0=xt[:], scalar1=rstd2[:, gt:gt + 1], scalar2=None,
                                                           op0=ALU.mult), r=[("x2", sl), ("rstd2", gt)], w=[("xs2", b_i)])
                    tp = bankbf(7, 1024)
                    tk.group("pe", [tr(tp[:, kc * 128:(kc + 1) * 128], xb[:, kc * 128:(kc + 1) * 128]) for kc in range(KC)],
                             r=[("xs2", b_i), "ident"], w=[("ps", 7)])
                    tk.op("dve", lambda h: h.tensor_tensor(out=hT2[c % 2][:, :, i * 128:(i + 1) * 128],
                                                           in0=tp.rearrange("p (k t) -> p k t", k=KC),
                                                           in1=_ap(gcols[:, GC_N2:GC_N2 + KC], [[1, KC], [0, 128]]), op=ALU.mult),
                          r=[("ps", 7), "gcols"], w=[("hT2", c % 2, i)])

                def down(c, i):
                    sl = (c * 4 + i) % NX2
                    gt = c * 4 + i
                    for half in range(2):
                        bk = 4 + (2 * i + half) % 3
                        tk.group("pe", [mm(bank(bk), aT[:, f, i * 128:(i + 1) * 128], wd_bf[:, f, half * 512:(half + 1) * 512],
                                           f == 0, f == FC - 1) for f in range(FC)],
                                 r=[("aT", f) for f in range(FC)] + [("wd", f) for f in range(FC)], w=[("ps", bk)])
                        tk.op("dve", lambda h, bk=bk, half=half: h.tensor_tensor(
                            out=x2sl[sl][:, half * 512:(half + 1) * 512], in0=bank(bk),
                            in1=x2sl[sl][:, half * 512:(half + 1) * 512], op=ALU.add),
                            r=[("ps", bk), ("x2", sl)], w=[("x2", sl)])
                    tk.dma("sp", y[gt * 128:(gt + 1) * 128, :], x2sl[sl][:], "y2_%d" % sl, r=[("x2", sl)])

                for i in range(4):
                    front_end2(0, i)
                load_gu(0)
                for c in range(NCH2):
                    hk = [("hT2", c % 2, i) for i in range(4)]
                    for f in range(FC):
                        gb, ub = f % 2, 2 + f % 2
                        if c == 0:
                            if f + 1 < FC:
                                load_gu(f + 1)
                            load_d(f)
                        tk.group("pe", [mm(bank(gb), wg_bf[:, kc, f * 128:(f + 1) * 128], hT2[c % 2][:, kc, :],
                                           kc == 0, kc == KC - 1) for kc in range(KC)], r=hk + [("wg", f)], w=[("ps", gb)])
                        tk.group("pe", [mm(bank(ub), wu_bf[:, kc, f * 128:(f + 1) * 128], hT2[c % 2][:, kc, :],
                                           kc == 0, kc == KC - 1) for kc in range(KC)], r=hk + [("wu", f)], w=[("ps", ub)])
                        tk.op("act", lambda h, f=f, gb=gb: h.activation(out=sg[f % 2][:], in_=bank(gb), func=AF.Silu),
                              r=[("ps", gb)], w=[("sg", f % 2)])
                        tk.op("dve", lambda h, f=f, ub=ub: h.tensor_tensor(out=aT[:, f, :], in0=bank(ub), in1=sg[f % 2][:],
                                                                          op=ALU.mult),
                              r=[("ps", ub), ("sg", f % 2)], w=[("aT", f)])
                        if c + 1 < NCH2 and f == 10:
                            front_end2(c + 1, 0)
                    for i in range(4):
                        down(c, i)
                        if c + 1 < NCH2 and i < 3:
                            front_end2(c + 1, i + 1)
        tk.finish("sp")
    return nc, dumps


def _rope_table(S):
    t = np.arange(S)
    row = (t // 64).astype(np.float32)
    col = (t % 64).astype(np.float32)
    out = np.zeros((S, 96), np.float32)
    inv_g = (np.float32(10000.0) ** (-(np.arange(0, 32, 2, dtype=np.float32) / np.float32(32)))).astype(np.float32)
    inv_m = (np.float32(10000.0) ** (-(np.arange(0, 16, 2, dtype=np.float32) / np.float32(16)))).astype(np.float32)
    ang = np.concatenate([row[:, None] * inv_g[None], col[:, None] * inv_g[None],
                          row[:, None] * inv_m[None], col[:, None] * inv_m[None]], axis=1).astype(np.float32)
    out[:, 0:48] = np.cos(ang)
    out[:, 48:96] = np.sin(ang)
    return out


def _host_inputs(inp, S):
    f = lambda a: np.ascontiguousarray(np.asarray(a, dtype=np.float32))
    col = lambda g: f(g).reshape(-1, 128).T
    gcols = np.concatenate([col(inp["norm1_g"][0]), col(inp["q_a_norm_g"][0]), col(inp["kv_a_norm_g"][0]),
                            col(np.concatenate([f(inp["mla_out_norm_g"][0]), f(inp["gqa_out_norm_g"][0])])),
                            col(inp["norm2_g"][0])], axis=1)
    grow = np.concatenate([f(inp["mla_q_norm_g"][0]), f(inp["mla_k_norm_g"][0]),
                           f(inp["gqa_q_norm_g"][0]), f(inp["gqa_k_norm_g"][0])])[None, :]
    shared = {
        "w_in": f(inp["w_in"][0]), "w_q_b": f(inp["w_q_b"][0]), "w_kv_b": f(inp["w_kv_b"][0]), "w_o": f(inp["w_o"][0]),
        "w_gate": f(inp["w_gate"][0]), "w_up": f(inp["w_up"][0]), "w_down": f(inp["w_down"][0]),
        "gcols": f(gcols), "grow": f(grow), "rope": _rope_table(S),
    }
    return shared


def kernel(**inputs):
    x = np.asarray(inputs["x"], dtype=np.float32)
    B, S, _ = x.shape
    nseq = B // N_CORES
    shared = _host_inputs(inputs, S)
    nc, _ = build_nc(nseq, S)
    in_maps = []
    for c in range(N_CORES):
        m = dict(shared)
        m["x"] = np.ascontiguousarray(x[c * nseq:(c + 1) * nseq].reshape(nseq * S, D))
        in_maps.append(m)
    res = run_bass_kernel_spmd(nc, in_maps, core_ids=list(range(N_CORES)))
    out = np.concatenate([np.asarray(r["y"]).reshape(nseq, S, D) for r in res.results], axis=0)
    return out.astype(np.float32)
```

```python
from contextlib import ExitStack

import numpy as np
import concourse.bass as bass
import concourse.mybir as mybir
from concourse.bass_utils import run_bass_kernel_spmd

F32 = mybir.dt.float32
BF16 = mybir.dt.bfloat16
AF = mybir.ActivationFunctionType
ALU = mybir.AluOpType
AX = mybir.AxisListType

D = 1024
KC = 8
DFF = 2816
FC = 22
EPS = 1e-6
N_CORES = 8


class TK:
    def __init__(self, nc, es):
        self.nc = nc
        self.es = es
        self.E = {}
        for name, h in (("pe", nc.tensor), ("act", nc.scalar), ("dve", nc.vector),
                        ("pool", nc.gpsimd), ("sp", nc.sync)):
            self.E[name] = dict(h=h, sem=es.enter_context(nc.semaphore("e_" + name)), cnt=0, waited={})
        self.res = {}
        self.dsem = {}

    @staticmethod
    def _excl(key):
        return isinstance(key, tuple) and key[0] == "ps"

    def _collect(self, eng, reads, writes):
        deps = {}

        def add(st, same_ok):
            if st is None:
                return
            key, sem, val = st
            if key == eng and (eng == "pe" or not same_ok):
                return
            if key not in deps or deps[key][1] < val:
                deps[key] = (sem, val)

        for k in reads:
            d = self.res.get(k)
            if d:
                add(d["w"], True)
        for k in writes:
            d = self.res.get(k)
            if d:
                add(d["w"], False)
                for st in d["r"].values():
                    add(st, False)
        return deps

    def _waits(self, eng, deps):
        E = self.E[eng]
        for key, (sem, val) in deps.items():
            if E["waited"].get(key, 0) >= val:
                continue
            E["waited"][key] = val
            E["h"].wait_ge(sem, val)

    def _record(self, stamp, reads, writes):
        for k in reads:
            d = self.res.setdefault(k, {"w": None, "r": {}})
            d["r"][stamp[0]] = stamp
        for k in writes:
            self.res[k] = {"w": stamp, "r": {}}

    def _split(self, r, w):
        r = list(r)
        w = list(w)
        ex = [k for k in r if self._excl(k)]
        r = [k for k in r if not self._excl(k)]
        for k in ex:
            if k not in w:
                w.append(k)
        return r, w

    def op(self, eng, fn, r=(), w=()):
        if eng == "act_tt":
            eng = "dve"
        r, w = self._split(r, w)
        self._waits(eng, self._collect(eng, r, w))
        E = self.E[eng]
        E["cnt"] += 1
        fn(E["h"]).then_inc(E["sem"], 1)
        self._record((eng, E["sem"], E["cnt"]), r, w)

    def group(self, eng, fns, r=(), w=()):
        r, w = self._split(r, w)
        self._waits(eng, self._collect(eng, r, w))
        E = self.E[eng]
        E["cnt"] += 1
        for i, fn in enumerate(fns):
            ins = fn(E["h"])
            if i == len(fns) - 1:
                ins.then_inc(E["sem"], 1)
        self._record((eng, E["sem"], E["cnt"]), r, w)

    def dma(self, eng, out, in_, semkey, r=(), w=(), **kw):
        r, w = self._split(r, w)
        self._waits(eng, self._collect(eng, r, w))
        if semkey not in self.dsem:
            self.dsem[semkey] = [self.es.enter_context(self.nc.semaphore("d_" + semkey)), 0]
        ds = self.dsem[semkey]
        ds[1] += 16
        self.E[eng]["h"].dma_start(out=out, in_=in_, **kw).then_inc(ds[0], 16)
        self._record(("d:" + semkey, ds[0], ds[1]), r, w)

    def barrier(self):
        for eng, E in self.E.items():
            deps = {}
            for e2, E2 in self.E.items():
                if e2 != eng and E2["cnt"] > 0:
                    deps[e2] = (E2["sem"], E2["cnt"])
            for k, ds in self.dsem.items():
                if ds[1] > 0:
                    deps["d:" + k] = (ds[0], ds[1])
            self._waits(eng, deps)

    def finish(self, eng="sp"):
        deps = {}
        for k, ds in self.dsem.items():
            if ds[1] > 0:
                deps["d:" + k] = (ds[0], ds[1])
        for e2, E2 in self.E.items():
            if e2 != eng and E2["cnt"] > 0:
                deps[e2] = (E2["sem"], E2["cnt"])
        self._waits(eng, deps)


def _ap(base, dims):
    return bass.AP(base.tensor, base.offset, [list(base.ap[0])] + [list(d) for d in dims])


def drive(gens, width):
    pending = list(gens)
    active = []
    while pending or active:
        while pending and len(active) < width:
            active.append(pending.pop(0))
        for g in list(active):
            try:
                next(g)
            except StopIteration:
                active.remove(g)


def build_nc(NSEQ, S, do_ffn=True, dump=()):
    nc = bass.Bass("TRN2", target_bir_lowering=False)
    NT = NSEQ * S
    NTT = S // 128
    NCH = S // 512

    def din(name, shape):
        return nc.dram_tensor(name, list(shape), F32, kind="ExternalInput").ap()

    x = din("x", [NT, D])
    w_in = din("w_in", [D, 1440])
    w_q_b = din("w_q_b", [384, 768])
    w_kv_b = din("w_kv_b", [256, 1024])
    w_o = din("w_o", [D, D])
    w_gate = din("w_gate", [D, DFF])
    w_up = din("w_up", [D, DFF])
    w_down = din("w_down", [DFF, D])
    gcols_d = din("gcols", [128, 29])
    grow_d = din("grow", [1, 320])
    rope_d = din("rope", [S, 96])
    y = nc.dram_tensor("y", [NT, D], F32, kind="ExternalOutput").ap()
    dumps = {}
    GC_N1, GC_QA, GC_KVA, GC_OUT, GC_N2 = 0, 8, 11, 13, 21
    G_QA, G_KA, G_QB, G_KB = 0, 96, 192, 256

    with ExitStack() as es:
        tk = TK(nc, es)
        ps = es.enter_context(nc.psum_tensor("ps", [128, 4096], F32))
        uniq = [0]

        def sb(name, shape, dt, stack=es):
            uniq[0] += 1
            return stack.enter_context(nc.sbuf_tensor("%s_%d" % (name, uniq[0]), list(shape), dt))

        def bank(b, lo=0, hi=512):
            return ps[:, b * 512 + lo: b * 512 + hi]

        def bankbf(b, n):
            return ps[:, b * 512: b * 512 + n // 2].bitcast(BF16)

        def mm(dst, lhsT, rhs, start, stop):
            return lambda h: h.matmul(dst, lhsT, rhs, start=start, stop=stop)

        def tr(dst, src):
            return lambda h: h.transpose(dst, src, ident[:])

        ident = sb("ident", [128, 128], BF16)
        ones_c = sb("ones_c", [128, 1], BF16)
        eps_t = sb("eps_t", [128, 1], F32)
        gcols = sb("gcols_t", [128, 29], F32)
        rstd2 = sb("rstd2", [128, NT // 128], F32)
        r2t = sb("r2t", [128, 4], F32)
        es1 = es.enter_context(ExitStack())
        grow = sb("grow_t", [128, 320], F32, es1)
        ropeT = [sb("rope_t%d" % i, [128, 96], F32, es1) for i in range(2)]
        tk.op("pool", lambda h: h.memset(ident[:], 0.0), w=["ident"])
        tk.op("pool", lambda h: h.affine_select(out=ident[:], in_=ident[:], compare_op=ALU.not_equal, fill=1.0,
                                                base=0, pattern=[[-1, 128]], channel_multiplier=1),
              r=["ident"], w=["ident"])
        tk.op("pool", lambda h: h.memset(ones_c[:], 1.0), w=["ones_c"])
        tk.op("pool", lambda h: h.memset(eps_t[:], EPS), w=["eps_t"])
        tk.dma("sp", gcols[:], gcols_d[:, :], "c0", w=["gcols"])
        tk.dma("sp", grow[:], grow_d.partition_broadcast(128), "c1", w=["grow"])

        w_in_bf = sb("w_in_bf", [128, KC, 1440], BF16, es1)
        wqb_bf = sb("wqb_bf", [128, 3, 768], BF16, es1)
        wkvb_bf = sb("wkvb_bf", [128, 2, 1024], BF16, es1)
        wo_bf = sb("wo_bf", [128, KC, 1024], BF16, es1)
        WIN = [("w_in", kc, hh) for kc in range(KC) for hh in range(2)]

        KaT = sb("KaT", [128, 8, S], BF16, es1)
        VA = sb("VA", [128, NTT, 4, 192], BF16, es1)
        KbT = sb("KbT", [128, 2, S], BF16, es1)
        VB = sb("VB", [128, NTT, 2, 192], BF16, es1)
        NXS = 2
        xsl = [sb("xsl%d" % i, [128, D], F32, es1) for i in range(NXS)]
        xres = xsl
        xs_bf = [sb("xs%d" % i, [128, D], BF16, es1) for i in range(NXS)]
        junk = sb("junk", [128, 512], BF16, es1)
        hT1 = [sb("hT%d" % i, [128, KC, 128], BF16, es1) for i in range(NXS)]
        fst = [sb("fst%d" % i, [128, 4], F32, es1) for i in range(NXS)]
        for t_ in (VA, VB):
            tk.op("pool", lambda h, t_=t_: h.memset(t_[:, :, :, 64:128], 1.0), w=["Vones"])
        stg_n = [0]

        def stage_cast(dst, src, ncol, wkey):
            n = stg_n[0]
            stg_n[0] += 1
            sl = n % 2
            tk.dma("sp", xsl[sl][:, 0:ncol], src, "x%d" % sl, w=[("x", sl)])
            if n % 2 == 0:
                tk.op("dve", lambda h: h.tensor_copy(out=dst, in_=xsl[sl][:, 0:ncol]), r=[("x", sl)], w=[wkey])
            else:
                tk.op("act", lambda h: h.copy(out=dst, in_=xsl[sl][:, 0:ncol]), r=[("x", sl)], w=[wkey])

        for kc in range(KC):
            for hh in range(2):
                stage_cast(w_in_bf[:, kc, hh * 720:(hh + 1) * 720], w_in[kc * 128:(kc + 1) * 128, hh * 720:(hh + 1) * 720],
                           720, ("w_in", kc, hh))
        for kc in range(2):
            stage_cast(wkvb_bf[:, kc, :], w_kv_b[kc * 128:(kc + 1) * 128, :], 1024, ("wkvb", kc))
        for kc in range(3):
            stage_cast(wqb_bf[:, kc, :], w_q_b[kc * 128:(kc + 1) * 128, :], 768, ("wqb", kc))
        WKVB = [("wkvb", kc) for kc in range(2)]
        WQB = [("wqb", kc) for kc in range(3)]
        for kc in range(KC):
            sl = kc % 2
            tk.dma("sp", xres[sl][:], w_o[kc * 128:(kc + 1) * 128, :], "x%d" % sl, w=[("x", sl)])
            if kc % 2 == 0:
                tk.op("dve", lambda h, kc=kc, sl=sl: h.tensor_scalar(
                    out=wo_bf[:, kc, :], in0=xres[sl][:], scalar1=gcols[:, GC_OUT + kc:GC_OUT + kc + 1], scalar2=None,
                    op0=ALU.mult), r=[("x", sl), "gcols"], w=[("wo", kc)])
            else:
                tk.op("act", lambda h, kc=kc, sl=sl: h.activation(
                    out=wo_bf[:, kc, :], in_=xres[sl][:], func=AF.Identity, scale=gcols[:, GC_OUT + kc:GC_OUT + kc + 1]),
                    r=[("x", sl), "gcols"], w=[("wo", kc)])
        WO = [("wo", kc) for kc in range(KC)]

        def rsq(src, dst, tmp, scale, bias_ap, rkeys, wkey, tmpkey):
            tk.op("act", lambda h: h.activation(out=tmp, in_=src, func=AF.Ln, scale=scale, bias=bias_ap),
                  r=rkeys, w=[tmpkey])
            yield
            tk.op("act", lambda h: h.activation(out=dst, in_=tmp, func=AF.Exp, scale=-0.5),
                  r=[tmpkey], w=[wkey])
            yield

        def fe(row0, xi, bk, gc0, evac_eng):
            xt, xb, st, hTt = xsl[xi], xs_bf[xi], fst[xi], hT1[xi]
            tk.dma("sp", xt[:], x[row0:row0 + 128, :], "x%d" % xi, w=[("x", xi)])
            yield
            tk.op("act", lambda h: h.activation(out=xb[:], in_=xt[:], func=AF.Square, accum_out=st[:, 0:1]),
                  r=[("x", xi)], w=[("xs", xi), ("fst", xi, 0)])
            yield
            yield from rsq(st[:, 0:1], st[:, 2:3], st[:, 1:2], 1.0 / D, eps_t[:], [("fst", xi, 0), "eps_t"],
                           ("fst", xi, 2), ("fst", xi, 1))
            tk.op("dve", lambda h: h.tensor_scalar(out=xb[:], in0=xt[:], scalar1=st[:, 2:3], scalar2=None, op0=ALU.mult),
                  r=[("x", xi), ("fst", xi, 2)], w=[("xs", xi)])
            yield
            tp = bankbf(bk, 1024)
            tk.group("pe", [tr(tp[:, kc * 128:(kc + 1) * 128], xb[:, kc * 128:(kc + 1) * 128]) for kc in range(KC)],
                     r=[("xs", xi), "ident"], w=[("ps", bk)])
            yield
            tk.op(evac_eng, lambda h: h.tensor_tensor(out=hTt[:], in0=tp.rearrange("p (k t) -> p k t", k=KC),
                                                      in1=_ap(gcols[:, gc0:gc0 + KC], [[1, KC], [0, 128]]), op=ALU.mult),
                  r=[("ps", bk), "gcols"], w=[("hT", xi)])
            yield

        def rope_apply(x1, x2, o1, o2, cos, sin, tmpa, tmpb, rk, wk, tmpk, ropek):
            tk.op("dve", lambda h: h.tensor_tensor(out=tmpa, in0=x1, in1=cos, op=ALU.mult), r=rk + [ropek], w=[tmpk + "a"])
            yield
            tk.op("dve", lambda h: h.tensor_tensor(out=tmpb, in0=x2, in1=sin, op=ALU.mult), r=rk + [ropek], w=[tmpk + "b"])
            yield
            tk.op("dve", lambda h: h.tensor_tensor(out=o1, in0=tmpa, in1=tmpb, op=ALU.subtract),
                  r=[tmpk + "a", tmpk + "b"], w=wk)
            yield
            tk.op("dve", lambda h: h.tensor_tensor(out=tmpa, in0=x2, in1=cos, op=ALU.mult), r=rk + [ropek], w=[tmpk + "a"])
            yield
            tk.op("dve", lambda h: h.tensor_tensor(out=tmpb, in0=x1, in1=sin, op=ALU.mult), r=rk + [ropek], w=[tmpk + "b"])
            yield
            tk.op("dve", lambda h: h.tensor_tensor(out=o2, in0=tmpa, in1=tmpb, op=ALU.add),
                  r=[tmpk + "a", tmpk + "b"], w=wk)
            yield

        def rope_load(t, ri):
            tk.dma("sp", ropeT[ri][:], rope_d[t * 128:(t + 1) * 128, :], "rope%d" % ri, w=[("rope", ri)])

        def rope_views(ri, H, gqa):
            n = 16 if gqa else 8
            c0 = 0 if gqa else 32
            dims = [[0, H], [n, 2], [1, n]]
            return _ap(ropeT[ri][:, c0:c0 + 2 * n], dims), _ap(ropeT[ri][:, 48 + c0:48 + c0 + 2 * n], dims)

        def chainA(s, t, sl, A):
            b = 4 * sl
            k = lambda n: (n, "A", sl)
            ast, sqs, kn, ka_tm = A["ast"], A["sqs"], A["kn"], A["ka_tm"]
            rope_load(t, sl)
            yield from fe(s * S + t * 128, sl, b, GC_N1, "act_tt")
            hTt, hk = hT1[sl], ("hT", sl)
            tk.group("pe", [mm(bank(b + 1, 0, 288), hTt[:, kc, :], w_in_bf[:, kc, 384:672], kc == 0, kc == KC - 1)
                            for kc in range(KC)], r=[hk] + WIN, w=[("ps", b + 1)])
            tk.group("pe", [mm(bank(b, 0, 256), hTt[:, kc, :], w_in_bf[:, kc, 1184:1440], kc == 0, kc == KC - 1)
                            for kc in range(KC)], r=[hk] + WIN, w=[("ps", b)])
            yield
            tk.op("act", lambda h: h.activation(out=junk[:, 0:256], in_=bank(b + 1, 0, 256), func=AF.Square,
                                                accum_out=ast[:, 0:1]), r=[("ps", b + 1)], w=["junk", k("a0")])
            yield
            tk.op("act", lambda h: h.activation(out=junk[:, 0:32], in_=bank(b + 1, 256, 288), func=AF.Square,
                                                accum_out=ast[:, 1:2]), r=[("ps", b + 1)], w=["junk", k("a1")])
            yield
            tk.op("dve", lambda h: h.tensor_copy(out=A["ckv_bf"][:], in_=bank(b + 1, 0, 256)), r=[("ps", b + 1)], w=[k("ckv_bf")])
            yield
            tk.op("dve", lambda h: h.tensor_tensor(out=A["kpe_g"][:], in0=bank(b + 1, 256, 288),
                                                   in1=grow[:, G_KA + 64:G_KA + 96], op=ALU.mult),
                  r=[("ps", b + 1), "grow"], w=[k("kpe_g")])
            yield
            tp3 = bankbf(b + 1, 256)
            tk.group("pe", [tr(tp3[:, kc * 128:(kc + 1) * 128], A["ckv_bf"][:, kc * 128:(kc + 1) * 128]) for kc in range(2)],
                     r=[k("ckv_bf"), "ident"], w=[("ps", b + 1)])
            yield
            tk.op("dve", lambda h: h.tensor_tensor(out=A["ckvT"][:], in0=tp3.rearrange("p (k t) -> p k t", k=2),
                                                   in1=_ap(gcols[:, GC_KVA:GC_KVA + 2], [[1, 2], [0, 128]]), op=ALU.mult),
                  r=[("ps", b + 1), "gcols"], w=[k("ckvT")])
            yield
            for half in range(2):
                tk.group("pe", [mm(bank(b + 2 + half), A["ckvT"][:, kc, :], wkvb_bf[:, kc, half * 512:(half + 1) * 512],
                                   kc == 0, kc == 1) for kc in range(2)], r=[k("ckvT")] + WKVB, w=[("ps", b + 2 + half)])
            yield
            kv = ps[:, (b + 2) * 512:(b + 4) * 512].rearrange("p (h d) -> p h d", h=8)
            KV = [("ps", b + 2), ("ps", b + 3)]
            yield from rsq(ast[:, 0:1], ast[:, 3:4], ast[:, 2:3], 1.0 / 256, eps_t[:], [k("a0"), "eps_t"], k("a3"), k("a2"))
            tk.op("act", lambda h: h.activation(out=sqs[:], in_=kv[:, :, 0:64], func=AF.Square), r=KV, w=[k("sqs")])
            yield
            tk.op("dve", lambda h: h.tensor_scalar(out=_ap(VA[:, t, 0, 0:64], [[192, 4], [128, 2], [1, 64]]),
                                                   in0=_ap(kv[:, 0, 64:128], [[256, 4], [128, 2], [1, 64]]),
                                                   scalar1=ast[:, 3:4], scalar2=None, op0=ALU.mult),
                  r=KV + [k("a3")], w=[("VA", t)])
            yield
            tk.op("dve", lambda h: h.tensor_reduce(out=ast[:, 8:16], in_=sqs[:], axis=AX.X, op=ALU.add), r=[k("sqs")], w=[k("a8")])
            yield
            tk.op("dve", lambda h: h.tensor_tensor(out=ast[:, 4:5], in0=ast[:, 3:4], in1=ast[:, 3:4], op=ALU.mult),
                  r=[k("a3")], w=[k("a4")])
            yield
            tk.op("dve", lambda h: h.tensor_scalar(out=ast[:, 8:16], in0=ast[:, 8:16], scalar1=ast[:, 4:5],
                                                   scalar2=ast[:, 1:2], op0=ALU.mult, op1=ALU.add),
                  r=[k("a8"), k("a4"), k("a1")], w=[k("a8")])
            yield
            yield from rsq(ast[:, 8:16], ast[:, 24:32], ast[:, 16:24], 1.0 / 96, eps_t[:], [k("a8"), "eps_t"], k("a24"), k("a16"))
            tk.op("dve", lambda h: h.tensor_scalar(out=ast[:, 32:40], in0=ast[:, 24:32], scalar1=ast[:, 3:4],
                                                   scalar2=None, op0=ALU.mult), r=[k("a24"), k("a3")], w=[k("a32")])
            yield
            tk.op("dve", lambda h: h.tensor_tensor(out=kn[:], in0=kv[:, :, 0:64],
                                                   in1=_ap(ast[:, 32:40], [[1, 8], [0, 64]]), op=ALU.mult),
                  r=KV + [k("a32")], w=[k("kn")])
            yield
            tk.op("dve", lambda h: h.tensor_tensor(out=ka_tm[:, :, 0:64], in0=kn[:],
                                                   in1=_ap(grow[:, G_KA:G_KA + 64], [[0, 8], [1, 64]]), op=ALU.mult),
                  r=[k("kn"), "grow"], w=[k("ka_n")])
            yield
            cv, sv = rope_views(sl, 1, False)
            kx = A["kpe_g"][:].rearrange("p (o r f n) -> p o r f n", o=1, r=2, f=2)
            ko = A["kpe_r"][:].rearrange("p (o r f n) -> p o r f n", o=1, r=2, f=2)
            ra = A["rta"][:, 0:16].rearrange("p (o r n) -> p o r n", o=1, r=2)
            rb = A["rtb"][:, 0:16].rearrange("p (o r n) -> p o r n", o=1, r=2)
            yield from rope_apply(kx[:, :, :, 0, :], kx[:, :, :, 1, :], ko[:, :, :, 0, :], ko[:, :, :, 1, :], cv, sv, ra, rb,
                                  [k("kpe_g")], [k("kpe_r")], "rtA%d" % sl, ("rope", sl))
            tk.op("dve", lambda h: h.tensor_tensor(out=ka_tm[:, :, 64:96], in0=_ap(A["kpe_r"][:], [[0, 8], [1, 32]]),
                                                   in1=_ap(ast[:, 24:32], [[1, 8], [0, 32]]), op=ALU.mult),
                  r=[k("kpe_r"), k("a24")], w=[k("ka_r")])
            yield
            gk = bank(b, 0, 128).rearrange("p (g d) -> p g d", g=2)
            tk.op("dve", lambda h: h.tensor_copy(out=_ap(VB[:, t, 0, 0:64], [[192, 2], [128, 2], [1, 64]]),
                                                 in_=_ap(bank(b, 128, 256), [[64, 2], [0, 2], [1, 64]])),
                  r=[("ps", b)], w=[("VB", t)])
            yield
            tk.op("act", lambda h: h.activation(out=sqs[:, 0:2, :], in_=gk, func=AF.Square), r=[("ps", b), k("a8")], w=[k("sqs")])
            yield
            tk.op("dve", lambda h: h.tensor_reduce(out=ast[:, 40:42], in_=sqs[:, 0:2, :], axis=AX.X, op=ALU.add),
                  r=[k("sqs")], w=[k("a40")])
            yield
            yield from rsq(ast[:, 40:42], ast[:, 44:46], ast[:, 42:44], 1.0 / 64, eps_t[:], [k("a40"), "eps_t"], k("a44"), k("a42"))
            gkn, gkr = A["gkn"], A["gkr"]
            tk.op("dve", lambda h: h.tensor_tensor(out=gkn[:], in0=gk, in1=_ap(ast[:, 44:46], [[1, 2], [0, 64]]),
                                                   op=ALU.mult), r=[("ps", b), k("a44")], w=[k("gkn")])
            yield
            tp6 = bankbf(b, 1024)
            tk.group("pe", [tr(tp6[0:96, hh * 128:(hh + 1) * 128], ka_tm[:, hh, :]) for hh in range(8)],
                     r=[k("ka_n"), k("ka_r"), "ident"], w=[("ps", b)])
            yield
            tk.op("dve", lambda h: h.tensor_tensor(out=gkn[:], in0=gkn[:],
                                                   in1=_ap(grow[:, G_KB:G_KB + 64], [[0, 2], [1, 64]]), op=ALU.mult),
                  r=[k("gkn"), "grow"], w=[k("gkn")])
            yield
            tk.op("act", lambda h: h.copy(out=KaT[0:96, :, t * 128:(t + 1) * 128],
                                          in_=tp6[0:96, :].rearrange("p (k t) -> p k t", k=8)),
                  r=[("ps", b)], w=[("KaT", t)])
            yield
            cv, sv = rope_views(sl, 2, True)
            gx = gkn[:].rearrange("p g (r f n) -> p g r f n", r=2, f=2)
            go = gkr[:].rearrange("p g (r f n) -> p g r f n", r=2, f=2)
            ra = A["rta"][:, 0:64].rearrange("p (g r n) -> p g r n", g=2, r=2)
            rb = A["rtb"][:, 0:64].rearrange("p (g r n) -> p g r n", g=2, r=2)
            yield from rope_apply(gx[:, :, :, 0, :], gx[:, :, :, 1, :], go[:, :, :, 0, :], go[:, :, :, 1, :], cv, sv, ra, rb,
                                  [k("gkn")], [k("gkr")], "rtA%d" % sl, ("rope", sl))
            tk.op("dve", lambda h: h.tensor_copy(out=A["kb_tm"][:], in_=_ap(gkr[:], [[64, 2], [0, 2], [1, 64]])),
                  r=[k("gkr")], w=[k("kb_tm")])
            yield
            tp3b = bankbf(b + 1, 256)
            tk.group("pe", [tr(tp3b[:, g * 128:(g + 1) * 128], A["kb_tm"][:, g, :, :].rearrange("p a d -> p (a d)"))
                            for g in range(2)], r=[k("kb_tm"), "ident"], w=[("ps", b + 1)])
            yield
            tk.op("act", lambda h: h.copy(out=KbT[:, :, t * 128:(t + 1) * 128], in_=tp3b.rearrange("p (g t) -> p g t", g=2)),
                  r=[("ps", b + 1)], w=[("KbT", t)])
            yield

        def prep(s, c, i, j, B):
            t = c * 4 + i
            xi, pb, pb2 = j, 4 + 2 * j, 5 + 2 * j
            k = lambda n: (n, "B", j)
            bst, sqq, qn, qa_tm, qb_tm, gq32 = B["bst"], B["sqq"], B["qn"], B["qa_tm"], B["qb_tm"], B["gq32"]
            sqf = sqq[:].rearrange("p h d -> p (h d)")
            rope_load(t, j)
            yield from fe(s * S + c * 512 + i * 128, xi, pb, GC_N1, "dve")
            hTt, hk = hT1[xi], ("hT", xi)
            tk.group("pe", [mm(bank(pb, 0, 384), hTt[:, kc, :], w_in_bf[:, kc, 0:384], kc == 0, kc == KC - 1)
                            for kc in range(KC)], r=[hk] + WIN, w=[("ps", pb)])
            tk.group("pe", [mm(bank(pb2), hTt[:, kc, :], w_in_bf[:, kc, 672:1184], kc == 0, kc == KC - 1)
                            for kc in range(KC)], r=[hk] + WIN, w=[("ps", pb2)])
            yield
            tk.op("act", lambda h: h.activation(out=sqf[:, 0:384], in_=bank(pb, 0, 384), func=AF.Square,
                                                accum_out=bst[:, 0:1]), r=[("ps", pb)], w=[k("sqq"), k("b0")])
            yield
            tk.op("dve", lambda h: h.tensor_copy(out=B["cq_bf"][:], in_=bank(pb, 0, 384)), r=[("ps", pb)], w=[k("cq_bf")])
            yield
            tk.op("act", lambda h: h.copy(out=gq32[:], in_=bank(pb2)), r=[("ps", pb2)], w=[k("gq32")])
            yield
            tp3 = bankbf(pb, 384)
            tk.group("pe", [tr(tp3[:, kc * 128:(kc + 1) * 128], B["cq_bf"][:, kc * 128:(kc + 1) * 128]) for kc in range(3)],
                     r=[k("cq_bf"), "ident"], w=[("ps", pb)])
            yield
            tk.op("dve", lambda h: h.tensor_tensor(out=B["cqT"][:], in0=tp3.rearrange("p (k t) -> p k t", k=3),
                                                   in1=_ap(gcols[:, GC_QA:GC_QA + 3], [[1, 3], [0, 128]]), op=ALU.mult),
                  r=[("ps", pb), "gcols"], w=[k("cqT")])
            yield
            for half, bk in ((0, pb), (1, pb2)):
                tk.group("pe", [mm(bank(bk, 0, 384), B["cqT"][:, kc, :], wqb_bf[:, kc, half * 384:(half + 1) * 384],
                                   kc == 0, kc == 2) for kc in range(3)], r=[k("cqT")] + WQB, w=[("ps", bk)])
            yield
            for half, bk in ((0, pb), (1, pb2)):
                tk.op("act", lambda h, half=half, bk=bk: h.copy(
                    out=qn[:, 4 * half:4 * half + 4, :], in_=bank(bk, 0, 384).rearrange("p (h d) -> p h d", h=4)),
                    r=[("ps", bk)], w=[k("qn%d" % half)])
                yield
            QN = [k("qn0"), k("qn1")]
            tk.op("dve", lambda h: h.tensor_scalar(out=bst[:, 1:2], in0=bst[:, 0:1], scalar1=EPS / 384.0,
                                                   scalar2=EPS * EPS, op0=ALU.mult, op1=ALU.add), r=[k("b0")], w=[k("b1")])
            yield
            tk.op("act", lambda h: h.activation(out=sqq[:], in_=qn[:], func=AF.Square), r=QN, w=[k("sqq")])
            yield
            tk.op("dve", lambda h: h.tensor_reduce(out=bst[:, 8:16], in_=sqq[:], axis=AX.X, op=ALU.add), r=[k("sqq")], w=[k("b8")])
            yield
            yield from rsq(bst[:, 8:16], bst[:, 24:32], bst[:, 16:24], 1.0 / 96, bst[:, 1:2], [k("b8"), k("b1")], k("b24"), k("b16"))
            tk.op("dve", lambda h: h.tensor_tensor(out=qn[:], in0=qn[:], in1=_ap(bst[:, 24:32], [[1, 8], [0, 96]]),
                                                   op=ALU.mult), r=QN + [k("b24")], w=[k("qn0"), k("qn1")])
            yield
            tk.op("dve", lambda h: h.tensor_tensor(out=qa_tm[:, :, 0:64], in0=qn[:, :, 0:64],
                                                   in1=_ap(grow[:, G_QA:G_QA + 64], [[0, 8], [1, 64]]), op=ALU.mult),
                  r=QN + ["grow"], w=[k("qa_n")])
            yield
            tk.op("dve", lambda h: h.tensor_tensor(out=qn[:, :, 64:96], in0=qn[:, :, 64:96],
                                                   in1=_ap(grow[:, G_QA + 64:G_QA + 96], [[0, 8], [1, 32]]), op=ALU.mult),
                  r=QN + ["grow"], w=[k("qn0"), k("qn1")])
            yield
            cv, sv = rope_views(j, 8, False)
            qx = qn[:, :, 64:96].rearrange("p h (r f n) -> p h r f n", r=2, f=2)
            qo = qa_tm[:, :, 64:96].rearrange("p h (r f n) -> p h r f n", r=2, f=2)
            ra = B["rta"][:, 0:128].rearrange("p (h r n) -> p h r n", h=8, r=2)
            rb = B["rtb"][:, 0:128].rearrange("p (h r n) -> p h r n", h=8, r=2)
            yield from rope_apply(qx[:, :, :, 0, :], qx[:, :, :, 1, :], qo[:, :, :, 0, :], qo[:, :, :, 1, :], cv, sv, ra, rb,
                                  QN, [k("qa_r")], "rtB%d" % j, ("rope", j))
            tp6 = bankbf(pb, 1024)
            tk.group("pe", [tr(tp6[0:96, hh * 128:(hh + 1) * 128], qa_tm[:, hh, :]) for hh in range(8)],
                     r=[k("qa_n"), k("qa_r"), "ident"], w=[("ps", pb)])
            yield
            tk.op("act", lambda h: h.copy(out=QaT[0:96, :, i * 128:(i + 1) * 128],
                                          in_=tp6[0:96, :].rearrange("p (k t) -> p k t", k=8)),
                  r=[("ps", pb)], w=[("QaT", i)])
            yield
            g3 = gq32[:].rearrange("p (h d) -> p h d", h=8)
            sqb = sqf[:, 0:512].rearrange("p (h d) -> p h d", h=8)
            tk.op("act", lambda h: h.activation(out=sqb, in_=g3, func=AF.Square), r=[k("gq32"), k("b8")], w=[k("sqq")])
            yield
            tk.op("dve", lambda h: h.tensor_reduce(out=bst[:, 32:40], in_=sqb, axis=AX.X, op=ALU.add), r=[k("sqq")], w=[k("b32")])
            yield
            yield from rsq(bst[:, 32:40], bst[:, 48:56], bst[:, 40:48], 1.0 / 64, eps_t[:], [k("b32"), "eps_t"], k("b48"), k("b40"))
            tk.op("dve", lambda h: h.tensor_tensor(out=g3, in0=g3, in1=_ap(bst[:, 48:56], [[1, 8], [0, 64]]), op=ALU.mult),
                  r=[k("gq32"), k("b48")], w=[k("gq32")])
            yield
            tk.op("dve", lambda h: h.tensor_tensor(out=g3, in0=g3, in1=_ap(grow[:, G_QB:G_QB + 64], [[0, 8], [1, 64]]),
                                                   op=ALU.mult), r=[k("gq32"), "grow"], w=[k("gq32")])
            yield
            cv, sv = rope_views(j, 8, True)
            bx = g3.rearrange("p h (r f n) -> p h r f n", r=2, f=2)
            bo = qb_tm[:].rearrange("p h (r f n) -> p h r f n", r=2, f=2)
            ra = B["rta"][:, 0:256].rearrange("p (h r n) -> p h r n", h=8, r=2)
            rb = B["rtb"][:, 0:256].rearrange("p (h r n) -> p h r n", h=8, r=2)
            yield from rope_apply(bx[:, :, :, 0, :], bx[:, :, :, 1, :], bo[:, :, :, 0, :], bo[:, :, :, 1, :], cv, sv, ra, rb,
                                  [k("gq32")], [k("qb_tm")], "rtB%d" % j, ("rope", j))
            tp7 = bankbf(pb2, 512)
            qbf = qb_tm[:].rearrange("p h d -> p (h d)")
            tk.group("pe", [tr(tp7[:, jj * 128:(jj + 1) * 128], qbf[:, jj * 128:(jj + 1) * 128]) for jj in range(4)],
                     r=[k("qb_tm"), "ident"], w=[("ps", pb2)])
            yield
            tk.op("act", lambda h: h.copy(out=QbT[:, :, i * 128:(i + 1) * 128], in_=tp7.rearrange("p (k t) -> p k t", k=4)),
                  r=[("ps", pb2)], w=[("QbT", i)])
            yield

        def prep_seq(s, c, j, B):
            for i in (j, j + 2):
                yield from prep(s, c, i, j, B)

        with ExitStack() as esb:
            def mkB():
                return dict(
                    cq_bf=sb("cq_bf", [128, 384], BF16, esb), cqT=sb("cqT", [128, 3, 128], BF16, esb),
                    bst=sb("bst", [128, 64], F32, esb), sqq=sb("sqq", [128, 8, 96], F32, esb),
                    qn=sb("qn", [128, 8, 96], F32, esb), qa_tm=sb("qa_tm", [128, 8, 96], BF16, esb),
                    qb_tm=sb("qb_tm", [128, 8, 64], BF16, esb), rta=sb("rtaB", [128, 256], F32, esb),
                    rtb=sb("rtbB", [128, 256], F32, esb), gq32=sb("gq32", [128, 512], F32, esb))
            BS = [mkB(), mkB()]
            QaT = sb("QaT", [128, 8, 512], BF16, esb)
            QbT = sb("QbT", [128, 4, 512], BF16, esb)
            xres1 = sb("xres1", [128, D], F32, esb)
            NP = 3
            shared = sb("shared", [128, 9216], BF16, esb)

            def carve(off, shape, dt):
                nel = int(np.prod(shape))
                size = nel * (2 if dt == BF16 else 4)
                assert off % 64 == 0 and off + size <= 18432
                a = shared[:, off // 2:(off + size) // 2]
                if dt == F32:
                    a = a.bitcast(F32)
                if len(shape) > 1:
                    names = ["a", "b", "c", "d"][:len(shape)]
                    a = a.rearrange("p (%s) -> p %s" % (" ".join(names), " ".join(names)),
                                    **{n: int(v) for n, v in zip(names, shape)})
                return a

            PT = [carve(i * 2048, [1024], BF16) for i in range(NP)]
            rec = carve(6144, [512], F32)
            onT = carve(8192, [KC, 512], BF16)
            sq_on = [carve(16384 + i * 1024, [512], BF16) for i in range(2)]
            ost = sb("ost", [128, 32], F32, esb)
            chunks = [(s, c) for s in range(NSEQ) for c in range(NCH)]
            pcount = [0]
            NSP = 3

            def attention(s, c):
                qkeys_a = [("QaT", i) for i in range(4)]
                qkeys_b = [("QbT", i) for i in range(4)]
                units = [(hd, p) for hd in range(16) for p in range(NTT // 2)]

                def s_mm(hd, p, sp_i):
                    fns, rk = [], []
                    for j in range(2):
                        kb = 2 * p + j
                        if hd < 8:
                            lhsT = KaT[0:96, hd, kb * 128:(kb + 1) * 128]
                            rhs = QaT[0:96, hd, :]
                            rk.append(("KaT", kb))
                        else:
                            jq = hd - 8
                            g, hf = jq // 4, jq % 2
                            lhsT = KbT[hf * 64:(hf + 1) * 64, g, kb * 128:(kb + 1) * 128]
                            rhs = QbT[hf * 64:(hf + 1) * 64, jq // 2, :]
                            rk.append(("KbT", kb))
                        fns.append(mm(bank(2 * sp_i + j), lhsT, rhs, True, True))
                    rk += (qkeys_a if hd < 8 else qkeys_b)
                    tk.group("pe", fns, r=rk, w=[("ps", 2 * sp_i), ("ps", 2 * sp_i + 1)])

                def exp_pv(hd, p, sp_i):
                    slot = pcount[0] % NP
                    pcount[0] += 1
                    scale = (96.0 if hd < 8 else 64.0) ** -0.5
                    src = ps[:, sp_i * 1024:(sp_i + 1) * 1024]
                    tk.op("act", lambda h: h.activation(out=PT[slot][:], in_=src, func=AF.Exp, scale=scale),
                          r=[("ps", 2 * sp_i), ("ps", 2 * sp_i + 1)], w=[("PT", slot)])
                    ob = 6 + hd % 2
                    fns, rk = [], [("PT", slot), "Vones"]
                    for j in range(2):
                        kb = 2 * p + j
                        if hd < 8:
                            lo = 0 if hd % 2 == 0 else 64
                            lhsT = VA[:, kb, hd // 2, lo:lo + 128]
                            rk.append(("VA", kb))
                        else:
                            jq = hd - 8
                            lo = 0 if jq % 2 == 0 else 64
                            lhsT = VB[:, kb, jq // 4, lo:lo + 128]
                            rk.append(("VB", kb))
                        fns.append(mm(bank(ob), lhsT, PT[slot][:, j * 512:(j + 1) * 512], kb == 0, kb == NTT - 1))
                    tk.group("pe", fns, r=rk, w=[("ps", ob)])
                    if p == NTT // 2 - 1:
                        head_done(hd, ob)

                def head_done(hd, ob):
                    cidx = hd // 2
                    if hd % 2 == 0:
                        num, den, o_lo, rc = bank(ob)[0:64, :], bank(ob)[64:128, :], 0, rec[64:128, :]
                    else:
                        num, den, o_lo, rc = bank(ob)[64:128, :], bank(ob)[0:64, :], 64, rec[0:64, :]
                    tk.op("dve", lambda h: h.reciprocal(out=rc, in_=den), r=[("ps", ob)], w=["rec"])
                    tk.op("dve", lambda h: h.tensor_tensor(out=onT[o_lo:o_lo + 64, cidx, :], in0=num, in1=rc, op=ALU.mult),
                          r=[("ps", ob), "rec"], w=[("onT", cidx, hd % 2)])

                for u in range(NSP - 1):
                    s_mm(units[u][0], units[u][1], u % NSP)
                for ui, (hd, p) in enumerate(units):
                    if ui + NSP - 1 < len(units):
                        h2, p2 = units[ui + NSP - 1]
                        s_mm(h2, p2, (ui + NSP - 1) % NSP)
                    exp_pv(hd, p, ui % NSP)

            def wo_residual(s, c):
                for cidx in range(KC):
                    sq = sq_on[cidx % 2]
                    tk.op("dve", lambda h, sq=sq, cidx=cidx: h.tensor_tensor(out=sq[:], in0=onT[:, cidx, :], in1=onT[:, cidx, :],
                                                                            op=ALU.mult),
                          r=[("onT", cidx, 0), ("onT", cidx, 1)], w=[("sq_on", cidx % 2)])
                    tk.group("pe", [mm(bank(0, i * 8 + cidx, i * 8 + cidx + 1), sq[:, i * 128:(i + 1) * 128], ones_c[:],
                                       True, True) for i in range(4)],
                             r=[("sq_on", cidx % 2), "ones_c"], w=[("ps", 0)])
                    yield
                tk.op("dve", lambda h: h.tensor_reduce(out=ost[:, 0:8],
                                                       in_=bank(0, 0, 32).rearrange("p (i g k) -> p i g k", i=4, g=2),
                                                       axis=AX.X, op=ALU.add), r=[("ps", 0)], w=["o0"])
                yield
                yield from rsq(ost[:, 0:8], ost[:, 16:24], ost[:, 8:16], 1.0 / 512, eps_t[:], ["o0", "eps_t"], "o16", "o8")
                xr = xres1
                for i in range(4):
                    row0 = s * S + c * 512 + i * 128
                    tk.dma("sp", xr[:], x[row0:row0 + 128, :], "xres1", w=["xres1"])
                    for g in range(2):
                        for half in range(2):
                            bk = g * 2 + half
                            tk.group("pe", [mm(bank(bk), onT[:, g * 4 + kc, i * 128:(i + 1) * 128],
                                               wo_bf[:, g * 4 + kc, half * 512:(half + 1) * 512], kc == 0, kc == 3)
                                            for kc in range(4)],
                                     r=[("onT", g * 4 + kc, q) for kc in range(4) for q in range(2)] + WO, w=[("ps", bk)])
                        yield
                    for g in range(2):
                        acc = ps[:, (g * 2) * 512:(g * 2 + 2) * 512]
                        tk.op("dve", lambda h, acc=acc, g=g: h.scalar_tensor_tensor(
                            out=xr[:], in0=acc, scalar=ost[:, 16 + i * 2 + g:16 + i * 2 + g + 1], in1=xr[:],
                            op0=ALU.mult, op1=ALU.add),
                            r=[("ps", g * 2), ("ps", g * 2 + 1), "o16", "xres1"], w=["xres1"])
                        yield
                    tk.dma("sp", y[row0:row0 + 128, :], xr[:], "yst", r=["xres1"])
                    gt = row0 // 128
                    tk.op("act", lambda h: h.activation(out=junk[:, 0:512], in_=xr[:, 0:512], func=AF.Square,
                                                        accum_out=r2t[:, 0:1]), r=["xres1"], w=["junk", "r2a"])
                    yield
                    tk.op("act", lambda h: h.activation(out=junk[:, 0:512], in_=xr[:, 512:1024], func=AF.Square,
                                                        accum_out=r2t[:, 1:2]), r=["xres1"], w=["junk", "r2b"])
                    yield
                    tk.op("dve", lambda h: h.tensor_tensor(out=r2t[:, 2:3], in0=r2t[:, 0:1], in1=r2t[:, 1:2], op=ALU.add),
                          r=["r2a", "r2b"], w=["r2c"])
                    yield
                    yield from rsq(r2t[:, 2:3], rstd2[:, gt:gt + 1], r2t[:, 3:4], 1.0 / D, eps_t[:], ["r2c", "eps_t"],
                                   ("rstd2", gt), "r2d")

            for s in range(NSEQ):
                if True:
                    AS = []
                    for sl in range(2):
                        o = [sl * 9216]

                        def cv_(shape, dt):
                            nel = int(np.prod(shape)) * (2 if dt == BF16 else 4)
                            a = carve(o[0], shape, dt)
                            o[0] += (nel + 63) // 64 * 64
                            return a
                        AS.append(dict(ckv_bf=cv_([256], BF16), ckvT=cv_([2, 128], BF16), ast=cv_([48], F32),
                                       sqs=cv_([8, 64], F32), kn=cv_([8, 64], F32), ka_tm=cv_([8, 96], BF16),
                                       kpe_g=cv_([32], F32), kpe_r=cv_([32], F32), rta=cv_([64], F32), rtb=cv_([64], F32),
                                       gkn=cv_([2, 64], F32), gkr=cv_([2, 64], F32), kb_tm=cv_([2, 2, 64], BF16)))
                        assert o[0] <= (sl + 1) * 9216

                    def seqA(sl):
                        for t in range(sl, NTT, 2):
                            yield from chainA(s, t, sl, AS[sl])

                    tk.barrier()
                    drive([seqA(0), seqA(1)], 2)
                    tk.barrier()
                if "kv" in dump and s == NSEQ - 1:
                    for nm, t_ in (("KaT", KaT), ("VA", VA), ("KbT", KbT), ("VB", VB)):
                        shp = [128, int(np.prod(t_.shape[1:]))]
                        dumps[nm] = nc.dram_tensor("dump_" + nm, shp, BF16, kind="ExternalOutput").ap()
                        nd = len(t_.shape)
                        src = t_[:].rearrange("p a b -> p (a b)") if nd == 3 else t_[:].rearrange("p a b c -> p (a b c)")
                        tk.dma("sp", dumps[nm][:, :], src, "dump", r=[])
                if s == 0:
                    drive([prep_seq(0, 0, 0, BS[0]), prep_seq(0, 0, 1, BS[1])], 2)
                for c in range(NCH):
                    ci = s * NCH + c
                    nxt = chunks[ci + 1] if ci + 1 < len(chunks) else None
                    attention(s, c)
                    gens = [wo_residual(s, c)]
                    if nxt:
                        gens += [prep_seq(nxt[0], nxt[1], 0, BS[0]), prep_seq(nxt[0], nxt[1], 1, BS[1])]
                    drive(gens, 3)
        es1.close()
        tk.barrier()

        if do_ffn:
            with ExitStack() as es2:
                wg_bf = sb("wg_bf", [128, KC, DFF], BF16, es2)
                wu_bf = sb("wu_bf", [128, KC, DFF], BF16, es2)
                wd_bf = sb("wd_bf", [128, FC, D], BF16, es2)
                NX2 = 5
                x2sl = [sb("x2sl%d" % i, [128, D], F32, es2) for i in range(NX2)]
                xs2 = [sb("xs2_%d" % i, [128, D], BF16, es2) for i in range(2)]
                hT2 = [sb("hT2_%d" % i, [128, KC, 512], BF16, es2) for i in range(2)]
                aT = sb("aT", [128, FC, 512], BF16, es2)
                sg = [sb("sg%d" % i, [128, 512], F32, es2) for i in range(2)]
                stg2 = [sb("stg2_%d" % i, [128, 1024], F32, es2) for i in range(2)]
                st2n = [0]

                def stage2(dst, src, wkey, view3):
                    n = st2n[0]
                    st2n[0] += 1
                    sl = n % 2
                    sv = stg2[sl][:].rearrange("p (k n) -> p k n", k=KC) if view3 else stg2[sl][:]
                    tk.dma("sp", sv, src, "stg2_%d" % sl, w=[("stg2", sl)])
                    if n % 2 == 0:
                        tk.op("dve", lambda h: h.tensor_copy(out=dst, in_=sv), r=[("stg2", sl)], w=[wkey])
                    else:
                        tk.op("act", lambda h: h.copy(out=dst, in_=sv), r=[("stg2", sl)], w=[wkey])

                def load_gu(f):
                    stage2(wg_bf[:, :, f * 128:(f + 1) * 128], wg_v[:, :, f * 128:(f + 1) * 128], ("wg", f), True)
                    stage2(wu_bf[:, :, f * 128:(f + 1) * 128], wu_v[:, :, f * 128:(f + 1) * 128], ("wu", f), True)

                def load_d(f):
                    stage2(wd_bf[:, f, :], w_down[f * 128:(f + 1) * 128, :], ("wd", f), False)
                wg_v = w_gate.rearrange("(k p) n -> p k n", p=128)
                wu_v = w_up.rearrange("(k p) n -> p k n", p=128)
                NCH2 = NT // 512
                fe2 = [0]

                def fe2_load(c, i):
                    sl = (c * 4 + i) % NX2
                    gt = c * 4 + i
                    tk.dma("sp", x2sl[sl][:], y[gt * 128:(gt + 1) * 128, :], "x2_%d" % sl, w=[("x2", sl)])

                def fe2_compute(c, i):
                    n = fe2[0]
                    fe2[0] += 1
                    sl, b_i = (c * 4 + i) % NX2, n % 2
                    gt = c * 4 + i
                    xt, xb = x2sl[sl], xs2[b_i]
                    tk.op("dve", lambda h: h.tensor_scalar(out=xb[:], in0=xt[:], scalar1=rstd2[:, gt:gt + 1], scalar2=None,
                                                           op0=ALU.mult), r=[("x2", sl), ("rstd2", gt)], w=[("xs2", b_i)])
                    tp = bankbf(7, 1024)
                    tk.group("pe", [tr(tp[:, kc * 128:(kc + 1) * 128], xb[:, kc * 128:(kc + 1) * 128]) for kc in range(KC)],
                             r=[("xs2", b_i), "ident"], w=[("ps", 7)])
                    tk.op("dve", lambda h: h.tensor_tensor(out=hT2[c % 2][:, :, i * 128:(i + 1) * 128],
                                                           in0=tp.rearrange("p (k t) -> p k t", k=KC),
                                                           in1=_ap(gcols[:, GC_N2:GC_N2 + KC], [[1, KC], [0, 128]]), op=ALU.mult),
                          r=[("ps", 7), "gcols"], w=[("hT2", c % 2, i)])

                def front_end2(c, i):
                    fe2_load(c, i)
                    fe2_compute(c, i)

                def down(c, i):
                    sl = (c * 4 + i) % NX2
                    gt = c * 4 + i
                    for half in range(2):
                        bk = 4 + (2 * i + half) % 3
                        tk.group("pe", [mm(bank(bk), aT[:, f, i * 128:(i + 1) * 128], wd_bf[:, f, half * 512:(half + 1) * 512],
                                           f == 0, f == FC - 1) for f in range(FC)],
                                 r=[("aT", f) for f in range(FC)] + [("wd", f) for f in range(FC)], w=[("ps", bk)])
                        tk.op("dve", lambda h, bk=bk, half=half: h.tensor_tensor(
                            out=x2sl[sl][:, half * 512:(half + 1) * 512], in0=bank(bk),
                            in1=x2sl[sl][:, half * 512:(half + 1) * 512], op=ALU.add),
                            r=[("ps", bk), ("x2", sl)], w=[("x2", sl)])
                    tk.dma("sp", y[gt * 128:(gt + 1) * 128, :], x2sl[sl][:], "y2_%d" % sl, r=[("x2", sl)])

                for i in range(4):
                    front_end2(0, i)
                load_gu(0)
                for c in range(NCH2):
                    hk = [("hT2", c % 2, i) for i in range(4)]
                    for f in range(FC):
                        gb, ub = f % 2, 2 + f % 2
                        if c == 0:
                            if f + 1 < FC:
                                load_gu(f + 1)
                            load_d(f)
                        tk.group("pe", [mm(bank(gb), wg_bf[:, kc, f * 128:(f + 1) * 128], hT2[c % 2][:, kc, :],
                                           kc == 0, kc == KC - 1) for kc in range(KC)], r=hk + [("wg", f)], w=[("ps", gb)])
                        tk.group("pe", [mm(bank(ub), wu_bf[:, kc, f * 128:(f + 1) * 128], hT2[c % 2][:, kc, :],
                                           kc == 0, kc == KC - 1) for kc in range(KC)], r=hk + [("wu", f)], w=[("ps", ub)])
                        tk.op("act", lambda h, f=f, gb=gb: h.activation(out=sg[f % 2][:], in_=bank(gb), func=AF.Silu),
                              r=[("ps", gb)], w=[("sg", f % 2)])
                        tk.op("dve", lambda h, f=f, ub=ub: h.tensor_tensor(out=aT[:, f, :], in0=bank(ub), in1=sg[f % 2][:],
                                                                          op=ALU.mult),
                              r=[("ps", ub), ("sg", f % 2)], w=[("aT", f)])
                        if c + 1 < NCH2 and f == 4:
                            fe2_load(c + 1, 0)
                        if c + 1 < NCH2 and f == 12:
                            fe2_compute(c + 1, 0)
                    for i in range(4):
                        down(c, i)
                        if c + 1 < NCH2:
                            if i >= 1:
                                fe2_compute(c + 1, i)
                            if i < 3:
                                fe2_load(c + 1, i + 1)
                    if c + 1 < NCH2:
                        fe2_compute(c + 1, 3)
        tk.finish("sp")
    return nc, dumps


def _rope_table(S):
    t = np.arange(S)
    row = (t // 64).astype(np.float32)
    col = (t % 64).astype(np.float32)
    out = np.zeros((S, 96), np.float32)
    inv_g = (np.float32(10000.0) ** (-(np.arange(0, 32, 2, dtype=np.float32) / np.float32(32)))).astype(np.float32)
    inv_m = (np.float32(10000.0) ** (-(np.arange(0, 16, 2, dtype=np.float32) / np.float32(16)))).astype(np.float32)
    ang = np.concatenate([row[:, None] * inv_g[None], col[:, None] * inv_g[None],
                          row[:, None] * inv_m[None], col[:, None] * inv_m[None]], axis=1).astype(np.float32)
    out[:, 0:48] = np.cos(ang)
    out[:, 48:96] = np.sin(ang)
    return out


def _host_inputs(inp, S):
    f = lambda a: np.ascontiguousarray(np.asarray(a, dtype=np.float32))
    col = lambda g: f(g).reshape(-1, 128).T
    gcols = np.concatenate([col(inp["norm1_g"][0]), col(inp["q_a_norm_g"][0]), col(inp["kv_a_norm_g"][0]),
                            col(np.concatenate([f(inp["mla_out_norm_g"][0]), f(inp["gqa_out_norm_g"][0])])),
                            col(inp["norm2_g"][0])], axis=1)
    grow = np.concatenate([f(inp["mla_q_norm_g"][0]), f(inp["mla_k_norm_g"][0]),
                           f(inp["gqa_q_norm_g"][0]), f(inp["gqa_k_norm_g"][0])])[None, :]
    shared = {
        "w_in": f(inp["w_in"][0]), "w_q_b": f(inp["w_q_b"][0]), "w_kv_b": f(inp["w_kv_b"][0]), "w_o": f(inp["w_o"][0]),
        "w_gate": f(inp["w_gate"][0]), "w_up": f(inp["w_up"][0]), "w_down": f(inp["w_down"][0]),
        "gcols": f(gcols), "grow": f(grow), "rope": _rope_table(S),
    }
    return shared


def kernel(**inputs):
    x = np.asarray(inputs["x"], dtype=np.float32)
    B, S, _ = x.shape
    nseq = B // N_CORES
    shared = _host_inputs(inputs, S)
    nc, _ = build_nc(nseq, S)
    in_maps = []
    for c in range(N_CORES):
        m = dict(shared)
        m["x"] = np.ascontiguousarray(x[c * nseq:(c + 1) * nseq].reshape(nseq * S, D))
        in_maps.append(m)
    res = run_bass_kernel_spmd(nc, in_maps, core_ids=list(range(N_CORES)))
    out = np.concatenate([np.asarray(r["y"]).reshape(nseq, S, D) for r in res.results], axis=0)
    return out.astype(np.float32)
```
